# Optimizing a Trainium2 kernel written in Bass

```python
import math
import jax, jax.numpy as jnp
from jax import lax
import numpy as np

D_MODEL = 1024
BATCH = 32
SEQ = 256
DEPTH = 2
DEC_BATCH = 2
DEC_SEQ = 4096
PAST_LEN = 256

GRID_W = 64
N_EVEN = (DEPTH + 1) // 2
N_ODD = DEPTH // 2
EPS = 1e-6
CONV_W = 3
D_A = D_MODEL
H_A = 16
P_A = D_A // H_A
N_A = 128
G_A = 2
CHUNK = 128
D_XBC = D_A + 2 * G_A * N_A
DT_MIN = 1e-3
DT_MAX = 1e-1
D_B = D_MODEL
HY_EMB = 33
HY_BANDS = (HY_EMB - 1) // 2
HY_HID = 64
HY_TARGET = 1e-2
HY_DECAY_PCT_HI = 0.3
HY_DECAY_PCT_LO = 1.5
D_C = D_MODEL
H_C = 16
HD_C = D_C // H_C
WIN_H = 8
WIN_W = 16
N_IN_E = D_A + D_XBC + 2 * H_A + 3 * D_B + D_B
SPLIT_E = (D_A, D_A + D_XBC, D_A + D_XBC + 2 * H_A, D_A + D_XBC + 2 * H_A + 3 * D_B)
N_IN_O = 4 * D_C

kernel_name = "hybrid_ssd_hyena_natten_diffusion_step"


def rms_norm(x, g):
    xf = x.astype(jnp.float32)
    y = xf * lax.rsqrt(jnp.mean(xf * xf, axis=-1, keepdims=True) + EPS)
    return (y * g.astype(jnp.float32)).astype(x.dtype)


def ada_mod(cvec, w, b):
    m = jax.nn.silu(cvec) @ w + b
    shift, scale, gate = jnp.split(m, 3, axis=-1)
    return shift[:, None], scale[:, None], gate[:, None]


def depthwise_conv(x, w, b):
    L = x.shape[1]
    pad = CONV_W // 2
    xp = jnp.pad(x, ((0, 0), (pad, CONV_W - 1 - pad), (0, 0)))
    out = b
    for j in range(CONV_W):
        out = out + xp[:, j:j + L] * w[j]
    return out


def segsum(x):
    T = x.shape[-1]
    cs = jnp.cumsum(x, axis=-1)
    diff = cs[..., :, None] - cs[..., None, :]
    mask = jnp.tril(jnp.ones((T, T), dtype=bool))
    return jnp.where(mask, diff, -jnp.inf)


def ssd_scan(x, dt, a, bh, ch, init):
    b, L, h, p = x.shape
    n = bh.shape[-1]
    nc = L // CHUNK
    f32 = jnp.float32
    xdt = (x.astype(f32) * dt[..., None]).reshape(b, nc, CHUNK, h, p)
    adt = (a * dt).reshape(b, nc, CHUNK, h).transpose(0, 3, 1, 2)
    bc = bh.astype(f32).reshape(b, nc, CHUNK, h, n)
    cc = ch.astype(f32).reshape(b, nc, CHUNK, h, n)
    acs = jnp.cumsum(adt, axis=-1)
    lmat = jnp.exp(segsum(adt))
    scores = jnp.einsum('bclhn,bcshn->bhcls', cc, bc) * lmat
    y_diag = jnp.einsum('bhcls,bcshp->bclhp', scores, xdt)
    decay_states = jnp.exp(acs[..., -1:] - acs)
    states = jnp.einsum('bclhn,bhcl,bclhp->bchpn', bc, decay_states, xdt)
    states = jnp.concatenate([init.astype(f32)[:, None], states], axis=1)
    chunk_a = jnp.pad(acs[..., -1], ((0, 0), (0, 0), (1, 0)))
    decay_chunk = jnp.exp(segsum(chunk_a))
    new_states = jnp.einsum('bhzc,bchpn->bzhpn', decay_chunk, states)
    y_off = jnp.einsum('bclhn,bchpn->bclhp', cc, new_states[:, :-1]) * jnp.exp(acs).transpose(0, 2, 3, 1)[..., None]
    return (y_diag + y_off).reshape(b, L, h, p), new_states[:, -1]


def hyena_filter(L, w1, b1, w2, b2, w3, freq):
    f32 = jnp.float32
    t = jnp.linspace(0.0, 1.0, L, dtype=f32)[:, None]
    w = 2.0 * math.pi * jnp.arange(L, dtype=f32)[:, None] / L
    f = jnp.linspace(1e-4, HY_BANDS - 1, HY_BANDS, dtype=f32)[None]
    feats = jnp.concatenate([t, jnp.cos(f * w), -jnp.sin(f * w)], axis=-1)
    hdn = jnp.sin(freq * (feats @ w1 + b1))
    hdn = jnp.sin(freq * (hdn @ w2 + b2))
    filt = (hdn @ w3).astype(f32)
    deltas = jnp.linspace(math.log(HY_TARGET) / HY_DECAY_PCT_HI, math.log(HY_TARGET) / HY_DECAY_PCT_LO, D_B, dtype=f32)
    decay = jnp.exp(-t * jnp.abs(deltas)[None])
    h_fwd = filt[:, :D_B] * decay
    h_bwd = filt[:, D_B:] * decay
    return jnp.concatenate([h_fwd, jnp.zeros((1, D_B), f32), h_bwd[:0:-1]], axis=0)


def fft_long_conv(u, k, bias):
    L = u.shape[1]
    n = 2 * L
    uf = jnp.fft.rfft(u.astype(jnp.float32), n=n, axis=1)
    kf = jnp.fft.rfft(k.astype(jnp.float32), n=n, axis=0)
    y = jnp.fft.irfft(uf * kf[None], n=n, axis=1)[:, :L]
    return (y + u.astype(jnp.float32) * bias.astype(jnp.float32)).astype(u.dtype)


def mixer_ssd_hyena(h, init_state, w_in, w_out, conv_a_w, conv_a_b, dt_bias, a_log, d_skip, norm_a_w,
                    conv_b_w, conv_b_b, hf_w1, hf_b1, hf_w2, hf_b2, hf_w3, hf_freq, hy_bias):
    b, L, _ = h.shape
    z_a, xbc, dt_raw, u_b, g_b = jnp.split(h @ w_in, SPLIT_E, axis=-1)
    xbc = jax.nn.silu(depthwise_conv(xbc, conv_a_w, conv_a_b))
    xa, bm, cm = jnp.split(xbc, (D_A, D_A + G_A * N_A), axis=-1)
    xa = xa.reshape(b, L, H_A, P_A)
    bh = jnp.repeat(bm.reshape(b, L, G_A, N_A), H_A // G_A, axis=2)
    ch = jnp.repeat(cm.reshape(b, L, G_A, N_A), H_A // G_A, axis=2)
    dt_raw = dt_raw.astype(jnp.float32).reshape(b, L, 2, H_A)
    ys, finals = [], []
    for d in range(2):
        rev = (lambda t: jnp.flip(t, axis=1)) if d == 1 else (lambda t: t)
        dt = jax.nn.softplus(dt_raw[:, :, d] + dt_bias[d].astype(jnp.float32))
        a = -jnp.exp(a_log[d].astype(jnp.float32))
        y_d, s_d = ssd_scan(rev(xa), rev(dt), a, rev(bh), rev(ch), init_state[:, d])
        ys.append(rev(y_d) + xa.astype(jnp.float32) * d_skip[d].astype(jnp.float32)[:, None])
        finals.append(s_d)
    y_a = (ys[0] + ys[1]).reshape(b, L, D_A).astype(h.dtype)
    y_a = rms_norm(y_a * jax.nn.silu(z_a), norm_a_w)
    u = depthwise_conv(u_b, conv_b_w, conv_b_b)
    x0, x1, v = jnp.split(u, 3, axis=-1)
    k = hyena_filter(L, hf_w1, hf_b1, hf_w2, hf_b2, hf_w3, hf_freq)
    y_b = x0 * fft_long_conv(v * x1, k, hy_bias)
    y_b = (y_b * jax.nn.silu(g_b)).astype(y_a.dtype)
    out = jnp.concatenate([y_a, y_b], axis=-1) @ w_out
    return out, jnp.stack(finals, axis=1).astype(h.dtype)


def na_context(h, w_in, w_out):
    b, L, _ = h.shape
    q, k, v, g = jnp.split(h @ w_in, 4, axis=-1)
    heads = lambda t: t.reshape(b, L, H_C, HD_C).transpose(0, 2, 1, 3)
    q, k, v = heads(q), heads(k), heads(v)
    s = jnp.einsum('bhqd,bhkd->bhqk', q, k).astype(jnp.float32) * (HD_C ** -0.5)
    p = jax.nn.softmax(s, axis=-1).astype(v.dtype)
    o = jnp.einsum('bhqk,bhkd->bhqd', p, v).transpose(0, 2, 1, 3).reshape(b, L, D_C)
    return (o * jax.nn.silu(g)) @ w_out, k, v


def na_latent(h, ck, cv, rpb, w_in, w_out):
    b, L, _ = h.shape
    R = L // GRID_W
    kh = min(WIN_H, R)
    q, k, v, g = jnp.split(h @ w_in, 4, axis=-1)
    grid = lambda t: t.reshape(b, R, GRID_W, H_C, HD_C).transpose(0, 3, 1, 2, 4)
    q, k, v = grid(q), grid(k), grid(v)
    r_idx = jnp.arange(R)
    c_idx = jnp.arange(GRID_W)
    rows = jnp.clip(r_idx - kh // 2, 0, R - kh)[:, None] + jnp.arange(kh)[None]
    col0 = jnp.clip(c_idx - WIN_W // 2, 0, GRID_W - WIN_W)
    col_in = (c_idx[None, :] >= col0[:, None]) & (c_idx[None, :] < col0[:, None] + WIN_W)
    k_rows = k[:, :, rows]
    v_rows = v[:, :, rows]
    s_loc = jnp.einsum('bhrqd,bhrikd->bhrqik', q, k_rows).astype(jnp.float32) * (HD_C ** -0.5)
    dr_idx = rows - r_idx[:, None] + WIN_H - 1
    dc_idx = jnp.clip(c_idx[None, :] - c_idx[:, None] + WIN_W - 1, 0, 2 * WIN_W - 2)
    bias = rpb[:, dr_idx][..., dc_idx]
    s_loc = s_loc + bias.transpose(0, 1, 3, 2, 4)[None].astype(jnp.float32)
    s_loc = jnp.where(col_in[:, None, :], s_loc, -jnp.inf)
    s_ctx = jnp.einsum('bhrqd,bhkd->bhrqk', q, ck).astype(jnp.float32) * (HD_C ** -0.5)
    s = jnp.concatenate([s_loc.reshape(b, H_C, R, GRID_W, kh * GRID_W), s_ctx], axis=-1)
    p = jax.nn.softmax(s, axis=-1).astype(v.dtype)
    p_loc = p[..., :kh * GRID_W].reshape(b, H_C, R, GRID_W, kh, GRID_W)
    p_ctx = p[..., kh * GRID_W:]
    o = jnp.einsum('bhrqik,bhrikd->bhrqd', p_loc, v_rows) + jnp.einsum('bhrqk,bhkd->bhrqd', p_ctx, cv)
    o = o.transpose(0, 2, 3, 1, 4).reshape(b, L, D_C)
    return (o * jax.nn.silu(g)) @ w_out


def setup_inputs(seed: int = 0) -> dict:
    key = jax.random.key(seed)
    ks = iter(jax.random.split(key, 40))

    def nrm(shape, scale):
        return scale * jax.random.normal(next(ks), shape, jnp.float32)

    x_prompt = nrm((BATCH, SEQ, D_MODEL), 1.0)
    x_sample = nrm((DEC_BATCH, DEC_SEQ, D_MODEL), 1.0)
    state_ssd = nrm((DEC_BATCH, N_EVEN, 2, H_A, P_A, N_A), 0.1)
    cache_k = nrm((DEC_BATCH, N_ODD, H_C, PAST_LEN, HD_C), 1.0)
    cache_v = nrm((DEC_BATCH, N_ODD, H_C, PAST_LEN, HD_C), 1.0)
    c = nrm((DEC_BATCH, D_MODEL), 1.0)
    c_ctx = nrm((D_MODEL,), 1.0)
    norm_w = 1.0 + nrm((DEPTH, D_MODEL), 0.02)
    w_ada = nrm((DEPTH, D_MODEL, 3 * D_MODEL), D_MODEL ** -0.5)
    b_ada = nrm((DEPTH, 3 * D_MODEL), 0.02)
    w_in_e = nrm((N_EVEN, D_MODEL, N_IN_E), D_MODEL ** -0.5)
    w_out_e = nrm((N_EVEN, D_A + D_B, D_MODEL), (D_A + D_B) ** -0.5)
    conv_a_w = nrm((N_EVEN, CONV_W, D_XBC), CONV_W ** -0.5)
    conv_a_b = nrm((N_EVEN, D_XBC), 0.02)
    dt0 = jnp.exp(jax.random.uniform(next(ks), (N_EVEN, 2, H_A), jnp.float32, math.log(DT_MIN), math.log(DT_MAX)))
    dt_bias = dt0 + jnp.log(-jnp.expm1(-dt0))
    a_log = jnp.log(jax.random.uniform(next(ks), (N_EVEN, 2, H_A), jnp.float32, 1.0, 16.0))
    d_skip = 1.0 + nrm((N_EVEN, 2, H_A), 0.1)
    norm_a_w = 1.0 + nrm((N_EVEN, D_A), 0.02)
    conv_b_w = nrm((N_EVEN, CONV_W, 3 * D_B), CONV_W ** -0.5)
    conv_b_b = nrm((N_EVEN, 3 * D_B), 0.02)
    hf_w1 = nrm((N_EVEN, HY_EMB, HY_HID), HY_EMB ** -0.5)
    hf_b1 = nrm((N_EVEN, HY_HID), 0.02)
    hf_w2 = nrm((N_EVEN, HY_HID, HY_HID), HY_HID ** -0.5)
    hf_b2 = nrm((N_EVEN, HY_HID), 0.02)
    hf_w3 = nrm((N_EVEN, HY_HID, 2 * D_B), 0.02 * HY_HID ** -0.5)
    hf_freq = 1.0 + nrm((N_EVEN, HY_HID), 0.1)
    hy_bias = nrm((N_EVEN, D_B), 1.0)
    w_in_o = nrm((N_ODD, D_MODEL, N_IN_O), D_MODEL ** -0.5)
    w_out_o = nrm((N_ODD, D_C, D_MODEL), D_C ** -0.5)
    rpb = nrm((N_ODD, H_C, 2 * WIN_H - 1, 2 * WIN_W - 1), 0.05)
    final_norm_w = 1.0 + nrm((D_MODEL,), 0.02)
    return {"x_prompt": x_prompt, "x_sample": x_sample, "state_ssd": state_ssd, "cache_k": cache_k,
            "cache_v": cache_v, "c": c, "c_ctx": c_ctx, "norm_w": norm_w, "w_ada": w_ada, "b_ada": b_ada,
            "w_in_e": w_in_e, "w_out_e": w_out_e, "conv_a_w": conv_a_w, "conv_a_b": conv_a_b,
            "dt_bias": dt_bias, "a_log": a_log, "d_skip": d_skip, "norm_a_w": norm_a_w,
            "conv_b_w": conv_b_w, "conv_b_b": conv_b_b, "hf_w1": hf_w1, "hf_b1": hf_b1, "hf_w2": hf_w2,
            "hf_b2": hf_b2, "hf_w3": hf_w3, "hf_freq": hf_freq, "hy_bias": hy_bias, "w_in_o": w_in_o,
            "w_out_o": w_out_o, "rpb": rpb, "final_norm_w": final_norm_w}


def reference(x_prompt, x_sample, state_ssd, cache_k, cache_v, c, c_ctx, norm_w, w_ada, b_ada,
              w_in_e, w_out_e, conv_a_w, conv_a_b, dt_bias, a_log, d_skip, norm_a_w,
              conv_b_w, conv_b_b, hf_w1, hf_b1, hf_w2, hf_b2, hf_w3, hf_freq, hy_bias,
              w_in_o, w_out_o, rpb, final_norm_w):
    xp, xs = x_prompt, x_sample
    new_ssd, new_k, new_v = [], [], []
    for l in range(DEPTH):
        i = l // 2
        sh_p, sc_p, g_p = ada_mod(c_ctx[None], w_ada[l], b_ada[l])
        sh_s, sc_s, g_s = ada_mod(c, w_ada[l], b_ada[l])
        hp = rms_norm(xp, norm_w[l]) * (1.0 + sc_p) + sh_p
        hs = rms_norm(xs, norm_w[l]) * (1.0 + sc_s) + sh_s
        if l % 2 == 0:
            ep = (w_in_e[i], w_out_e[i], conv_a_w[i], conv_a_b[i], dt_bias[i], a_log[i], d_skip[i], norm_a_w[i],
                  conv_b_w[i], conv_b_b[i], hf_w1[i], hf_b1[i], hf_w2[i], hf_b2[i], hf_w3[i], hf_freq[i], hy_bias[i])
            zeros_state = jnp.zeros((hp.shape[0], 2, H_A, P_A, N_A), hp.dtype)
            op, fin_ctx = mixer_ssd_hyena(hp, zeros_state, *ep)
            os_, _ = mixer_ssd_hyena(hs, state_ssd[:, i], *ep)
            new_ssd.append(fin_ctx)
        else:
            op, k_ctx, v_ctx = na_context(hp, w_in_o[i], w_out_o[i])
            os_ = na_latent(hs, cache_k[:, i], cache_v[:, i], rpb[i], w_in_o[i], w_out_o[i])
            new_k.append(k_ctx)
            new_v.append(v_ctx)
        xp = xp + g_p * op
        xs = xs + g_s * os_
    y_prompt = rms_norm(xp, final_norm_w)
    y_sample = rms_norm(xs, final_norm_w)
    new_state_ssd = jnp.stack(new_ssd, axis=1)
    new_cache_k = jnp.stack(new_k, axis=1)
    new_cache_v = jnp.stack(new_v, axis=1)
    return (y_prompt, y_sample, new_state_ssd, new_cache_k, new_cache_v)
```

```python
import contextlib
import math
import numpy as np
import concourse.bass as bass
import concourse.mybir as mybir
from concourse.bass_utils import run_bass_kernel_spmd

F32 = mybir.dt.float32
BF16 = mybir.dt.bfloat16
AF = mybir.ActivationFunctionType
ALU = mybir.AluOpType
AX = mybir.AxisListType

D = 1024
L_P = 256
NSEQ = 4
EPS = 1e-6
NEG = -30000.0
TWO_PI = 2.0 * math.pi
DO_PROMPT = True
DO_SAMPLE = True
STAGE_S = 99
L_S = 4096
NWIN = 9


class Prog:
    R = 8

    def __init__(self, nc):
        self.nc = nc
        self.q = {e: [] for e in ("pe", "act", "dve", "pool", "sp")}
        self.cnt = {e: 0 for e in self.q}
        self.dma_cnt = {e: 0 for e in self.q}
        self.lastw = {}
        self.readers = {}
        self.seen = {e: {} for e in self.q}
        self.nbank = 0
        self.last_tok = {e: [] for e in self.q}
        self.snaps = {}

    def _need(self, eng, tok, waits):
        if tok is None:
            return
        semkey, val, teng = tok
        if teng == eng and eng == "pe":
            return
        cur = self.seen[eng].get(semkey, 0)
        if cur >= val:
            return
        self.seen[eng][semkey] = val
        waits.append((semkey, val))
        se = self.seen[eng]
        for sk2, v2 in self.snaps.get((semkey, val), ()):
            if se.get(sk2, 0) < v2:
                se[sk2] = v2

    def op(self, eng, fn, reads=(), writes=(), dma=False):
        writes = list(writes) + [r for r in reads if isinstance(r, tuple) and r[0] == "ps" and r not in writes]
        waits = []
        for r in reads:
            self._need(eng, self.lastw.get(r), waits)
        for w in writes:
            self._need(eng, self.lastw.get(w), waits)
            for t in self.readers.get(w, ()):
                self._need(eng, t, waits)
        if dma:
            j = self.dma_cnt[eng]
            self.dma_cnt[eng] += 1
            R = 1 if eng == "pool" else self.R
            semkey = ("dma", eng, j % R)
            val = 16 * (j // R + 1)
            if j >= R:
                self._need(eng, (semkey, val - 16, "dma"), waits)
            tok = (semkey, val, "dma")
            self.last_tok[eng] = (self.last_tok[eng] + [tok])[-R:]
        else:
            self.cnt[eng] += 1
            semkey = ("eng", eng)
            tok = (semkey, self.cnt[eng], eng)
        self.q[eng].append((fn, waits, semkey, dma))
        self.snaps[(tok[0], tok[1])] = tuple(self.seen[eng].items())
        for w in writes:
            self.lastw[w] = tok
            self.readers[w] = []
        for r in reads:
            if r not in writes:
                self.readers.setdefault(r, []).append(tok)
        return tok

    def barrier(self):
        toks = []
        for e in self.q:
            if e in ("pe", "act", "dve", "pool") and self.cnt[e]:
                toks.append((("eng", e), self.cnt[e], e))
            toks += self.last_tok[e]
        for e in self.q:
            waits = []
            for t in toks:
                self._need(e, t, waits)
            if waits:
                self.q[e].append((None, waits, None, False))

    def finish(self, eng, toks):
        waits = []
        for t in toks:
            self._need(eng, t, waits)
        self.q[eng].append((None, waits, None, False))

    def emit(self):
        nc = self.nc
        semkeys = set()
        for e, lst in self.q.items():
            for fn, waits, semkey, dma in lst:
                if semkey is not None:
                    semkeys.add(semkey)
                for (sk, v) in waits:
                    semkeys.add(sk)
        sems = {}
        with contextlib.ExitStack() as st:
            for sk in sorted(semkeys, key=str):
                sems[sk] = st.enter_context(nc.semaphore("s_" + "_".join(str(x) for x in sk)))
            block = st.enter_context(nc.Block())
            engmap = {"pe": block.tensor, "act": block.scalar, "dve": block.vector,
                      "pool": block.gpsimd, "sp": block.sync}

            def make(e):
                lst = self.q[e]

                def body(eng):
                    for fn, waits, semkey, dma in lst:
                        fuse = (fn is not None) and (not dma) and len(waits) > 0 and e != "pool"
                        for (sk, v) in (waits[:-1] if fuse else waits):
                            eng.wait_ge(sems[sk], v)
                        if fn is None:
                            continue
                        n0 = nc.n_instructions()
                        ins = fn(eng)
                        if fuse:
                            assert nc.n_instructions() - n0 == 1, ("multi-instruction op cannot carry a fused wait", e)
                            ins._wait_ge(sems[waits[-1][0]], waits[-1][1])
                        ins.then_inc(sems[semkey], 16 if dma else 1)
                return body
            for e in self.q:
                if self.q[e]:
                    engmap[e](make(e))


def fm(v, nt):
    return np.ascontiguousarray(np.asarray(v, np.float32).reshape(nt, 128).T)


def hyena_feats(L):
    f32 = np.float32
    t = np.linspace(0.0, 1.0, L, dtype=f32)[:, None]
    w = (f32(2.0 * math.pi) * np.arange(L, dtype=f32)[:, None] / f32(L)).astype(f32)
    f = np.linspace(1e-4, 15, 16, dtype=f32)[None]
    fw = (f * w).astype(f32)
    feats = np.concatenate([t, np.cos(fw), -np.sin(fw)], -1).astype(f32)
    allf = np.zeros((2 * L, 33), f32)
    allt = np.zeros((2 * L,), f32)
    allf[:L] = feats
    allt[:L] = t[:, 0]
    for j in range(L + 1, 2 * L):
        allf[j] = feats[2 * L - j]
        allt[j] = t[2 * L - j, 0]
    return np.ascontiguousarray(allf.T), allt


def build_nc():
    nc = bass.Bass("TRN2", target_bir_lowering=False)
    P = Prog(nc)

    def din(name, shape, dt=F32):
        return nc.dram_tensor(name, list(shape), dt, kind="ExternalInput").ap()

    def dout(name, shape, dt=F32):
        return nc.dram_tensor(name, list(shape), dt, kind="ExternalOutput").ap()

    ARENA_BYTES = 207 * 1024
    arena = nc.alloc_sbuf_tensor("arena", [128, ARENA_BYTES // 4], F32)
    cur = [0]

    def sb(name, shape, dt=F32):
        item = 4 if dt == F32 else 2
        n = 1
        for s_ in shape[1:]:
            n *= s_
        nbytes = (n * item + 31) // 32 * 32
        off = cur[0]
        cur[0] += nbytes
        assert cur[0] <= ARENA_BYTES, (name, cur[0])
        v = arena[:, off // 4:(off + nbytes) // 4]
        if dt != F32:
            v = v.bitcast(dt)
        v = v[0:shape[0], 0:n]
        if len(shape) == 3:
            v = v.rearrange("p (a b) -> p a b", b=shape[2])
        elif len(shape) == 4:
            v = v.rearrange("p (a b c) -> p a b c", b=shape[2], c=shape[3])
        return v

    def mm(out, lhsT, rhs, start, stop, reads, writes):
        P.op("pe", lambda e: e.matmul(out, lhsT=lhsT, rhs=rhs, start=start, stop=stop), reads, writes)

    def tr(out, in_, ident, reads, writes):
        P.op("pe", lambda e: e.transpose(out=out, in_=in_, identity=ident), reads, writes)

    def act(out, in_, func, reads, writes, bias=None, scale=None, accum=None):
        kw = {}
        if bias is not None:
            kw["bias"] = bias
        if scale is not None:
            kw["scale"] = scale
        if accum is not None:
            kw["accum_out"] = accum
        P.op("act", lambda e: e.activation(out=out, in_=in_, func=func, **kw), reads, writes)

    def tt(eng, out, in0, in1, op, reads, writes):
        P.op(eng, lambda e: e.tensor_tensor(out=out, in0=in0, in1=in1, op=op), reads, writes)

    def ts(eng, out, in0, s1, s2, op0, op1, reads, writes):
        if s2 is None:
            P.op(eng, lambda e: e.tensor_scalar(out=out, in0=in0, scalar1=s1, scalar2=None, op0=op0), reads, writes)
        else:
            P.op(eng, lambda e: e.tensor_scalar(out=out, in0=in0, scalar1=s1, scalar2=s2, op0=op0, op1=op1), reads, writes)

    def stt(eng, out, in0, scalar, in1, op0, op1, reads, writes):
        P.op(eng, lambda e: e.scalar_tensor_tensor(out=out, in0=in0, scalar=scalar, in1=in1, op0=op0, op1=op1), reads, writes)

    def cp(eng, out, in_, reads, writes):
        if eng == "act":
            P.op("act", lambda e: e.activation(out=out, in_=in_, func=AF.Copy), reads, writes)
        else:
            P.op(eng, lambda e: e.tensor_copy(out=out, in_=in_), reads, writes)

    def dma(eng, out, in_, reads, writes):
        return P.op(eng, lambda e: e.dma_start(out=out, in_=in_), reads, writes, dma=True)

    def memset(out, val, writes, eng="pool"):
        P.op(eng, lambda e: e.memset(out, val), (), writes)

    xp = din("xp", [NSEQ, L_P, D])
    cv = din("cv", [128, 8, 2])
    w_ada = din("w_ada", [2, D, 3 * D])
    b_ada_fm = din("b_ada_fm", [2, 128, 24])
    norm_w_fm = din("norm_w_fm", [2, 128, 8])
    w_in_e = din("w_in_e", [D, 6688])
    w_out_e = din("w_out_e", [2 * D, D])
    w_in_o = din("w_in_o", [D, 4 * D])
    w_out_o = din("w_out_o", [D, D])
    conv_a_fm = din("conv_a_fm", [128, 12, 4])
    conv_b_fm = din("conv_b_fm", [128, 24, 4])
    ssd_par = din("ssd_par", [32, 2])
    sel_c = din("sel_c", [32, 32 * 128])
    dskip_rep = din("dskip_rep", [2, D])
    naw_fm = din("naw_fm", [128, 8])
    hyb_fm = din("hyb_fm", [128, 8])
    fnw = din("fnw", [D])
    featsP = din("featsP", [33, 2 * L_P])
    tnP = din("tnP", [2 * L_P])
    hfw1 = din("hfw1", [33, 64])
    hfw2 = din("hfw2", [64, 64])
    hfw3 = din("hfw3", [64, 2 * D])
    hfpar = din("hfpar", [64, 3])
    ndelta_fm = din("ndelta_fm", [128, 8])
    dftP = din("dftP", [4, 512, 512])

    xs_in = din("xs_in", [L_S, D])
    w_in_es4 = din("w_in_es", [4, D, 1800])
    conva_s_fm4 = din("conva_s_fm", [4, 128, 4, 4])
    convb_s_fm4 = din("convb_s_fm", [4, 128, 6, 4])
    ssd_par_s4 = din("ssd_par_s", [4, 8, 2])
    dskip_s4 = din("dskip_s", [4, 2, 256])
    st_in4 = din("st_in", [4, 2, 256, 128])
    featsS = din("featsS", [33, 2 * L_S])
    tnS = din("tnS", [2 * L_S])
    hfw3_s4 = din("hfw3_s", [4, 64, 512])
    hyp_s_fm4 = din("hyp_s_fm", [4, 128, 2, 2])
    w1tab = din("w1tab", [64, 128])
    w1inv = din("w1inv", [128, 32])
    w128 = din("w128", [3, 128, 128])
    twid = din("twid", [2, 128, 64])
    x_ext = din("x_ext", [2048, D])
    blkmask = din("blkmask", [16])
    w_in_o_pairs = din("w_in_o_pairs", [8, D, 512])
    ckT_in = din("ckT_in", [8, 128, 256])
    cv_in = din("cv_in", [8, 128, 2, 128])
    strip_in = din("strip_in", [8, 128, 19 * 64])
    rm_in = din("rm_in", [16, 18 * 64])
    y_all_dbg = None
    x1_d = nc.dram_tensor("x1_d", [8, 128, 2048], F32, kind="Internal").ap()
    h2_d = nc.dram_tensor("h2_d", [64, 2 * L_S], F32, kind="Internal").ap()
    hTw_d = nc.dram_tensor("hTw_d", [NWIN, 8, 128, 512], BF16, kind="Internal").ap()
    o_ys = dout("o_ys", [1024, D])
    zs_d = nc.dram_tensor("zs_d", [2, 128, L_S], BF16, kind="Internal").ap()
    gs_d = nc.dram_tensor("gs_d", [2, 128, L_S], BF16, kind="Internal").ap()
    x0_d = nc.dram_tensor("x0_d", [2, 128, L_S], BF16, kind="Internal").ap()
    dt_d = nc.dram_tensor("dt_d", [8, L_S], F32, kind="Internal").ap()
    y_all = nc.dram_tensor("y_all", [2 * D, L_S], BF16, kind="Internal").ap()
    o_dbg = None

    o_state = dout("o_state", [NSEQ, 2, 1024, 128])
    o_yp = dout("o_yp", [NSEQ, L_P, D])
    o_k = dout("o_k", [NSEQ, 16, L_P, 64])
    o_v = dout("o_v", [NSEQ, 16, L_P, 64])
    out_toks = []

    ps = [nc.alloc_psum_tensor(f"ps{i}", [128, 512], F32) for i in range(8)]

    def bank():
        i = P.nbank % 8
        P.nbank += 1
        return i

    def psb(b):
        return ps[b][:].bitcast(BF16)

    identf = sb("identf", [128, 128])
    identb = sb("identb", [128, 128], BF16)
    onesf = sb("onesf", [128, 128])
    maskF = sb("maskF", [128, 128])
    maskB = sb("maskB", [128, 128])
    triLE = sb("triLE", [128, 128])
    triGE = sb("triGE", [128, 128])
    zer = sb("zer", [128, 128])
    memset(zer, 0.0, ["zer"])
    memset(onesf, 1.0, ["onesf"])

    def aff(out, in_, pattern, cmp_op, fill, cm, reads, writes):
        P.op("pool", lambda e: e.affine_select(out=out, in_=in_, pattern=pattern, compare_op=cmp_op, fill=fill, base=0,
                                               channel_multiplier=cm), reads, writes)
    aff(identf, zer, [[-1, 128]], ALU.not_equal, 1.0, 1, ["zer"], ["identf"])
    aff(maskF, zer, [[1, 128]], ALU.is_ge, NEG, -1, ["zer"], ["maskF"])
    aff(maskB, zer, [[-1, 128]], ALU.is_ge, NEG, 1, ["zer"], ["maskB"])
    aff(triLE, onesf, [[1, 128]], ALU.is_ge, 0.0, -1, ["onesf"], ["triLE"])
    aff(triGE, onesf, [[-1, 128]], ALU.is_ge, 0.0, 1, ["onesf"], ["triGE"])
    cp("dve", identb, identf, ["identf"], ["identb"])

    sel = sb("sel", [32, 32, 128])
    dma("sp", sel, sel_c.rearrange("k (j m) -> k j m", m=128), [], ["sel"])
    conva = sb("conva", [128, 12, 4])
    dma("sp", conva, conv_a_fm, [], ["conva"])
    convb = sb("convb", [128, 24, 4])
    dma("sp", convb, conv_b_fm, [], ["convb"])
    ssdp = sb("ssdp", [32, 2])
    dma("sp", ssdp, ssd_par, [], ["ssdp"])
    aneg = sb("aneg", [32, 1])
    act(aneg, ssdp[:, 1:2], AF.Exp, ["ssdp"], ["aneg"])
    ts("dve", aneg, aneg, -1.0, None, ALU.mult, None, ["aneg"], ["aneg"])
    naw = sb("naw", [128, 8])
    dma("sp", naw, naw_fm, [], ["naw"])
    hyb = sb("hyb", [128, 8])
    dma("sp", hyb, hyb_fm, [], ["hyb"])
    dsk = sb("dsk", [128, D])
    dsk2 = sb("dsk2", [128, D])
    dma("sp", dsk, dskip_rep[0].partition_broadcast(128), [], ["dsk"])
    dma("sp", dsk2, dskip_rep[1].partition_broadcast(128), [], ["dsk2"])
    tt("dve", dsk, dsk, dsk2, ALU.add, ["dsk", "dsk2"], ["dsk"])
    fnw_bc = dsk2
    dma("sp", fnw_bc, fnw.partition_broadcast(128), ["dsk"], ["dsk2"])

    cvt = sb("cvt", [128, 8, 2])
    cvs = sb("cvs", [128, 8, 2], BF16)
    dma("sp", cvt, cv, [], ["cvt"])
    act(cvs, cvt, AF.Silu, ["cvt"], ["cvs"])
    bada = sb("bada", [128, 2, 24])
    nw = sb("nw", [128, 2, 8])
    dma("sp", bada, b_ada_fm.rearrange("l p j -> p l j"), [], ["bada"])
    dma("sp", nw, norm_w_fm.rearrange("l p j -> p l j"), [], ["nw"])
    mod = sb("mod", [128, 2, 24, 2])
    modA = sb("modA", [128, 2, 8, 2])
    persist_mark = cur[0]
    wa = sb("wa", [128, 8, 3 * D], BF16)
    for l in range(2):
        for half in range(2):
            dma("pool", wa[:, half * 4:(half + 1) * 4, :],
                w_ada[l, half * 512:(half + 1) * 512, :].rearrange("(kc p) n -> p kc n", p=128), [], [("wa", half)])
        b = bank()
        for j in range(24):
            for kc in range(8):
                mm(ps[b][:, j * 2:(j + 1) * 2], wa[:, kc, j * 128:(j + 1) * 128], cvs[:, kc, :], kc == 0, kc == 7,
                   [("wa", kc // 4), "cvs"], [("ps", b)])
        tt("dve", mod[:, l], ps[b][:, 0:48].rearrange("p (j c) -> p j c", c=2),
           bada[:, l, :].unsqueeze(2).to_broadcast([128, 24, 2]), ALU.add, [("ps", b), "bada"], [("mod", l)])
        stt("dve", modA[:, l], mod[:, l, 8:16, :], 1.0, nw[:, l, :].unsqueeze(2).to_broadcast([128, 8, 2]),
            ALU.add, ALU.mult, [("mod", l), "nw"], [("modA", l)])
    P.barrier()
    cur[0] = persist_mark

    prompt_mark = cur[0]
    dft = sb("dft", [128, 4, 4, 512], BF16)
    for t_ in range(4):
        dma("pool", dft[:, t_], dftP[t_].rearrange("(c p) n -> p c n", p=128), [], [("dft", t_)])
    Kre = sb("Kre", [128, 4, 1024], BF16)
    Kim = sb("Kim", [128, 4, 1024], BF16)
    filt_mark = cur[0]
    if True:
        n2 = 2 * L_P
        fe = sb("fe", [33, n2])
        tnb = sb("tnb", [128, n2])
        w1s = sb("w1s", [33, 64])
        w2s = sb("w2s", [64, 64])
        w3s = sb("w3s", [64, 2 * D])
        hp_ = sb("hp_", [64, 3])
        hsc = sb("hsc", [64, 4])
        ndl = sb("ndl", [128, 8])
        h1 = sb("h1", [64, n2])
        h2 = sb("h2", [64, n2])
        kT = sb("kT", [128, 8, n2], BF16)
        dec = sb("dec", [128, n2])
        ktok = sb("ktok", [128, 4, 1024], BF16)
        dma("sp", fe, featsP, [], ["fe"])
        dma("sp", tnb, tnP.partition_broadcast(128), [], ["tnb"])
        dma("sp", w1s, hfw1, [], ["w1s"])
        dma("sp", w2s, hfw2, [], ["w2s"])
        dma("sp", w3s, hfw3, [], ["w3s"])
        dma("sp", hp_, hfpar, [], ["hp_"])
        dma("sp", ndl, ndelta_fm, [], ["ndl"])
        ts("dve", hsc[:, 0:1], hp_[:, 2:3], 1.0 / TWO_PI, None, ALU.mult, None, ["hp_"], ["hsc"])
        for j in range(2):
            tt("dve", hsc[:, 1 + j:2 + j], hp_[:, j:j + 1], hsc[:, 0:1], ALU.mult, ["hp_", "hsc"], ["hsc"])

        MAGIC = 12582912.0
        kk = sb("kk", [64, n2])

        def sin_layer(dst, src_w, src_x, j):
            b = bank()
            mm(ps[b][0:64, 0:n2], src_w, src_x, True, True, ["w1s", "w2s", "fe", "h1"], [("ps", b)])
            ts("dve", dst, ps[b][0:64, 0:n2], hsc[:, 0:1], hsc[:, 1 + j:2 + j], ALU.mult, ALU.add, [("ps", b), "hsc"], ["hl"])
            ts("dve", kk, dst, MAGIC, None, ALU.add, None, ["hl"], ["kk"])
            ts("dve", kk, kk, -MAGIC, None, ALU.add, None, ["kk"], ["kk"])
            tt("dve", dst, dst, kk, ALU.subtract, ["hl", "kk"], ["hl"])
            ts("dve", dst, dst, TWO_PI, None, ALU.mult, None, ["hl"], ["hl"])
            ts("dve", dst, dst, math.pi, -math.pi, ALU.min, ALU.max, ["hl"], ["hl"])
            act(dst, dst, AF.Sin, ["hl"], ["h1", "h2"])
        sin_layer(h1, w1s, fe, 0)
        sin_layer(h2, w2s, h1, 1)
        for ct in range(8):
            b = bank()
            mm(ps[b][:, 0:L_P], w3s[:, ct * 128:(ct + 1) * 128], h2[:, 0:L_P], True, True, ["w3s", "h2"], [("ps", b)])
            mm(ps[b][:, L_P:n2], w3s[:, D + ct * 128:D + (ct + 1) * 128], h2[:, L_P:n2], True, True, ["w3s", "h2"], [("ps", b)])
            act(dec, tnb, AF.Exp, ["tnb", "ndl"], ["dec"], scale=ndl[:, ct:ct + 1])
            tt("dve", kT[:, ct, :], ps[b][:, 0:n2], dec, ALU.mult, [("ps", b), "dec"], ["kT"])
        memset(kT[:, :, L_P:L_P + 1], 0.0, ["kT"], eng="dve")
        for dc in range(4):
            for half in range(2):
                b = bank()
                for j in range(4):
                    ct = half * 4 + j
                    tr(psb(b)[:, j * 128:(j + 1) * 128], kT[:, ct, dc * 128:(dc + 1) * 128], identb, ["kT", "identb"], [("ps", b)])
                cp("act", ktok[:, dc, half * 512:(half + 1) * 512], psb(b)[:, 0:512], [("ps", b)], ["ktok"])
        for ft in range(4):
            for half in range(2):
                for ri in range(2):
                    b = bank()
                    for dc in range(4):
                        mm(ps[b][:, :], dft[:, ri, dc, ft * 128:(ft + 1) * 128], ktok[:, dc, half * 512:(half + 1) * 512],
                           dc == 0, dc == 3, [("dft", ri), "ktok"], [("ps", b)])
                    cp("act", (Kim if ri else Kre)[:, ft, half * 512:(half + 1) * 512], ps[b][:, :],
                       [("ps", b)], ["Kim" if ri else "Kre"])
    P.barrier()
    cur[0] = filt_mark

    xin = sb("xin", [128, 2, D])
    xT = sb("xT", [128, 8, L_P])
    scr8 = sb("scr8", [128, 8, L_P])
    rstd = sb("rstd", [128, L_P])
    hT = sb("hT", [128, 8, L_P], BF16)
    wbuf = [sb(f"wbuf{i}", [128, 8, 512], BF16) for i in range(2)]
    tmpc = [sb(f"tmpc{i}", [128, L_P]) for i in range(2)]
    sm = sb("sm", [128, 8])
    l0_mark = cur[0]
    xbcT = sb("xbcT", [128, 12, L_P], BF16)
    dtraw = sb("dtraw", [32, L_P])
    dtT = sb("dtT", [32, L_P])
    adtT = sb("adtT", [32, L_P])
    dt_tok = sb("dt_tok", [128, 2, 32])
    adt_tok = sb("adt_tok", [128, 2, 32])
    cs_tok = sb("cs_tok", [128, 2, 32])
    ncs_tok = sb("ncs_tok", [128, 2, 32])
    dec_tok = sb("dec_tok", [128, 2, 32])
    ecs_tok = sb("ecs_tok", [128, 2, 32])
    etot = sb("etot", [128, 2, 32])
    dtdec_tok = sb("dtdec_tok", [128, 2, 32])
    csT = sb("csT", [32, 2, 128])
    x_tok = sb("x_tok", [128, 2, 1024], BF16)
    B_tok = sb("B_tok", [128, 2, 2, 128], BF16)
    xdt = sb("xdt", [128, 2, 2, 1024], BF16)
    xdd = sb("xdd", [128, 2, 2, 1024], BF16)
    GT = sb("GT", [128, 2, 2, 128])
    t1 = [sb(f"t1_{i}", [128, 4, 128]) for i in range(1)]
    t2 = [sb(f"t2_{i}", [128, 4, 128]) for i in range(1)]
    MT = [sb(f"MT_{i}", [128, 4, 128], BF16) for i in range(2)]
    y_tok = sb("y_tok", [128, 2, 1024])
    ytmp = sb("ytmp", [128, 512])
    ST = [sb(f"ST{d}", [128, 1024]) for d in range(2)]
    STb = [sb(f"STb{d}", [128, 1024], BF16) for d in range(2)]
    stout = xin.rearrange("p c d -> p (c d)")[:, 0:1024].rearrange("p (a n) -> p a n", n=128)
    ssd_end = cur[0]
    u16 = sb("u16", [128, 24, L_P], BF16)
    zs = sb("zs", [128, 8, L_P], BF16)
    gs = sb("gs", [128, 8, L_P], BF16)
    ygT = sb("ygT", [128, 8, L_P], BF16)
    ybT = sb("ybT", [128, 8, L_P], BF16)
    rstdy = sb("rstdy", [128, L_P])
    l0_end = cur[0]
    cur[0] = l0_mark
    uvb = sb("uvb", [128, 8, L_P], BF16)
    uv_tok = sb("uv_tok", [128, 2, 1024], BF16)
    Xs = [sb(f"Xs{i}", [128, 512]) for i in range(2)]
    ta = [sb(f"ta{i}", [128, 512]) for i in range(2)]
    Yre = sb("Yre", [128, 4, 1024], BF16)
    Yim = sb("Yim", [128, 4, 1024], BF16)
    assert cur[0] <= ssd_end
    cur[0] = l0_mark
    qT = sb("qT", [128, 8, L_P], BF16)
    QBD = sb("QBD", [128, 8, 4, 128], BF16)
    kTf = sb("kTf", [128, 8, L_P])
    vTf = sb("vTf", [128, 8, L_P])
    kTb = sb("kTb", [128, 8, L_P], BF16)
    v_tok = sb("v_tok", [128, 2, 1024], BF16)
    gs1 = sb("gs1", [128, 8, L_P], BF16)
    ogT = sb("ogT", [128, 8, L_P], BF16)
    Pm = [sb(f"Pm{i}", [128, L_P], BF16) for i in range(2)]
    PT = [sb(f"PT{i}", [128, 2, 128], BF16) for i in range(2)]
    assert cur[0] <= l0_end
    cur[0] = l0_end
    print("SBUF bytes used per partition:", cur[0])

    gcount = [0]

    def load_w(wsrc, row0, col0, ncols):
        g = gcount[0]
        gcount[0] += 1
        wb = wbuf[g % 2]
        dma("pool", wb[:, :, 0:ncols], wsrc[row0:row0 + 1024, col0:col0 + ncols].rearrange("(kc p) n -> p kc n", p=128),
            [], [("wbuf", g % 2)])
        return wb, ("wbuf", g % 2)

    def proj_tile(wb, wkey, off, m, src, srckey, evac, b=None):
        if b is None:
            b = bank()
        for kc in range(8):
            mm(ps[b][0:m, 0:L_P], wb[:, kc, off:off + m], src[:, kc, :], kc == 0, kc == 7, [wkey, srckey], [("ps", b)])
        if evac is not None:
            evac(b)
        return b

    def norm_mod(layer):
        act(scr8, xT, AF.Square, ["xT"], ["scr8"])
        b = bank()
        for ft in range(8):
            mm(ps[b][:, 0:L_P], onesf, scr8[:, ft, :], ft == 0, ft == 7, ["onesf", "scr8"], [("ps", b)])
        ts("dve", rstd, ps[b][:, 0:L_P], 1.0 / D, EPS, ALU.mult, ALU.add, [("ps", b)], ["rstd"])
        act(rstd, rstd, AF.Sqrt, ["rstd"], ["rstd"])
        P.op("dve", lambda e: e.reciprocal(out=rstd, in_=rstd), ["rstd"], ["rstd"])
        for ft in range(8):
            stt("dve", scr8[:, ft, :], xT[:, ft, :], modA[:, layer, ft, 0:1], rstd, ALU.mult, ALU.mult,
                ["xT", ("modA", layer), "rstd"], ["scr8"])
            act(hT[:, ft, :], scr8[:, ft, :], AF.Identity, ["scr8", ("mod", layer)], ["hT"], bias=mod[:, layer, ft, 0:1], scale=1.0)

    def conv_evac(b, dst, cw, i, silu, key):
        tc_ = tmpc[i % 2]
        tk = ("tmpc", i % 2)
        ts("dve", tc_, ps[b][:, 0:L_P], cw[:, i, 1:2], cw[:, i, 3:4], ALU.mult, ALU.add, [("ps", b), "conva", "convb"], [tk])
        stt("dve", tc_[:, 1:L_P], ps[b][:, 0:L_P - 1], cw[:, i, 0:1], tc_[:, 1:L_P], ALU.mult, ALU.add, [("ps", b), tk], [tk])
        stt("dve", tc_[:, 0:L_P - 1], ps[b][:, 1:L_P], cw[:, i, 2:3], tc_[:, 0:L_P - 1], ALU.mult, ALU.add, [("ps", b), tk], [tk])
        if silu:
            act(dst, tc_, AF.Silu, [tk], [key])
        else:
            cp("act", dst, tc_, [tk], [key])

    def seq_body(s):
        dma("sp", xin, xp[s].rearrange("(tt p) d -> p tt d", p=128), [], ["xin"])
        for ft in range(8):
            b = bank()
            for tt_ in range(2):
                tr(ps[b][:, tt_ * 128:(tt_ + 1) * 128], xin[:, tt_, ft * 128:(ft + 1) * 128], identf, ["xin", "identf"], [("ps", b)])
            cp("act", xT[:, ft, :], ps[b][:, 0:L_P], [("ps", b)], ["xT"])
        norm_mod(0)

        for gi in range(2):
            wb, wk = load_w(w_in_e, 0, gi * 512, 512)
            for j in range(4):
                i = gi * 4 + j
                proj_tile(wb, wk, j * 128, 128, hT, "hT",
                          lambda b, i=i: act(zs[:, i, :], ps[b][:, 0:L_P], AF.Silu, [("ps", b)], ["zs"]))
        for gi in range(3):
            wb, wk = load_w(w_in_e, 0, 1024 + gi * 512, 512)
            for j in range(4):
                i = gi * 4 + j
                proj_tile(wb, wk, j * 128, 128, hT, "hT", lambda b, i=i: conv_evac(b, xbcT[:, i, :], conva, i, True, ("xbcT", i)))
        wb, wk = load_w(w_in_e, 0, 2560, 32)
        proj_tile(wb, wk, 0, 32, hT, "hT", lambda b: cp("act", dtraw, ps[b][0:32, 0:L_P], [("ps", b)], ["dtraw"]))
        for gi in range(6):
            wb, wk = load_w(w_in_e, 0, 2592 + gi * 512, 512)
            for j in range(4):
                i = gi * 4 + j
                proj_tile(wb, wk, j * 128, 128, hT, "hT", lambda b, i=i: conv_evac(b, u16[:, i, :], convb, i, False, "u16"))
        for gi in range(2):
            wb, wk = load_w(w_in_e, 0, 5664 + gi * 512, 512)
            for j in range(4):
                i = gi * 4 + j
                proj_tile(wb, wk, j * 128, 128, hT, "hT",
                          lambda b, i=i: act(gs[:, i, :], ps[b][:, 0:L_P], AF.Silu, [("ps", b)], ["gs"]))

        act(dtT, dtraw, AF.Exp, ["dtraw", "ssdp"], ["dtT"], bias=ssdp[:, 0:1], scale=1.0)
        act(dtT, dtT, AF.Ln, ["dtT"], ["dtT"], bias=1.0, scale=1.0)
        ts("dve", adtT, dtT, aneg[:, 0:1], None, ALU.mult, None, ["dtT", "aneg"], ["adtT"])
        b = bank()
        for c in range(2):
            tr(ps[b][:, c * 32:(c + 1) * 32], dtT[:, c * 128:(c + 1) * 128], identf[0:32, 0:32], ["dtT", "identf"], [("ps", b)])
            tr(ps[b][:, 64 + c * 32:64 + (c + 1) * 32], adtT[:, c * 128:(c + 1) * 128], identf[0:32, 0:32], ["adtT", "identf"], [("ps", b)])
        cp("dve", dt_tok, ps[b][:, 0:64].rearrange("p (c j) -> p c j", j=32), [("ps", b)], ["dt_tok"])
        cp("dve", adt_tok, ps[b][:, 64:128].rearrange("p (c j) -> p c j", j=32), [("ps", b)], ["adt_tok"])
        b = bank()
        b2 = bank()
        for c in range(2):
            mm(ps[b][:, c * 32:c * 32 + 16], triLE, adt_tok[:, c, 0:16], True, True, ["triLE", "adt_tok"], [("ps", b)])
            mm(ps[b][:, c * 32 + 16:c * 32 + 32], triGE, adt_tok[:, c, 16:32], True, True, ["triGE", "adt_tok"], [("ps", b)])
            mm(ps[b2][:, c * 32:(c + 1) * 32], onesf, adt_tok[:, c, :], True, True, ["onesf", "adt_tok"], [("ps", b2)])
        cp("dve", cs_tok, ps[b][:, 0:64].rearrange("p (c j) -> p c j", j=32), [("ps", b)], ["cs_tok"])
        ts("dve", ncs_tok, cs_tok, -1.0, None, ALU.mult, None, ["cs_tok"], ["ncs_tok"])
        act(etot, ps[b2][:, 0:64].rearrange("p (c j) -> p c j", j=32), AF.Exp, [("ps", b2)], ["etot"])
        tt("dve", dec_tok, ps[b2][:, 0:64].rearrange("p (c j) -> p c j", j=32), cs_tok, ALU.subtract, [("ps", b2), "cs_tok"], ["dec_tok"])
        act(dec_tok, dec_tok, AF.Exp, ["dec_tok"], ["dec_tok"])
        act(ecs_tok, cs_tok, AF.Exp, ["cs_tok"], ["ecs_tok"])
        tt("dve", dtdec_tok, dt_tok, dec_tok, ALU.mult, ["dt_tok", "dec_tok"], ["dtdec_tok"])
        b = bank()
        for c in range(2):
            tr(ps[b][0:32, c * 128:(c + 1) * 128], cs_tok[:, c, :], identf, ["cs_tok", "identf"], [("ps", b)])
        cp("dve", csT, ps[b][0:32, 0:256].rearrange("p (c j) -> p c j", j=128), [("ps", b)], ["csT"])

        for c in range(2):
            for half in range(2):
                b = bank()
                for j in range(4):
                    i = half * 4 + j
                    tr(psb(b)[:, j * 128:(j + 1) * 128], xbcT[:, i, c * 128:(c + 1) * 128], identb, [("xbcT", i), "identb"], [("ps", b)])
                cp("act", x_tok[:, c, half * 512:(half + 1) * 512], psb(b)[:, 0:512], [("ps", b)], ["x_tok"])
            b = bank()
            for g2 in range(2):
                tr(psb(b)[:, g2 * 128:(g2 + 1) * 128], xbcT[:, 8 + g2, c * 128:(c + 1) * 128], identb, [("xbcT", 8 + g2), "identb"], [("ps", b)])
            cp("act", B_tok[:, c].rearrange("p g n -> p (g n)"), psb(b)[:, 0:256], [("ps", b)], ["B_tok"])
            b = bank()
            for g2 in range(2):
                mm(ps[b][:, g2 * 128:(g2 + 1) * 128], xbcT[:, 8 + g2, c * 128:(c + 1) * 128], xbcT[:, 10 + g2, c * 128:(c + 1) * 128],
                   True, True, [("xbcT", 8 + g2), ("xbcT", 10 + g2)], [("ps", b)])
            cp("dve", GT[:, c].rearrange("p g n -> p (g n)"), ps[b][:, 0:256], [("ps", b)], ["GT"])
            for d in range(2):
                tt("dve", xdt[:, c, d].rearrange("p (h q) -> p h q", q=64), x_tok[:, c].rearrange("p (h q) -> p h q", q=64),
                   dt_tok[:, c, d * 16:(d + 1) * 16].unsqueeze(2).to_broadcast([128, 16, 64]), ALU.mult, ["x_tok", "dt_tok"], ["xdt"])
                tt("dve", xdd[:, c, d].rearrange("p (h q) -> p h q", q=64), x_tok[:, c].rearrange("p (h q) -> p h q", q=64),
                   dtdec_tok[:, c, d * 16:(d + 1) * 16].unsqueeze(2).to_broadcast([128, 16, 64]), ALU.mult, ["x_tok", "dtdec_tok"], ["xdd"])

        for c in range(2):
            tt("dve", y_tok[:, c, :], x_tok[:, c, :], dsk, ALU.mult, ["x_tok", "dsk"], ["y_tok"])
        def p_prep(d, c, hg, k):
            mask = maskF if d == 0 else maskB
            b = bank()
            for j in range(4):
                h = hg * 4 + j
                mm(ps[b][:, j * 128:(j + 1) * 128], sel[:, d * 16 + h, :], csT[:, c, :], True, True, ["sel", "csT"], [("ps", b)])
            tt("dve", t1[0], ps[b][:, :].rearrange("p (j l) -> p j l", l=128), mask.unsqueeze(1).to_broadcast([128, 4, 128]),
               ALU.add, [("ps", b), "maskF", "maskB"], [("t1", 0)])
            for j in range(4):
                h = hg * 4 + j
                act(t2[0][:, j, :], t1[0][:, j, :], AF.Exp, [("t1", 0), "ncs_tok"], [("t2", 0)],
                    bias=ncs_tok[:, c, d * 16 + h:d * 16 + h + 1], scale=1.0)
            g2 = hg // 2
            tt("dve", MT[k], t2[0], GT[:, c, g2, :].unsqueeze(1).to_broadcast([128, 4, 128]), ALU.mult,
               [("t2", 0), "GT"], [("MT", k)])

        byb = [None]

        def p_consume(d, ci, c, hg, k):
            g2 = hg // 2
            if hg % 2 == 0:
                byb[0] = bank()
            by = byb[0]
            for j in range(4):
                h = hg * 4 + j
                hs = (h % 8) * 64
                mm(ps[by][:, hs:hs + 64], MT[k][:, j, :], xdt[:, c, d, h * 64:(h + 1) * 64], True, True,
                   [("MT", k), "xdt"], [("ps", by)])
            if hg % 2 == 1:
                tt("dve", y_tok[:, c, g2 * 512:(g2 + 1) * 512], y_tok[:, c, g2 * 512:(g2 + 1) * 512], ps[by][:, :], ALU.add,
                   ["y_tok", ("ps", by)], ["y_tok"])
            if hg != 3:
                return
            if ci > 0:
                for g2 in range(2):
                    b = bank()
                    mm(ps[b][:, :], xbcT[:, 10 + g2, c * 128:(c + 1) * 128], STb[d][:, g2 * 512:(g2 + 1) * 512], True, True,
                       [("xbcT", 10 + g2), ("STb", d)], [("ps", b)])
                    tt("dve", ytmp.rearrange("p (h q) -> p h q", q=64), ps[b][:, :].rearrange("p (h q) -> p h q", q=64),
                       ecs_tok[:, c, d * 16 + g2 * 8:d * 16 + g2 * 8 + 8].unsqueeze(2).to_broadcast([128, 8, 64]), ALU.mult,
                       [("ps", b), "ecs_tok"], ["ytmp"])
                    tt("dve", y_tok[:, c, g2 * 512:(g2 + 1) * 512], y_tok[:, c, g2 * 512:(g2 + 1) * 512], ytmp, ALU.add,
                       ["y_tok", "ytmp"], ["y_tok"])
            for g2 in range(2):
                b = bank()
                mm(ps[b][:, :], B_tok[:, c, g2, :], xdd[:, c, d, g2 * 512:(g2 + 1) * 512], True, True, ["B_tok", "xdd"], [("ps", b)])
                sl = ST[d][:, g2 * 512:(g2 + 1) * 512]
                tt("dve", sl.rearrange("p (h q) -> p h q", q=64), sl.rearrange("p (h q) -> p h q", q=64),
                   etot[:, c, d * 16 + g2 * 8:d * 16 + g2 * 8 + 8].unsqueeze(2).to_broadcast([128, 8, 64]), ALU.mult,
                   [("ST", d), "etot"], [("ST", d)])
                tt("dve", sl, sl, ps[b][:, :], ALU.add, [("ST", d), ("ps", b)], [("ST", d)])
            if ci == 0:
                cp("act", STb[d], ST[d], [("ST", d)], [("STb", d)])

        for d in range(2):
            memset(ST[d], 0.0, [("ST", d)])
            memset(STb[d], 0.0, [("STb", d)])
        p_items = []
        for d in range(2):
            for ci, c in enumerate([0, 1] if d == 0 else [1, 0]):
                for hg in range(4):
                    p_items.append((d, ci, c, hg))
        p_prep(p_items[0][0], p_items[0][2], p_items[0][3], 0)
        for d in range(2):
            for i_ in range(8 * d, 8 * d + 8):
                d_, ci, c, hg = p_items[i_]
                if i_ + 1 < len(p_items):
                    n_ = p_items[i_ + 1]
                    p_prep(n_[0], n_[2], n_[3], (i_ + 1) % 2)
                p_consume(d_, ci, c, hg, i_ % 2)
            for half in range(2):
                b = bank()
                for j in range(4):
                    i = half * 4 + j
                    tr(ps[b][:, j * 128:(j + 1) * 128], ST[d][:, i * 128:(i + 1) * 128], identf, [("ST", d), "identf"], [("ps", b)])
                cp("act", stout[:, half * 4:(half + 1) * 4, :].rearrange("p a n -> p (a n)"), ps[b][:, :], [("ps", b)], ["xin"])
            out_toks.append(dma("sp", o_state[s, d].rearrange("(a p) n -> p a n", p=128), stout, ["xin"], [("o_state", s, d)]))

        for c in range(2):
            for half in range(2):
                b = bank()
                for j in range(4):
                    ft = half * 4 + j
                    tr(ps[b][:, j * 128:(j + 1) * 128], y_tok[:, c, ft * 128:(ft + 1) * 128], identf, ["y_tok", "identf"], [("ps", b)])
                tt("dve", scr8[:, half * 4:(half + 1) * 4, c * 128:(c + 1) * 128], ps[b][:, :].rearrange("p (j l) -> p j l", l=128),
                   zs[:, half * 4:(half + 1) * 4, c * 128:(c + 1) * 128], ALU.mult, [("ps", b), "zs"], ["scr8"])
        for ft in range(8):
            act(ygT[:, ft, :], scr8[:, ft, :], AF.Copy, ["scr8", "naw"], ["ygT"], scale=naw[:, ft:ft + 1])
        act(scr8, scr8, AF.Square, ["scr8"], ["scr8"])
        b = bank()
        for ft in range(8):
            mm(ps[b][:, 0:L_P], onesf, scr8[:, ft, :], ft == 0, ft == 7, ["onesf", "scr8"], [("ps", b)])
        ts("dve", rstdy, ps[b][:, 0:L_P], 1.0 / D, EPS, ALU.mult, ALU.add, [("ps", b)], ["rstdy"])
        act(rstdy, rstdy, AF.Sqrt, ["rstdy"], ["rstdy"])
        P.op("dve", lambda e: e.reciprocal(out=rstdy, in_=rstdy), ["rstdy"], ["rstdy"])

        P.barrier()
        tt("dve", uvb, u16[:, 16:24, :], u16[:, 8:16, :], ALU.mult, ["u16"], ["uvb"])
        for c in range(2):
            for half in range(2):
                b = bank()
                for j in range(4):
                    ct = half * 4 + j
                    tr(psb(b)[:, j * 128:(j + 1) * 128], uvb[:, ct, c * 128:(c + 1) * 128], identb, ["uvb", "identb"], [("ps", b)])
                cp("act", uv_tok[:, c, half * 512:(half + 1) * 512], psb(b)[:, 0:512], [("ps", b)], ["uv_tok"])
        for ft in range(4):
            for half in range(2):
                bre, bim = bank(), bank()
                for ri, bb in ((0, bre), (1, bim)):
                    for c in range(2):
                        mm(ps[bb][:, :], dft[:, ri, c, ft * 128:(ft + 1) * 128], uv_tok[:, c, half * 512:(half + 1) * 512],
                           c == 0, c == 1, [("dft", ri), "uv_tok"], [("ps", bb)])
                cp("act", Xs[0], ps[bre][:, :], [("ps", bre)], [("Xs", 0)])
                cp("act", Xs[1], ps[bim][:, :], [("ps", bim)], [("Xs", 1)])
                kr = Kre[:, ft, half * 512:(half + 1) * 512]
                ki = Kim[:, ft, half * 512:(half + 1) * 512]
                yr = Yre[:, ft, half * 512:(half + 1) * 512]
                yi = Yim[:, ft, half * 512:(half + 1) * 512]
                tt("dve", ta[0], Xs[0], kr, ALU.mult, [("Xs", 0), "Kre"], [("ta", 0)])
                tt("dve", ta[1], Xs[1], ki, ALU.mult, [("Xs", 1), "Kim"], [("ta", 1)])
                tt("dve", yr, ta[0], ta[1], ALU.subtract, [("ta", 0), ("ta", 1)], ["Yre"])
                tt("dve", ta[1], Xs[0], ki, ALU.mult, [("Xs", 0), "Kim"], [("ta", 1)])
                tt("dve", ta[0], Xs[1], kr, ALU.mult, [("Xs", 1), "Kre"], [("ta", 0)])
                tt("dve", yi, ta[0], ta[1], ALU.add, [("ta", 0), ("ta", 1)], ["Yim"])
        for ct in range(8):
            b = bank()
            for ft in range(4):
                mm(ps[b][:, 0:L_P], Yre[:, ft, ct * 128:(ct + 1) * 128], dft[:, 2, ft, 0:L_P], ft == 0, False, ["Yre", ("dft", 2)], [("ps", b)])
            for ft in range(4):
                mm(ps[b][:, 0:L_P], Yim[:, ft, ct * 128:(ct + 1) * 128], dft[:, 3, ft, 0:L_P], False, ft == 3, ["Yim", ("dft", 3)], [("ps", b)])
            tc_ = tmpc[ct % 2]
            tk = ("tmpc", ct % 2)
            stt("dve", tc_, uvb[:, ct, :], hyb[:, ct:ct + 1], ps[b][:, 0:L_P], ALU.mult, ALU.add, ["uvb", "hyb", ("ps", b)], [tk])
            tt("dve", tc_, tc_, u16[:, ct, :], ALU.mult, [tk, "u16"], [tk])
            tt("dve", ybT[:, ct, :], tc_, gs[:, ct, :], ALU.mult, [tk, "gs"], ["ybT"])

        for nh in range(2):
            wa_, wka = load_w(w_out_e, 0, nh * 512, 512)
            wb_, wkb = load_w(w_out_e, 1024, nh * 512, 512)
            for j in range(4):
                ot = nh * 4 + j
                ba = proj_tile(wa_, wka, j * 128, 128, ygT, "ygT", None)
                bb = proj_tile(wb_, wkb, j * 128, 128, ybT, "ybT", None)
                tc_ = tmpc[ot % 2]
                tk = ("tmpc", ot % 2)
                tt("dve", tc_, ps[ba][:, 0:L_P], rstdy, ALU.mult, [("ps", ba), "rstdy"], [tk])
                tt("dve", tc_, tc_, ps[bb][:, 0:L_P], ALU.add, [tk, ("ps", bb)], [tk])
                stt("dve", xT[:, ot, :], tc_, mod[:, 0, 16 + ot, 0:1], xT[:, ot, :], ALU.mult, ALU.add, [tk, ("mod", 0), "xT"], ["xT"])

        P.barrier()
        memset(QBD, 0.0, ["QBD"])
        norm_mod(1)
        for seg in range(4):
            for gi in range(2):
                wb, wk = load_w(w_in_o, 0, seg * 1024 + gi * 512, 512)
                for j in range(4):
                    i = gi * 4 + j
                    if seg == 0:
                        ev = lambda b, i=i: cp("act", qT[:, i, :], ps[b][:, 0:L_P], [("ps", b)], ["qT"])
                    elif seg == 1:
                        def ev(b, i=i):
                            cp("act", kTf[:, i, :], ps[b][:, 0:L_P], [("ps", b)], ["kTf"])
                            cp("dve", kTb[:, i, :], ps[b][:, 0:L_P], [("ps", b)], ["kTb"])
                    elif seg == 2:
                        ev = lambda b, i=i: cp("act", vTf[:, i, :], ps[b][:, 0:L_P], [("ps", b)], ["vTf"])
                    else:
                        ev = lambda b, i=i: act(gs1[:, i, :], ps[b][:, 0:L_P], AF.Silu, [("ps", b)], ["gs1"])
                    proj_tile(wb, wk, j * 128, 128, hT, "hT", ev)
        for which, src, odram in ((0, kTf, o_k), (1, vTf, o_v)):
            for c in range(2):
                for half in range(2):
                    b = bank()
                    for j in range(4):
                        ft = half * 4 + j
                        tr(ps[b][:, j * 128:(j + 1) * 128], src[:, ft, c * 128:(c + 1) * 128], identf, ["kTf", "vTf", "identf"], [("ps", b)])
                    cp("act", xin[:, c, half * 512:(half + 1) * 512], ps[b][:, :], [("ps", b)], ["xin"])
                    if which == 1:
                        cp("dve", v_tok[:, c, half * 512:(half + 1) * 512], ps[b][:, :], [("ps", b)], ["v_tok"])
            for c in range(2):
                out_toks.append(dma("sp", odram[s][:, c * 128:(c + 1) * 128, :].rearrange("h p e -> p h e"),
                                    xin[:, c, :].rearrange("p (h e) -> p h e", e=64), ["xin"], [("o_kv", s, which, c)]))
        for hp2 in range(8):
            cp("dve", QBD[0:64, hp2, :, 0:64], qT[0:64, hp2, :].rearrange("p (qb q) -> p qb q", q=64), ["qT"], ["QBD"])
            cp("dve", QBD[64:128, hp2, :, 64:128], qT[64:128, hp2, :].rearrange("p (qb q) -> p qb q", q=64), ["qT"], ["QBD"])
        def a_S1(hp2, qb):
            b = bank()
            mm(ps[b][:, 0:L_P], QBD[:, hp2, qb, :], kTb[:, hp2, :], True, True, ["QBD", "kTb"], [("ps", b)])
            return b

        def a_S2(b, k):
            P.op("dve", lambda e, b=b: e.tensor_reduce(out=sm[:, 0:1], in_=ps[b][:, 0:L_P], axis=AX.X, op=ALU.max, negate=True),
                 [("ps", b)], ["sm0"])
            ts("dve", sm[:, 1:2], sm[:, 0:1], 0.125, None, ALU.mult, None, ["sm0"], ["sm1"])
            act(Pm[k], ps[b][:, 0:L_P], AF.Exp, [("ps", b), "sm1"], [("Pm", k), "sm2"], bias=sm[:, 1:2], scale=0.125, accum=sm[:, 2:3])
            P.op("dve", lambda e: e.reciprocal(out=sm[:, 3:4], in_=sm[:, 2:3]), ["sm2"], ["sm3"])
            ts("dve", Pm[k], Pm[k], sm[:, 3:4], None, ALU.mult, None, [("Pm", k), "sm3"], [("Pm", k)])

        def a_S3(hp2, qb, k):
            b2 = bank()
            for kt in range(2):
                tr(psb(b2)[:, kt * 128:(kt + 1) * 128], Pm[k][:, kt * 128:(kt + 1) * 128], identb, [("Pm", k), "identb"], [("ps", b2)])
            cp("act", PT[k].rearrange("p a q -> p (a q)"), psb(b2)[:, 0:256], [("ps", b2)], [("PT", k)])
            b3 = bank()
            for kt in range(2):
                mm(ps[b3][:, 0:128], v_tok[:, kt, hp2 * 128:(hp2 + 1) * 128], PT[k][:, kt, :], kt == 0, kt == 1,
                   ["v_tok", ("PT", k)], [("ps", b3)])
            tt("dve", ogT[0:64, hp2, qb * 64:(qb + 1) * 64], ps[b3][0:64, 0:64], gs1[0:64, hp2, qb * 64:(qb + 1) * 64], ALU.mult,
               [("ps", b3), "gs1"], ["ogT"])
            tt("dve", ogT[64:128, hp2, qb * 64:(qb + 1) * 64], ps[b3][64:128, 64:128], gs1[64:128, hp2, qb * 64:(qb + 1) * 64], ALU.mult,
               [("ps", b3), "gs1"], ["ogT"])

        a_items = [(hp2, qb) for hp2 in range(8) for qb in range(4)]
        nb_ = a_S1(*a_items[0])
        for i_, (hp2, qb) in enumerate(a_items):
            a_S2(nb_, i_ % 2)
            if i_ + 1 < len(a_items):
                nb_ = a_S1(*a_items[i_ + 1])
            a_S3(hp2, qb, i_ % 2)
        for nh in range(2):
            wb, wk = load_w(w_out_o, 0, nh * 512, 512)
            for j in range(4):
                ot = nh * 4 + j
                bo = proj_tile(wb, wk, j * 128, 128, ogT, "ogT", None)
                stt("dve", xT[:, ot, :], ps[bo][:, 0:L_P], mod[:, 1, 16 + ot, 0:1], xT[:, ot, :], ALU.mult, ALU.add,
                    [("ps", bo), ("mod", 1), "xT"], ["xT"])

        for c in range(2):
            for half in range(2):
                b = bank()
                for j in range(4):
                    ft = half * 4 + j
                    tr(ps[b][:, j * 128:(j + 1) * 128], xT[:, ft, c * 128:(c + 1) * 128], identf, ["xT", "identf"], [("ps", b)])
                cp("act", xin[:, c, half * 512:(half + 1) * 512], ps[b][:, :], [("ps", b)], ["xin"])
            act(scr8.rearrange("p a b -> p (a b)")[:, 0:1024], xin[:, c, :], AF.Square, ["xin"], ["scr8", "sm4"], accum=sm[:, 4:5])
            ts("dve", sm[:, 5:6], sm[:, 4:5], 1.0 / D, EPS, ALU.mult, ALU.add, ["sm4"], ["sm5"])
            act(sm[:, 5:6], sm[:, 5:6], AF.Sqrt, ["sm5"], ["sm5"])
            P.op("dve", lambda e: e.reciprocal(out=sm[:, 6:7], in_=sm[:, 5:6]), ["sm5"], ["sm6"])
            stt("dve", xin[:, c, :], xin[:, c, :], sm[:, 6:7], fnw_bc, ALU.mult, ALU.mult, ["xin", "sm6", "dsk2"], ["xin"])
        out_toks.append(dma("sp", o_yp[s].rearrange("(c p) d -> p c d", p=128), xin, ["xin"], [("o_yp", s)]))
        P.barrier()

    if DO_PROMPT:
        for s in range(NSEQ):
            seq_body(s)
    P.barrier()

    def sample_layer0(hg):
        w_in_es, conva_s_fm, convb_s_fm = w_in_es4[hg], conva_s_fm4[hg], convb_s_fm4[hg]
        ssd_par_s, dskip_s, st_in = ssd_par_s4[hg], dskip_s4[hg], st_in4[hg]
        cur[0] = prompt_mark
        onesb = sb("onesb", [128, 128], BF16)
        cp("dve", onesb, onesf, ["onesf"], ["onesb"])
        conva_s = sb("conva_s", [128, 4, 4])
        convb_s = sb("convb_s", [128, 6, 4])
        ssdp_s = sb("ssdp_s", [8, 2])
        aneg_s = sb("aneg_s", [8, 1])
        dsk_s = sb("dsk_s", [128, 256])
        dsk_s2 = sb("dsk_s2", [128, 256])
        dma("sp", conva_s, conva_s_fm, [], ["conva_s"])
        dma("sp", convb_s, convb_s_fm, [], ["convb_s"])
        dma("sp", ssdp_s, ssd_par_s, [], ["ssdp_s"])
        dma("sp", dsk_s, dskip_s[0].partition_broadcast(128), [], ["dsk_s"])
        dma("sp", dsk_s2, dskip_s[1].partition_broadcast(128), [], ["dsk_s2"])
        tt("dve", dsk_s, dsk_s, dsk_s2, ALU.add, ["dsk_s", "dsk_s2"], ["dsk_s"])
        act(aneg_s, ssdp_s[:, 1:2], AF.Exp, ["ssdp_s"], ["aneg_s"])
        ts("dve", aneg_s, aneg_s, -1.0, None, ALU.mult, None, ["aneg_s"], ["aneg_s"])
        uvb_s = sb("uvb_s", [128, 2, L_S], BF16)
        hy_mark = cur[0]
        xbcT_s = sb("xbcT_s", [128, 4, L_S], BF16)
        ph_mark = cur[0]
        w_s = sb("w_s", [128, 8, 1800], BF16)
        dma("pool", w_s, w_in_es.rearrange("(kc p) n -> p kc n", p=128), [], ["w_s"])
        xin_s = sb("xin_s", [128, 4, D])
        xT_w2 = [sb(f"xT_w{i}", [128, 8, 512]) for i in range(2)]
        sqb2 = [sb(f"sqb{i}", [128, 8, 512], BF16) for i in range(2)]
        hT_w2 = [sb(f"hT_w{i}", [128, 8, 512], BF16) for i in range(2)]
        rstd_w = sb("rstd_w", [128, 512])
        tmpw = [sb(f"tmpw{i}", [128, 512]) for i in range(2)]
        stg = [sb(f"stg{i}", [128, 512], BF16) for i in range(3)]
        x1w = sb("x1w", [128, 2, 512], BF16)
        vw = sb("vw", [128, 2, 512], BF16)
        dtst = sb("dtst", [8, 512])
        nst = [0]

        def conv_w(b, dst, cw, i, silu, key, n0, nv):
            tc_ = tmpw[i % 2]
            tk = ("tmpw", i % 2)
            ts("dve", tc_, ps[b][:, :], cw[:, i, 1:2], cw[:, i, 3:4], ALU.mult, ALU.add, [("ps", b), "conva_s", "convb_s"], [tk])
            stt("dve", tc_[:, 1:512], ps[b][:, 0:511], cw[:, i, 0:1], tc_[:, 1:512], ALU.mult, ALU.add, [("ps", b), tk], [tk])
            stt("dve", tc_[:, 0:511], ps[b][:, 1:512], cw[:, i, 2:3], tc_[:, 0:511], ALU.mult, ALU.add, [("ps", b), tk], [tk])
            if silu:
                act(dst, tc_[:, 1:1 + nv], AF.Silu, [tk], [key])
            else:
                cp("act", dst, tc_[:, 1:1 + nv], [tk], [key])

        def prep_w(k):
            xT_w, sqb, hT_w = xT_w2[k % 2], sqb2[k % 2], hT_w2[k % 2]
            kx, ks, kh = ("xT_w", k % 2), ("sqb", k % 2), ("hT_w", k % 2)
            t0 = 510 * k - 1
            n0 = 510 * k
            if hg > 0:
                dma("sp", hT_w, hTw_d[k].rearrange("a p t -> p a t"), [("hTw_d", k)], [kh])
                return
            for tt_ in range(4):
                lo = max(t0 + 128 * tt_, 0)
                hi = min(t0 + 128 * tt_ + 128, L_S)
                if hi - lo < 128:
                    memset(xin_s[:, tt_, :], 0.0, ["xin_s"])
                if hi > lo:
                    p0 = lo - (t0 + 128 * tt_)
                    dma("sp", xin_s[p0:p0 + (hi - lo), tt_, :], xs_in[lo:hi, :], [], ["xin_s"])
            for ft in range(8):
                b = bank()
                for tt_ in range(4):
                    tr(ps[b][:, tt_ * 128:(tt_ + 1) * 128], xin_s[:, tt_, ft * 128:(ft + 1) * 128], identf, ["xin_s", "identf"], [("ps", b)])
                cp("act", xT_w[:, ft, :], ps[b][:, :], [("ps", b)], [kx])
            act(sqb, xT_w, AF.Square, [kx], [ks])
            b = bank()
            for ft in range(8):
                mm(ps[b][:, :], onesb, sqb[:, ft, :], ft == 0, ft == 7, ["onesb", ks], [("ps", b)])
            ts("dve", rstd_w, ps[b][:, :], 1.0 / D, EPS, ALU.mult, ALU.add, [("ps", b)], ["rstd_w"])
            act(rstd_w, rstd_w, AF.Sqrt, ["rstd_w"], ["rstd_w"])
            P.op("dve", lambda e: e.reciprocal(out=rstd_w, in_=rstd_w), ["rstd_w"], ["rstd_w"])
            for ft in range(8):
                stt("dve", xT_w[:, ft, :], xT_w[:, ft, :], modA[:, 0, ft, 1:2], rstd_w, ALU.mult, ALU.mult,
                    [kx, ("modA", 0), "rstd_w"], [kx])
                act(hT_w[:, ft, :], xT_w[:, ft, :], AF.Identity, [kx, ("mod", 0)], [kh], bias=mod[:, 0, ft, 1:2], scale=1.0)
            if k == 0:
                memset(hT_w[:, :, 0:1], 0.0, [kh], eng="dve")
            if n0 + 511 > L_S:
                memset(hT_w[:, :, L_S - t0:512], 0.0, [kh], eng="dve")
            dma("sp", hTw_d[k].rearrange("a p t -> p a t"), hT_w, [kh], [("hTw_d", k)])

        def inproj_w(k):
            hT_w = hT_w2[k % 2]
            kh = ("hT_w", k % 2)
            n0 = 510 * k
            nv = min(510, L_S - n0)
            for i in range(14):
                b = bank()
                for kc in range(8):
                    mm(ps[b][:, :], w_s[:, kc, i * 128:(i + 1) * 128], hT_w[:, kc, :], kc == 0, kc == 7, ["w_s", kh], [("ps", b)])
                if i in (0, 1, 12, 13):
                    j = nst[0] % 3
                    nst[0] += 1
                    act(stg[j], ps[b][:, :], AF.Silu, [("ps", b)], [("stg", j)])
                    dst = (zs_d if i < 2 else gs_d)[i % 2 if i < 2 else i - 12, :, n0:n0 + nv]
                    dma("sp", dst, stg[j][:, 1:1 + nv], [("stg", j)], ["zs_d" if i < 2 else "gs_d"])
                elif i in (2, 3, 4, 5):
                    conv_w(b, xbcT_s[:, i - 2, n0:n0 + nv], conva_s, i - 2, True, "xbcT_s", n0, nv)
                elif i in (6, 7):
                    j = nst[0] % 3
                    nst[0] += 1
                    conv_w(b, stg[j][:, 1:1 + nv], convb_s, i - 6, False, ("stg", j), n0, nv)
                    dma("sp", x0_d[i - 6, :, n0:n0 + nv], stg[j][:, 1:1 + nv], [("stg", j)], ["x0_d"])
                elif i in (8, 9):
                    conv_w(b, x1w[:, i - 8, 0:nv], convb_s, i - 6, False, "x1w", n0, nv)
                else:
                    conv_w(b, vw[:, i - 10, 0:nv], convb_s, i - 6, False, "vw", n0, nv)
            tt("dve", uvb_s[:, :, n0:n0 + nv], vw[:, :, 0:nv], x1w[:, :, 0:nv], ALU.mult, ["vw", "x1w"], ["uvb_s"])
            b = bank()
            for kc in range(8):
                mm(ps[b][0:8, :], w_s[:, kc, 1792:1800], hT_w[:, kc, :], kc == 0, kc == 7, ["w_s", kh], [("ps", b)])
            cp("act", dtst, ps[b][0:8, :], [("ps", b)], ["dtst"])
            dma("sp", dt_d[:, n0:n0 + nv], dtst[:, 1:1 + nv], ["dtst"], ["dt_d"])
        prep_w(0)
        for k in range(NWIN):
            if k + 1 < NWIN:
                prep_w(k + 1)
            inproj_w(k)
        P.barrier()
        cur[0] = ph_mark

        NCH = L_S // 128
        x_tok_s = sb("x_tok_s", [128, NCH, 256], BF16)
        B_tok_s = sb("B_tok_s", [128, NCH, 128], BF16)
        GT_s = sb("GT_s", [128, NCH, 128])
        dt_tok_s = sb("dt_tok_s", [128, NCH, 8])
        adt_tok_s = sb("adt_tok_s", [128, NCH, 8])
        cs_tok_s = sb("cs_tok_s", [128, NCH, 8])
        ncs_tok_s = sb("ncs_tok_s", [128, NCH, 8])
        dec_tok_s = sb("dec_tok_s", [128, NCH, 8])
        ecs_tok_s = sb("ecs_tok_s", [128, NCH, 8])
        etot_s = sb("etot_s", [128, NCH, 8])
        dtdec_tok_s = sb("dtdec_tok_s", [128, NCH, 8])
        y_tok_s = sb("y_tok_s", [128, NCH, 256])
        dtp = sb("dtp", [8, 1024])
        adtp = sb("adtp", [8, 1024])
        csT_r = [sb(f"csT_r{i}", [8, 128]) for i in range(2)]
        xdt_r = [sb(f"xdt_r{i}", [128, 256], BF16) for i in range(2)]
        xdd_r = [sb(f"xdd_r{i}", [128, 256], BF16) for i in range(2)]
        t1s = sb("t1s", [128, 4, 128])
        t2s = sb("t2s", [128, 4, 128])
        MTs = [sb(f"MTs{i}", [128, 4, 128], BF16) for i in range(2)]
        ytmp_s = sb("ytmp_s", [128, 256])
        STs = [sb(f"STs{d}", [128, 256]) for d in range(2)]
        STbs = [sb(f"STbs{d}", [128, 256], BF16) for d in range(2)]
        stld = sb("stld", [128, 2, 128])
        for q4 in range(4):
            dma("sp", dtp, dt_d[:, q4 * 1024:(q4 + 1) * 1024], ["dt_d"], ["dtp"])
            act(dtp, dtp, AF.Exp, ["dtp", "ssdp_s"], ["dtp"], bias=ssdp_s[:, 0:1], scale=1.0)
            act(dtp, dtp, AF.Ln, ["dtp"], ["dtp"], bias=1.0, scale=1.0)
            ts("dve", adtp, dtp, aneg_s[:, 0:1], None, ALU.mult, None, ["dtp", "aneg_s"], ["adtp"])
            b = bank()
            for c8 in range(8):
                tr(ps[b][:, c8 * 8:(c8 + 1) * 8], dtp[:, c8 * 128:(c8 + 1) * 128], identf[0:8, 0:8], ["dtp", "identf"], [("ps", b)])
                tr(ps[b][:, 64 + c8 * 8:64 + (c8 + 1) * 8], adtp[:, c8 * 128:(c8 + 1) * 128], identf[0:8, 0:8], ["adtp", "identf"], [("ps", b)])
            cp("dve", dt_tok_s[:, q4 * 8:(q4 + 1) * 8, :], ps[b][:, 0:64].rearrange("p (c j) -> p c j", j=8), [("ps", b)], ["dt_tok_s"])
            cp("dve", adt_tok_s[:, q4 * 8:(q4 + 1) * 8, :], ps[b][:, 64:128].rearrange("p (c j) -> p c j", j=8), [("ps", b)], ["adt_tok_s"])
        adt_v = adt_tok_s.rearrange("p c (d j) -> p c d j", d=2)
        b = bank()
        b2 = bank()
        b3 = bank()
        mm(ps[b][:, 0:NCH * 4], triLE, adt_v[:, :, 0, :], True, True, ["triLE", "adt_tok_s"], [("ps", b)])
        mm(ps[b2][:, 0:NCH * 4], triGE, adt_v[:, :, 1, :], True, True, ["triGE", "adt_tok_s"], [("ps", b2)])
        mm(ps[b3][:, 0:NCH * 8], onesf, adt_tok_s, True, True, ["onesf", "adt_tok_s"], [("ps", b3)])
        cs_v = cs_tok_s.rearrange("p c (d j) -> p c d j", d=2)
        cp("dve", cs_v[:, :, 0, :], ps[b][:, 0:NCH * 4].rearrange("p (c j) -> p c j", j=4), [("ps", b)], ["cs_tok_s"])
        cp("dve", cs_v[:, :, 1, :], ps[b2][:, 0:NCH * 4].rearrange("p (c j) -> p c j", j=4), [("ps", b2)], ["cs_tok_s"])
        ts("dve", ncs_tok_s, cs_tok_s, -1.0, None, ALU.mult, None, ["cs_tok_s"], ["ncs_tok_s"])
        act(etot_s, ps[b3][:, 0:NCH * 8].rearrange("p (c j) -> p c j", j=8), AF.Exp, [("ps", b3)], ["etot_s"])
        tt("dve", dec_tok_s, ps[b3][:, 0:NCH * 8].rearrange("p (c j) -> p c j", j=8), cs_tok_s, ALU.subtract, [("ps", b3), "cs_tok_s"], ["dec_tok_s"])
        act(dec_tok_s, dec_tok_s, AF.Exp, ["dec_tok_s"], ["dec_tok_s"])
        act(ecs_tok_s, cs_tok_s, AF.Exp, ["cs_tok_s"], ["ecs_tok_s"])
        tt("dve", dtdec_tok_s, dt_tok_s, dec_tok_s, ALU.mult, ["dt_tok_s", "dec_tok_s"], ["dtdec_tok_s"])
        for c in range(NCH):
            b = bank()
            pb = psb(b)
            for j in range(3):
                tr(pb[:, j * 128:(j + 1) * 128], xbcT_s[:, j, c * 128:(c + 1) * 128], identb, ["xbcT_s", "identb"], [("ps", b)])
            cp("act", x_tok_s[:, c, :], pb[:, 0:256], [("ps", b)], ["x_tok_s"])
            cp("dve", B_tok_s[:, c, :], pb[:, 256:384], [("ps", b)], ["B_tok_s"])
            b = bank()
            mm(ps[b][:, 0:128], xbcT_s[:, 2, c * 128:(c + 1) * 128], xbcT_s[:, 3, c * 128:(c + 1) * 128], True, True, ["xbcT_s"], [("ps", b)])
            cp("act", GT_s[:, c, :], ps[b][:, 0:128], [("ps", b)], ["GT_s"])
        tt("dve", y_tok_s, x_tok_s, dsk_s.unsqueeze(1).to_broadcast([128, NCH, 256]), ALU.mult, ["x_tok_s", "dsk_s"], ["y_tok_s"])
        def load_state(d):
            dma("sp", stld, st_in[d].rearrange("(a p) n -> p a n", p=128), [], ["stld"])
            b = bank()
            for a2 in range(2):
                tr(ps[b][:, a2 * 128:(a2 + 1) * 128], stld[:, a2, :], identf, ["stld", "identf"], [("ps", b)])
            cp("dve", STs[d], ps[b][:, 0:256], [("ps", b)], [("STs", d)])
            cp("act", STbs[d], ps[b][:, 0:256], [("ps", b)], [("STbs", d)])

        def prep(d, c, k):
            mask = maskF if d == 0 else maskB
            b = bank()
            tr(ps[b][0:8, 0:128], cs_tok_s[:, c, :], identf, ["cs_tok_s", "identf"], [("ps", b)])
            cp("act", csT_r[k], ps[b][0:8, 0:128], [("ps", b)], [("csT_r", k)])
            tt("dve", xdt_r[k].rearrange("p (h q) -> p h q", q=64), x_tok_s[:, c, :].rearrange("p (h q) -> p h q", q=64),
               dt_tok_s[:, c, d * 4:(d + 1) * 4].unsqueeze(2).to_broadcast([128, 4, 64]), ALU.mult, ["x_tok_s", "dt_tok_s"], [("xdt_r", k)])
            tt("dve", xdd_r[k].rearrange("p (h q) -> p h q", q=64), x_tok_s[:, c, :].rearrange("p (h q) -> p h q", q=64),
               dtdec_tok_s[:, c, d * 4:(d + 1) * 4].unsqueeze(2).to_broadcast([128, 4, 64]), ALU.mult, ["x_tok_s", "dtdec_tok_s"], [("xdd_r", k)])
            b = bank()
            for j in range(4):
                mm(ps[b][:, j * 128:(j + 1) * 128], sel[0:8, d * 4 + j, :], csT_r[k], True, True, ["sel", ("csT_r", k)], [("ps", b)])
            tt("dve", t1s, ps[b][:, :].rearrange("p (j l) -> p j l", l=128), mask.unsqueeze(1).to_broadcast([128, 4, 128]),
               ALU.add, [("ps", b), "maskF", "maskB"], ["t1s"])
            for j in range(4):
                act(t2s[:, j, :], t1s[:, j, :], AF.Exp, ["t1s", "ncs_tok_s"], ["t2s"],
                    bias=ncs_tok_s[:, c, d * 4 + j:d * 4 + j + 1], scale=1.0)
            tt("dve", MTs[k], t2s, GT_s[:, c, :].unsqueeze(1).to_broadcast([128, 4, 128]), ALU.mult, ["t2s", "GT_s"], [("MTs", k)])

        def consume(d, c, k):
            by = bank()
            for j in range(4):
                mm(ps[by][:, j * 64:(j + 1) * 64], MTs[k][:, j, :], xdt_r[k][:, j * 64:(j + 1) * 64], True, True,
                   [("MTs", k), ("xdt_r", k)], [("ps", by)])
            mm(ps[by][:, 256:512], xbcT_s[:, 3, c * 128:(c + 1) * 128], STbs[d], True, True, ["xbcT_s", ("STbs", d)], [("ps", by)])
            tt("dve", ytmp_s.rearrange("p (h q) -> p h q", q=64), ps[by][:, 256:512].rearrange("p (h q) -> p h q", q=64),
               ecs_tok_s[:, c, d * 4:(d + 1) * 4].unsqueeze(2).to_broadcast([128, 4, 64]), ALU.mult, [("ps", by), "ecs_tok_s"], ["ytmp_s"])
            tt("dve", ytmp_s, ytmp_s, ps[by][:, 0:256], ALU.add, ["ytmp_s", ("ps", by)], ["ytmp_s"])
            tt("dve", y_tok_s[:, c, :], y_tok_s[:, c, :], ytmp_s, ALU.add, ["y_tok_s", "ytmp_s"], ["y_tok_s"])
            b = bank()
            mm(ps[b][:, 0:256], B_tok_s[:, c, :], xdd_r[k], True, True, ["B_tok_s", ("xdd_r", k)], [("ps", b)])
            tt("dve", STs[d].rearrange("p (h q) -> p h q", q=64), STs[d].rearrange("p (h q) -> p h q", q=64),
               etot_s[:, c, d * 4:(d + 1) * 4].unsqueeze(2).to_broadcast([128, 4, 64]), ALU.mult, [("STs", d), "etot_s"], [("STs", d)])
            tt("dve", STs[d], STs[d], ps[b][:, 0:256], ALU.add, [("STs", d), ("ps", b)], [("STs", d)])
            cp("act", STbs[d], STs[d], [("STs", d)], [("STbs", d)])

        iters = [(0, c) for c in range(NCH)] + [(1, c) for c in range(NCH - 1, -1, -1)]
        load_state(0)
        load_state(1)
        prep(iters[0][0], iters[0][1], 0)
        for i_, (d, c) in enumerate(iters):
            if i_ + 1 < len(iters):
                prep(iters[i_ + 1][0], iters[i_ + 1][1], (i_ + 1) % 2)
            consume(d, c, i_ % 2)
        zs_s = sb("zs_s", [128, 2, 512], BF16)
        ygs = sb("ygs", [128, 2, 512], BF16)
        ydbg = sb("ydbg", [128, 2, 512])
        for blk in range(8):
            dma("sp", zs_s, zs_d[:, :, blk * 512:(blk + 1) * 512].rearrange("a p t -> p a t"), ["zs_d"], ["zs_s"])
            for ft in range(2):
                b = bank()
                for c4 in range(4):
                    c = blk * 4 + c4
                    tr(ps[b][:, c4 * 128:(c4 + 1) * 128], y_tok_s[:, c, ft * 128:(ft + 1) * 128], identf, ["y_tok_s", "identf"], [("ps", b)])
                tt("dve", ygs[:, ft, :], ps[b][:, :], zs_s[:, ft, :], ALU.mult, [("ps", b), "zs_s"], ["ygs"])
                if STAGE_S == 1:
                    tt("dve", ydbg[:, ft, :], ps[b][:, :], zs_s[:, ft, :], ALU.mult, [("ps", b), "zs_s"], ["ydbg"])
            dma("sp", y_all[256 * hg:256 * hg + 256, blk * 512:(blk + 1) * 512].rearrange("(a p) t -> p a t", p=128), ygs, ["ygs"], ["y_all"])
            if STAGE_S == 1:
                out_toks.append(dma("sp", o_dbg[:, blk * 512:(blk + 1) * 512].rearrange("(a p) t -> p a t", p=128), ydbg, ["ydbg"], [("o_dbg", blk)]))
        P.barrier()
        return uvb_s, hy_mark


    def sample_hyena(uvb_s, ph_mark, hg):
        hfw3_s, hyp_s_fm = hfw3_s4[hg], hyp_s_fm4[hg]
        cur[0] = ph_mark
        n2 = 2 * L_S
        w1t = sb("w1t", [64, 128], BF16)
        w1i = sb("w1i", [128, 32], BF16)
        wtab = sb("wtab", [128, 3, 128], BF16)
        tw = sb("tw", [128, 2, 64])
        dma("pool", w1t, w1tab, [], ["w1t"])
        dma("pool", w1i, w1inv, [], ["w1i"])
        dma("pool", wtab, w128.rearrange("t p f -> p t f"), [], ["wtab"])
        dma("sp", tw, twid.rearrange("t p f -> p t f"), [], ["tw"])
        w1s_ = sb("w1s_", [33, 64])
        w2s_ = sb("w2s_", [64, 64])
        w3s_ = sb("w3s_", [64, 512])
        hp2 = sb("hp2", [64, 3])
        hsc2 = sb("hsc2", [64, 4])
        hyp = sb("hyp", [128, 2, 2])
        dma("sp", w1s_, hfw1, [], ["w1s_"])
        dma("sp", w2s_, hfw2, [], ["w2s_"])
        dma("sp", w3s_, hfw3_s, [], ["w3s_"])
        dma("sp", hp2, hfpar, [], ["hp2"])
        dma("sp", hyp, hyp_s_fm, [], ["hyp"])
        ts("dve", hsc2[:, 0:1], hp2[:, 2:3], 1.0 / TWO_PI, None, ALU.mult, None, ["hp2"], ["hsc2"])
        for j in range(2):
            tt("dve", hsc2[:, 1 + j:2 + j], hp2[:, j:j + 1], hsc2[:, 0:1], ALU.mult, ["hp2", "hsc2"], ["hsc2"])
        MAGIC = 12582912.0
        feb = sb("feb", [33, 512])
        tnb2 = sb("tnb2", [128, 512])
        hA = sb("hA", [64, 512])
        hB = sb("hB", [64, 512])
        kk2 = sb("kk2", [64, 512])
        dec2 = sb("dec2", [128, 512])
        kT_f = sb("kT_f", [128, n2], BF16)
        tnb2x = [tnb2, sb("tnb2b", [128, 512])]
        hBx = [hB, sb("hBb", [64, 512])]
        dec2x = [dec2, sb("dec2b", [128, 512])]
        lay = sb("lay", [128, 128 * 64], BF16)
        Gb = sb("Gb", [128, 64 * 128], BF16)
        Gp = sb("Gp", [128, 2, 4096], BF16)
        tq = [sb(f"tq{i}", [128, 1024]) for i in range(2)]
        Ksp = sb("Ksp", [128, 2, 4096], BF16)
        Xe = sb("Xe", [128, 2, 512])
        Ysp = sb("Ysp", [128, 2, 4096], BF16)
        y1 = Ksp.rearrange("p t n -> p (t n)")[0:32, :]
        ycv = sb("ycv", [64, L_S], BF16)
        x0h = sb("x0h", [64, L_S // 2], BF16)
        gsh = sb("gsh", [64, L_S // 2], BF16)
        ybh = sb("ybh", [64, L_S // 2], BF16)

        def sin_l(dst, src_w, src_x, j, keys):
            b = bank()
            mm(ps[b][0:64, :], src_w, src_x, True, True, keys, [("ps", b)])
            ts("dve", dst, ps[b][0:64, :], hsc2[:, 0:1], hsc2[:, 1 + j:2 + j], ALU.mult, ALU.add, [("ps", b), "hsc2"], ["hl2"])
            ts("dve", kk2, dst, MAGIC, None, ALU.add, None, ["hl2"], ["kk2"])
            ts("dve", kk2, kk2, -MAGIC, None, ALU.add, None, ["kk2"], ["kk2"])
            tt("dve", dst, dst, kk2, ALU.subtract, ["hl2", "kk2"], ["hl2"])
            ts("dve", dst, dst, TWO_PI, None, ALU.mult, None, ["hl2"], ["hl2"])
            ts("dve", dst, dst, math.pi, -math.pi, ALU.min, ALU.max, ["hl2"], ["hl2"])
            act(dst, dst, AF.Sin, ["hl2"], ["hA", "hB"])

        def fwd_half(srcT, hoff, NB, srckey):
            layv = lay[0:NB, :].rearrange("p (a c) -> p a c", c=64)
            for a0 in range(0, 128, 16):
                b = bank()
                pb = psb(b)
                for j in range(16):
                    a = a0 + j
                    tr(pb[0:NB, j * 64:(j + 1) * 64], srcT[hoff:hoff + 64, a:NB * 128:128], identb[hoff:hoff + 64, hoff:hoff + 64],
                       [srckey, "identb"], [("ps", b)])
                cp("act" if (a0 // 16) % 2 else "dve", layv[:, a0:a0 + 16, :], pb[0:NB, 0:1024].rearrange("p (a c) -> p a c", c=64),
                   [("ps", b)], ["lay"])
            Gv = Gb.rearrange("p (c f) -> p c f", f=128)
            for c0 in range(0, 64, 4):
                b = bank()
                for j in range(4):
                    mm(ps[b][:, j * 128:(j + 1) * 128], layv[:, :, c0 + j], w1t[0:NB, :], True, True, ["lay", "w1t"], [("ps", b)])
                cp("act" if (c0 // 4) % 2 else "dve", Gv[:, c0:c0 + 4, :], ps[b][:, :].rearrange("p (c f) -> p c f", f=128), [("ps", b)], ["Gb"])
            for hf in range(4):
                gre = Gv[:, hf * 16:(hf + 1) * 16, 0:64]
                gim = Gv[:, hf * 16:(hf + 1) * 16, 64:128]
                tre = tw[:, 0, :].unsqueeze(1).to_broadcast([128, 16, 64])
                tim = tw[:, 1, :].unsqueeze(1).to_broadcast([128, 16, 64])
                q0 = tq[0].rearrange("p (c f) -> p c f", f=64)
                q1 = tq[1].rearrange("p (c f) -> p c f", f=64)
                ore = Gp[:, 0, hf * 1024:(hf + 1) * 1024].rearrange("p (c f) -> p c f", f=64)
                oim = Gp[:, 1, hf * 1024:(hf + 1) * 1024].rearrange("p (c f) -> p c f", f=64)
                tt("dve", q0, gre, tre, ALU.mult, ["Gb", "tw"], [("tq", 0)])
                tt("dve", q1, gim, tim, ALU.mult, ["Gb", "tw"], [("tq", 1)])
                tt("dve", ore, q0, q1, ALU.subtract, [("tq", 0), ("tq", 1)], ["Gp"])
                tt("dve", q1, gre, tim, ALU.mult, ["Gb", "tw"], [("tq", 1)])
                tt("dve", q0, gim, tre, ALU.mult, ["Gb", "tw"], [("tq", 0)])
                tt("dve", oim, q0, q1, ALU.add, [("tq", 0), ("tq", 1)], ["Gp"])

        def stageB(ch, src, srckey, inverse):
            bre, bim = bank(), bank()
            sl = slice(ch * 512, (ch + 1) * 512)
            s_a, s_b = (2, 1) if inverse else (1, 2)
            mm(ps[bre][:, :], wtab[:, 0, :], src[:, 0, sl], True, False, ["wtab", srckey], [("ps", bre)])
            mm(ps[bre][:, :], wtab[:, s_a, :], src[:, 1, sl], False, True, ["wtab", srckey], [("ps", bre)])
            mm(ps[bim][:, :], wtab[:, 0, :], src[:, 1, sl], True, False, ["wtab", srckey], [("ps", bim)])
            mm(ps[bim][:, :], wtab[:, s_b, :], src[:, 0, sl], False, True, ["wtab", srckey], [("ps", bim)])
            return bre, bim

        for ct in range(2):
            for blk in range(n2 // 512):
                c0 = blk * 512
                kq = blk % 2
                tnb_, hB_, dec_ = tnb2x[kq], hBx[kq], dec2x[kq]
                dma("sp", tnb_, tnS[c0:c0 + 512].partition_broadcast(128), [], [("tnb2", kq)])
                if hg == 0 and ct == 0:
                    dma("sp", feb, featsS[:, c0:c0 + 512], [], ["feb"])
                    sin_l(hA, w1s_, feb, 0, ["w1s_", "feb"])
                    sin_l(hB_, w2s_, hA, 1, ["w2s_", "hA"])
                    dma("sp", h2_d[:, c0:c0 + 512], hB_, ["hA", "hB"], [("h2_d", blk)])
                else:
                    dma("sp", hB_, h2_d[:, c0:c0 + 512], [("h2_d", blk)], [("hBx", kq)])
                b = bank()
                wcol = (0 if c0 < L_S else 256) + ct * 128
                mm(ps[b][:, :], w3s_[:, wcol:wcol + 128], hB_, True, True, ["w3s_", "hB", ("hBx", kq)], [("ps", b)])
                act(dec_, tnb_, AF.Exp, [("tnb2", kq), "hyp"], [("dec2", kq)], scale=hyp[:, ct, 0:1])
                tt("dve", kT_f[:, c0:c0 + 512], ps[b][:, :], dec_, ALU.mult, [("ps", b), ("dec2", kq)], ["kT_f"])
            memset(kT_f[:, L_S:L_S + 1], 0.0, ["kT_f"], eng="dve")
            ts("dve", kT_f[:, 0:1], kT_f[:, 0:1], hyp[:, ct, 1:2], None, ALU.add, None, ["kT_f", "hyp"], ["kT_f"])
            for half in range(2):
                hoff = half * 64
                fwd_half(kT_f, hoff, 64, "kT_f")
                for ch in range(8):
                    bre, bim = stageB(ch, Gp, "Gp", False)
                    cp("act", Ksp[:, 0, ch * 512:(ch + 1) * 512], ps[bre][:, :], [("ps", bre)], ["Ksp"])
                    cp("dve", Ksp[:, 1, ch * 512:(ch + 1) * 512], ps[bim][:, :], [("ps", bim)], ["Ksp"])
                fwd_half(uvb_s[:, ct, :], hoff, 32, "uvb_s")
                for ch in range(8):
                    bre, bim = stageB(ch, Gp, "Gp", False)
                    sl = slice(ch * 512, (ch + 1) * 512)
                    cp("act", Xe[:, 0, :], ps[bre][:, :], [("ps", bre)], [("Xe", 0)])
                    cp("act", Xe[:, 1, :], ps[bim][:, :], [("ps", bim)], [("Xe", 1)])
                    qa = tq[0][:, 0:512]
                    qb_ = tq[1][:, 0:512]
                    tt("dve", qa, Xe[:, 0, :], Ksp[:, 0, sl], ALU.mult, [("Xe", 0), "Ksp"], [("tq", 0)])
                    tt("dve", qb_, Xe[:, 1, :], Ksp[:, 1, sl], ALU.mult, [("Xe", 1), "Ksp"], [("tq", 1)])
                    tt("dve", Ysp[:, 0, sl], qa, qb_, ALU.subtract, [("tq", 0), ("tq", 1)], ["Ysp"])
                    tt("dve", qb_, Xe[:, 0, :], Ksp[:, 1, sl], ALU.mult, [("Xe", 0), "Ksp"], [("tq", 1)])
                    tt("dve", qa, Xe[:, 1, :], Ksp[:, 0, sl], ALU.mult, [("Xe", 1), "Ksp"], [("tq", 0)])
                    tt("dve", Ysp[:, 1, sl], qa, qb_, ALU.add, [("tq", 0), ("tq", 1)], ["Ysp"])
                Vv = Gb.rearrange("p (c f) -> p c f", f=128)
                for ch in range(8):
                    bre, bim = stageB(ch, Ysp, "Ysp", True)
                    csl = slice(ch * 8, (ch + 1) * 8)
                    cp("act", Xe[:, 0, :], ps[bre][:, :], [("ps", bre)], [("Xe", 0)])
                    cp("act", Xe[:, 1, :], ps[bim][:, :], [("ps", bim)], [("Xe", 1)])
                    vre = Xe[:, 0, :].rearrange("p (c f) -> p c f", f=64)
                    vim = Xe[:, 1, :].rearrange("p (c f) -> p c f", f=64)
                    tre = tw[:, 0, :].unsqueeze(1).to_broadcast([128, 8, 64])
                    tim = tw[:, 1, :].unsqueeze(1).to_broadcast([128, 8, 64])
                    qa = tq[0][:, 0:512].rearrange("p (c f) -> p c f", f=64)
                    qb_ = tq[1][:, 0:512].rearrange("p (c f) -> p c f", f=64)
                    tt("dve", qa, vre, tre, ALU.mult, [("Xe", 0), "tw"], [("tq", 0)])
                    tt("dve", qb_, vim, tim, ALU.mult, [("Xe", 1), "tw"], [("tq", 1)])
                    tt("dve", Vv[:, csl, 0:64], qa, qb_, ALU.add, [("tq", 0), ("tq", 1)], ["Gb"])
                    tt("dve", qb_, vre, tim, ALU.mult, [("Xe", 0), "tw"], [("tq", 1)])
                    tt("dve", qa, vim, tre, ALU.mult, [("Xe", 1), "tw"], [("tq", 0)])
                    tt("dve", Vv[:, csl, 64:128], qa, qb_, ALU.subtract, [("tq", 0), ("tq", 1)], ["Gb"])
                Tl = lay.rearrange("p (c a) -> p c a", a=128)
                for c0 in range(0, 64, 8):
                    b = bank()
                    pb = psb(b)
                    for j in range(8):
                        tr(pb[:, j * 128:(j + 1) * 128], Vv[:, c0 + j, :], identb, ["Gb", "identb"], [("ps", b)])
                    cp("act" if (c0 // 8) % 2 else "dve", Tl[:, c0:c0 + 8, :], pb[:, 0:1024].rearrange("p (c a) -> p c a", a=128),
                       [("ps", b)], ["lay"])
                ycv3 = ycv.rearrange("p (i a) -> p i a", a=128)
                for a0 in range(0, 128, 16):
                    b = bank()
                    for j in range(16):
                        mm(ps[b][0:64, j * 32:(j + 1) * 32], Tl[:, :, a0 + j], w1i, True, True, ["lay", "w1i"], [("ps", b)])
                    cp("act", ycv3[:, :, a0:a0 + 16], ps[b][0:64, 0:512].rearrange("p (a i) -> p i a", i=32), [("ps", b)], ["ycv"])
                r0 = ct * 128 + hoff
                for tq_ in range(2):
                    tsl = slice(tq_ * 2048, (tq_ + 1) * 2048)
                    dma("sp", x0h, x0_d[ct, hoff:hoff + 64, tsl], ["x0_d"], ["x0h"])
                    dma("sp", gsh, gs_d[ct, hoff:hoff + 64, tsl], ["gs_d"], ["gsh"])
                    tt("dve", ybh, ycv[:, tsl], x0h, ALU.mult, ["ycv", "x0h"], ["ybh"])
                    tt("dve", ybh, ybh, gsh, ALU.mult, ["ybh", "gsh"], ["ybh"])
                    dma("sp", y_all[D + 256 * hg + r0:D + 256 * hg + r0 + 64, tsl], ybh, ["ybh"], ["y_all"])
                    if STAGE_S == 2:
                        out_toks.append(dma("sp", o_dbg.bitcast(BF16)[r0:r0 + 64, tsl], ybh, ["ybh"], [("o_dbg", r0, tq_)]))

        P.barrier()


    def sample_tail():
        P.barrier()
        cur[0] = prompt_mark
        NE = 2048
        onesb2 = sb("onesb2", [128, 128], BF16)
        cp("dve", onesb2, onesf, ["onesf"], ["onesb2"])
        bm = sb("bm", [128, 16])
        dma("sp", bm, blkmask.partition_broadcast(128), [], ["bm"])
        hT1x = sb("hT1x", [128, 8, NE], BF16)
        ogT_own = sb("ogT_own", [128, 8, 1024], BF16)
        t1_mark = cur[0]
        wo = sb("wo", [128, 16, D], BF16)
        for half in range(2):
            dma("pool", wo[:, half * 8:(half + 1) * 8, :], w_out_e[half * 1024:(half + 1) * 1024, :].rearrange("(kc p) n -> p kc n", p=128),
                [], [("wo", half)])
        if y_all_dbg is not None:
            stgd = sb("stgd", [128, L_S], BF16)
            for rt in range(16):
                dma("pool", stgd, y_all_dbg[rt * 128:(rt + 1) * 128, :], [], ["stgd"])
                dma("sp", y_all[rt * 128:(rt + 1) * 128, :], stgd, ["stgd"], ["y_all"])
        cand = [sb(f"cand{i}", [128, 4, 512], BF16) for i in range(2)]
        ygx = sb("ygx", [128, 16, 512], BF16)
        xin_t = [sb(f"xin_t{i}", [128, D]) for i in range(2)]
        xT_b = sb("xT_b", [128, 8, 512])
        sq_b = sb("sq_b", [128, 8, 512], BF16)
        rstd_b = sb("rstd_b", [128, 512])
        rstdy_b = sb("rstdy_b", [128, 512])
        tmp_b = [sb(f"tmp_b{i}", [128, 512]) for i in range(2)]
        for kb in range(4):
            piece = 0 if kb == 0 else (2 if kb == 3 else 1)
            off = 512 if kb in (0, 2) else 0
            for rt in range(16):
                cd = cand[rt % 2]
                ck = ("cand", rt % 2)
                dma("sp", cd, y_all[rt * 128:(rt + 1) * 128, :].rearrange("p (j t) -> p j t", t=1024)[:, :, off:off + 512], ["y_all"], [ck])
                for jj in range(4):
                    m = bm[:, piece * 4 + jj:piece * 4 + jj + 1]
                    if jj == 0:
                        ts("dve", ygx[:, rt, :], cd[:, jj, :], m, None, ALU.mult, None, [ck, "bm"], ["ygx"])
                    else:
                        stt("dve", ygx[:, rt, :], cd[:, jj, :], m, ygx[:, rt, :], ALU.mult, ALU.add, [ck, "bm", "ygx"], ["ygx"])
            for ft in range(8):
                pass
            bks = [bank() for _ in range(8)]
            for tt_ in range(4):
                xt = xin_t[tt_ % 2]
                xk = ("xin_t", tt_ % 2)
                dma("sp", xt, x_ext[kb * 512 + tt_ * 128:kb * 512 + (tt_ + 1) * 128, :], [], [xk])
                for ft in range(8):
                    tr(ps[bks[ft]][:, tt_ * 128:(tt_ + 1) * 128], xt[:, ft * 128:(ft + 1) * 128], identf, [xk, "identf"], [("ps", bks[ft])])
            for ft in range(8):
                cp("act", xT_b[:, ft, :], ps[bks[ft]][:, :], [("ps", bks[ft])], ["xT_b"])
            act(sq_b, ygx[:, 0:8, :], AF.Square, ["ygx"], ["sq_b"])
            b = bank()
            for ft in range(8):
                mm(ps[b][:, :], onesb2, sq_b[:, ft, :], ft == 0, ft == 7, ["onesb2", "sq_b"], [("ps", b)])
            ts("dve", rstdy_b, ps[b][:, :], 1.0 / D, EPS, ALU.mult, ALU.add, [("ps", b)], ["rstdy_b"])
            act(rstdy_b, rstdy_b, AF.Sqrt, ["rstdy_b"], ["rstdy_b"])
            P.op("dve", lambda e: e.reciprocal(out=rstdy_b, in_=rstdy_b), ["rstdy_b"], ["rstdy_b"])
            for ft in range(8):
                act(ygx[:, ft, :], ygx[:, ft, :], AF.Copy, ["ygx", "naw"], ["ygx"], scale=naw[:, ft:ft + 1])
            for ot in range(8):
                ba, bb = bank(), bank()
                for kc in range(8):
                    mm(ps[ba][:, :], wo[:, kc, ot * 128:(ot + 1) * 128], ygx[:, kc, :], kc == 0, kc == 7, [("wo", 0), "ygx"], [("ps", ba)])
                for kc in range(8):
                    mm(ps[bb][:, :], wo[:, 8 + kc, ot * 128:(ot + 1) * 128], ygx[:, 8 + kc, :], kc == 0, kc == 7, [("wo", 1), "ygx"], [("ps", bb)])
                tb = tmp_b[ot % 2]
                tk = ("tmp_b", ot % 2)
                tt("dve", tb, ps[ba][:, :], rstdy_b, ALU.mult, [("ps", ba), "rstdy_b"], [tk])
                tt("dve", tb, tb, ps[bb][:, :], ALU.add, [tk, ("ps", bb)], [tk])
                stt("dve", xT_b[:, ot, :], tb, mod[:, 0, 16 + ot, 1:2], xT_b[:, ot, :], ALU.mult, ALU.add, [tk, ("mod", 0), "xT_b"], ["xT_b"])
            dma("sp", x1_d[:, :, kb * 512:(kb + 1) * 512].rearrange("a p t -> p a t"), xT_b, ["xT_b"], ["x1_d"])
            act(sq_b, xT_b, AF.Square, ["xT_b"], ["sq_b"])
            b = bank()
            for ft in range(8):
                mm(ps[b][:, :], onesb2, sq_b[:, ft, :], ft == 0, ft == 7, ["onesb2", "sq_b"], [("ps", b)])
            ts("dve", rstd_b, ps[b][:, :], 1.0 / D, EPS, ALU.mult, ALU.add, [("ps", b)], ["rstd_b"])
            act(rstd_b, rstd_b, AF.Sqrt, ["rstd_b"], ["rstd_b"])
            P.op("dve", lambda e: e.reciprocal(out=rstd_b, in_=rstd_b), ["rstd_b"], ["rstd_b"])
            for ft in range(8):
                stt("dve", xT_b[:, ft, :], xT_b[:, ft, :], modA[:, 1, ft, 1:2], rstd_b, ALU.mult, ALU.mult,
                    ["xT_b", ("modA", 1), "rstd_b"], ["xT_b"])
                act(hT1x[:, ft, kb * 512:(kb + 1) * 512], xT_b[:, ft, :], AF.Identity, ["xT_b", ("mod", 1)], ["hT1x"],
                    bias=mod[:, 1, ft, 1:2], scale=1.0)
        P.barrier()
        cur[0] = t1_mark
        wp = [sb(f"wp{i}", [128, 8, 512], BF16) for i in range(2)]
        qT_p = sb("qT_p", [128, NE], BF16)
        kT_p = sb("kT_p", [128, NE], BF16)
        vT_p = sb("vT_p", [128, NE], BF16)
        gs_p = sb("gs_p", [128, NE], BF16)
        v_tok_p = sb("v_tok_p", [128, 16, 128], BF16)
        ckT_p = sb("ckT_p", [128, 256], BF16)
        cv_p = sb("cv_p", [128, 2, 128], BF16)
        strip = sb("strip", [128, 19 * 64])
        rm = sb("rm", [16, 18 * 64], BF16)
        dma("pool", rm, rm_in, [], ["rm"])
        selb = sb("selb", [16, 16, 128], BF16)
        cp("dve", selb, sel[0:16, 0:16, :], ["sel"], ["selb"])
        qbd = [sb(f"qbd{i}", [128, 128], BF16) for i in range(2)]
        sc = [sb(f"sc{i}", [128, 1408]) for i in range(2)]
        Pn = [sb(f"Pn{i}", [128, 1408], BF16) for i in range(2)]
        PTn = [sb(f"PTn{i}", [128, 11, 128], BF16) for i in range(2)]
        sms = sb("sms", [128, 8])
        for i in range(2):
            memset(qbd[i], 0.0, [("qbd", i)])
        itn = [0]
        for hp2 in range(8):
            w_ = wp[hp2 % 2]
            wk_ = ("wp", hp2 % 2)
            dma("pool", w_, w_in_o_pairs[hp2].rearrange("(kc p) n -> p kc n", p=128), [], [wk_])
            dma("pool", ckT_p, ckT_in[hp2], [], ["ckT_p"])
            dma("pool", cv_p, cv_in[hp2], [], ["cv_p"])
            dma("sp", strip, strip_in[hp2], [], ["strip"])
            for seg, dst_, key_ in ((0, qT_p, "qT_p"), (1, kT_p, "kT_p"), (2, vT_p, "vT_p"), (3, gs_p, "gs_p")):
                for kb in ((1, 2) if seg in (0, 3) else range(4)):
                    b = bank()
                    for kc in range(8):
                        mm(ps[b][:, :], w_[:, kc, seg * 128:(seg + 1) * 128], hT1x[:, kc, kb * 512:(kb + 1) * 512], kc == 0, kc == 7,
                           [wk_, "hT1x"], [("ps", b)])
                    if seg == 3:
                        act(dst_[:, kb * 512:(kb + 1) * 512], ps[b][:, :], AF.Silu, [("ps", b)], [key_])
                    else:
                        cp("act", dst_[:, kb * 512:(kb + 1) * 512], ps[b][:, :], [("ps", b)], [key_])
            for t4 in range(4):
                b = bank()
                pb = psb(b)
                for j in range(4):
                    tl = t4 * 4 + j
                    tr(pb[:, j * 128:(j + 1) * 128], vT_p[:, tl * 128:(tl + 1) * 128], identb, ["vT_p", "identb"], [("ps", b)])
                cp("act", v_tok_p[:, t4 * 4:(t4 + 1) * 4, :].rearrange("p a d -> p (a d)"), pb[:, 0:512], [("ps", b)], ["v_tok_p"])
            def S1(rho, k):
                q0 = 512 + 64 * rho
                nt = 9 if rho % 2 == 0 else 8
                st_ = rho // 2 if rho % 2 == 0 else (rho + 1) // 2
                nk = nt * 128
                cp("dve", qbd[k][0:64, 0:64], qT_p[0:64, q0:q0 + 64], ["qT_p"], [("qbd", k)])
                cp("dve", qbd[k][64:128, 64:128], qT_p[64:128, q0:q0 + 64], ["qT_p"], [("qbd", k)])
                banks = []
                for c0 in range(0, nk, 512):
                    cn = min(512, nk - c0)
                    b = bank()
                    banks.append((b, c0, cn))
                    mm(ps[b][:, 0:cn], qbd[k], kT_p[:, st_ * 128 + c0:st_ * 128 + c0 + cn], True, False, [("qbd", k), "kT_p"], [("ps", b)])
                    mm(ps[b][:, 0:cn], selb[:, rho, :], rm[:, c0:c0 + cn], False, True, ["selb", "rm"], [("ps", b)])
                bc = bank()
                mm(ps[bc][:, 0:256], qbd[k], ckT_p, True, True, [("qbd", k), "ckT_p"], [("ps", bc)])
                return banks, bc

            def S2(rho, k, banks, bc):
                nt = 9 if rho % 2 == 0 else 8
                nk = nt * 128
                srow0 = 0 if rho % 2 == 0 else 1
                for (b, c0, cn) in banks:
                    stt("dve", sc[k][:, c0:c0 + cn], ps[b][:, 0:cn], 0.125, strip[:, srow0 * 64 + c0:srow0 * 64 + c0 + cn], ALU.mult, ALU.add,
                        [("ps", b), "strip"], [("sc", k)])
                act(sc[k][:, nk:nk + 256], ps[bc][:, 0:256], AF.Copy, [("ps", bc)], [("sc", k)], scale=0.125)
                tot = nk + 256
                P.op("dve", lambda e, k=k, tot=tot: e.tensor_reduce(out=sms[:, 0:1], in_=sc[k][:, 0:tot], axis=AX.X, op=ALU.max, negate=True),
                     [("sc", k)], ["sms0"])
                act(Pn[k][:, 0:tot], sc[k][:, 0:tot], AF.Exp, [("sc", k), "sms0"], [("Pn", k), "sms2"], bias=sms[:, 0:1], scale=1.0, accum=sms[:, 2:3])
                P.op("dve", lambda e: e.reciprocal(out=sms[:, 3:4], in_=sms[:, 2:3]), ["sms2"], ["sms3"])
                ts("dve", Pn[k][:, 0:tot], Pn[k][:, 0:tot], sms[:, 3:4], None, ALU.mult, None, [("Pn", k), "sms3"], [("Pn", k)])

            def S3(rho, k):
                q0 = 512 + 64 * rho
                nt = 9 if rho % 2 == 0 else 8
                st_ = rho // 2 if rho % 2 == 0 else (rho + 1) // 2
                ntt = nt + 2
                for g0 in range(0, ntt, 8):
                    gn = min(8, ntt - g0)
                    b = bank()
                    pb = psb(b)
                    for j in range(gn):
                        tr(pb[:, j * 128:(j + 1) * 128], Pn[k][:, (g0 + j) * 128:(g0 + j + 1) * 128], identb, [("Pn", k), "identb"], [("ps", b)])
                    cp("act", PTn[k][:, g0:g0 + gn, :].rearrange("p a q -> p (a q)"), pb[:, 0:gn * 128], [("ps", b)], [("PTn", k)])
                bo = bank()
                for j in range(ntt):
                    lhs = v_tok_p[:, st_ + j, :] if j < nt else cv_p[:, j - nt, :]
                    mm(ps[bo][:, 0:128], lhs, PTn[k][:, j, :], j == 0, j == ntt - 1, ["v_tok_p", "cv_p", ("PTn", k)], [("ps", bo)])
                tt("dve", ogT_own[0:64, hp2, 64 * rho:64 * rho + 64], ps[bo][0:64, 0:64], gs_p[0:64, q0:q0 + 64], ALU.mult,
                   [("ps", bo), "gs_p"], ["ogT_own"])
                tt("dve", ogT_own[64:128, hp2, 64 * rho:64 * rho + 64], ps[bo][64:128, 64:128], gs_p[64:128, q0:q0 + 64], ALU.mult,
                   [("ps", bo), "gs_p"], ["ogT_own"])

            nxt = S1(0, itn[0] % 2)
            for rho in range(16):
                k = itn[0] % 2
                itn[0] += 1
                S2(rho, k, *nxt)
                if rho + 1 < 16:
                    nxt = S1(rho + 1, itn[0] % 2)
                S3(rho, k)
        P.barrier()
        cur[0] = t1_mark
        woo = sb("woo", [128, 8, D], BF16)
        dma("pool", woo, w_out_o.rearrange("(kc p) n -> p kc n", p=128), [], ["woo"])
        x1o = sb("x1o", [128, 8, 512])
        ytk = sb("ytk", [128, 4, D])
        sq3 = sb("sq3", [128, D])
        sms3 = sb("sms3", [128, 8])
        for kb in range(2):
            dma("sp", x1o, x1_d[:, :, 512 + kb * 512:512 + (kb + 1) * 512].rearrange("a p t -> p a t"), ["x1_d"], ["x1o"])
            for ot in range(8):
                b = bank()
                for kc in range(8):
                    mm(ps[b][:, :], woo[:, kc, ot * 128:(ot + 1) * 128], ogT_own[:, kc, kb * 512:(kb + 1) * 512], kc == 0, kc == 7,
                       ["woo", "ogT_own"], [("ps", b)])
                stt("dve", x1o[:, ot, :], ps[b][:, :], mod[:, 1, 16 + ot, 1:2], x1o[:, ot, :], ALU.mult, ALU.add,
                    [("ps", b), ("mod", 1), "x1o"], ["x1o"])
            for c in range(4):
                for half in range(2):
                    b = bank()
                    for j in range(4):
                        ft = half * 4 + j
                        tr(ps[b][:, j * 128:(j + 1) * 128], x1o[:, ft, c * 128:(c + 1) * 128], identf, ["x1o", "identf"], [("ps", b)])
                    cp("act", ytk[:, c, half * 512:(half + 1) * 512], ps[b][:, :], [("ps", b)], ["ytk"])
                act(sq3, ytk[:, c, :], AF.Square, ["ytk"], ["sq3", "sms4"], accum=sms3[:, 4:5])
                ts("dve", sms3[:, 5:6], sms3[:, 4:5], 1.0 / D, EPS, ALU.mult, ALU.add, ["sms4"], ["sms5"])
                act(sms3[:, 5:6], sms3[:, 5:6], AF.Sqrt, ["sms5"], ["sms5"])
                P.op("dve", lambda e: e.reciprocal(out=sms3[:, 6:7], in_=sms3[:, 5:6]), ["sms5"], ["sms6"])
                stt("dve", ytk[:, c, :], ytk[:, c, :], sms3[:, 6:7], fnw_bc, ALU.mult, ALU.mult, ["ytk", "sms6", "dsk2"], ["ytk"])
            out_toks.append(dma("sp", o_ys[kb * 512:(kb + 1) * 512, :].rearrange("(c p) d -> p c d", p=128), ytk, ["ytk"], [("o_ys", kb)]))

    NHG = 4
    if DO_SAMPLE:
        for hg in range(NHG):
            uvb_s_, phm_ = sample_layer0(hg)
            if STAGE_S >= 2:
                sample_hyena(uvb_s_, phm_, hg)
        if STAGE_S >= 3:
            sample_tail()

    P.finish("sp", out_toks)
    P.emit()
    return nc


def kernel(**inp):
    f32 = np.float32
    inp = {k: np.asarray(v) for k, v in inp.items()}
    nc = build_nc()
    sel = np.zeros((32, 32, 128), f32)
    for j in range(32):
        sel[j, j, :] = 1.0
    conv_a = np.concatenate([inp["conv_a_w"][0], inp["conv_a_b"][0][None]], 0)
    conv_a_fm = np.ascontiguousarray(conv_a.reshape(4, 12, 128).transpose(2, 1, 0))
    conv_b = np.concatenate([inp["conv_b_w"][0], inp["conv_b_b"][0][None]], 0)
    conv_b_fm = np.ascontiguousarray(conv_b.reshape(4, 24, 128).transpose(2, 1, 0))
    ssd_par = np.ascontiguousarray(np.stack([inp["dt_bias"][0].reshape(32), inp["a_log"][0].reshape(32)], 1))
    dskip_rep = np.ascontiguousarray(np.repeat(inp["d_skip"][0], 64, axis=1))
    featsP, tnP = hyena_feats(L_P)
    deltas = np.linspace(math.log(1e-2) / 0.3, math.log(1e-2) / 1.5, D, dtype=f32)
    ndelta_fm = fm(-np.abs(deltas), 8)
    n = 2 * L_P
    tt_ = np.arange(n, dtype=np.float64)
    ang = 2.0 * np.pi * np.outer(tt_, tt_) / n
    dftP = np.stack([np.cos(ang), -np.sin(ang), np.cos(ang) / n, -np.sin(ang) / n]).astype(f32)
    hfpar = np.ascontiguousarray(np.stack([inp["hf_b1"][0], inp["hf_b2"][0], inp["hf_freq"][0]], 1))
    featsS, tnS = hyena_feats(L_S)
    ii = np.arange(64, dtype=np.float64)
    w1tab = np.concatenate([np.cos(2 * np.pi * np.outer(ii, ii) / 64), -np.sin(2 * np.pi * np.outer(ii, ii) / 64)], 1).astype(f32)
    i32 = np.arange(32, dtype=np.float64)
    w1inv = np.concatenate([np.cos(2 * np.pi * np.outer(ii, i32) / 64), -np.sin(2 * np.pi * np.outer(ii, i32) / 64)], 0).astype(np.float64)
    w1inv = (w1inv / (2 * L_S)).astype(f32)
    aa = np.arange(128, dtype=np.float64)
    th = 2 * np.pi * np.outer(aa, aa) / 128
    w128 = np.stack([np.cos(th), np.sin(th), -np.sin(th)]).astype(f32)
    tht = 2 * np.pi * np.outer(aa, ii) / (2 * L_S)
    twid = np.stack([np.cos(tht), -np.sin(tht)]).astype(f32)
    WE = inp["w_in_e"][0]
    W3 = inp["hf_w3"][0]

    def hg_slices(g):
        g2 = g // 2
        dtc = [2560 + d_ * 16 + 4 * g + j for d_ in range(2) for j in range(4)]
        r256 = np.arange(256 * g, 256 * g + 256)
        r128 = np.arange(128 * g2, 128 * g2 + 128)
        cols = np.concatenate([r256, 1024 + r256, 2048 + r128, 2304 + r128, 2592 + r256, 3616 + r256, 4640 + r256, 5664 + r256,
                               np.array(dtc)])
        ca_cols = np.concatenate([r256, 1024 + r128, 1280 + r128])
        cb_cols = np.concatenate([r256, 1024 + r256, 2048 + r256])
        return dict(
            w_in_es=WE[:, cols],
            conva_s_fm=conv_a[:, ca_cols].reshape(4, 4, 128).transpose(2, 1, 0),
            convb_s_fm=conv_b[:, cb_cols].reshape(4, 6, 128).transpose(2, 1, 0),
            ssd_par_s=np.stack([inp["dt_bias"][0][:, 4 * g:4 * g + 4].reshape(8), inp["a_log"][0][:, 4 * g:4 * g + 4].reshape(8)], 1),
            dskip_s=np.repeat(inp["d_skip"][0][:, 4 * g:4 * g + 4], 64, axis=1),
            hfw3_s=np.concatenate([W3[:, r256], W3[:, 1024 + r256]], 1),
            hyp_s_fm=np.stack([fm(-np.abs(deltas)[r256], 2), fm(inp["hy_bias"][0][r256], 2)], -1),
        )
    hgs = [hg_slices(g) for g in range(4)]
    hg_in = {k: np.ascontiguousarray(np.stack([h[k] for h in hgs]).astype(f32)) for k in hgs[0]}
    WO = inp["w_in_o"][0]
    w_in_o_pairs = np.ascontiguousarray(np.stack([
        np.concatenate([WO[:, seg * 1024 + hp * 128:seg * 1024 + hp * 128 + 128] for seg in range(4)], 1) for hp in range(8)]))
    rpb = inp["rpb"][0]
    qc_ = np.arange(64)[:, None]
    kc_ = np.arange(64)[None, :]
    col0 = np.clip(qc_ - 8, 0, 48)
    col_in = (kc_ >= col0) & (kc_ < col0 + 16)
    dcidx = np.clip(kc_ - qc_ + 15, 0, 30)
    strip_all = np.full((16, 64, 19, 64), NEG, f32)
    for sr in range(15):
        strip_all[:, :, sr + 1, :] = np.where(col_in[None], rpb[:, sr][:, dcidx], f32(NEG))
    strip_in = np.ascontiguousarray(strip_all.reshape(8, 128, 19 * 64))
    in_maps = []
    for core in range(8):
        b = core // 4
        g = core % 4
        j_ = g
        xe = np.zeros((2048, D), f32)
        lo, hi = 1024 * j_ - 512, 1024 * j_ + 1536
        slo, shi = max(lo, 0), min(hi, L_S)
        xe[slo - lo:shi - lo] = inp["x_sample"][b][slo:shi]
        bmk = np.zeros((16,), f32)
        if j_ - 1 >= 0:
            bmk[0 + j_ - 1] = 1.0
        bmk[4 + j_] = 1.0
        if j_ + 1 <= 3:
            bmk[8 + j_ + 1] = 1.0
        rm_ = np.full((16, 18, 64), NEG, f32)
        for rho in range(16):
            r = 16 * j_ + rho
            row0 = min(max(r - 4, 0), 56)
            for ip in range(18):
                sr = ip - 1 if rho % 2 == 0 else ip
                kr = r + sr - 7
                if 0 <= sr <= 14 and row0 <= kr < row0 + 8:
                    rm_[rho, ip, :] = 0.0
        ck = inp["cache_k"][b, 0]
        cvv = inp["cache_v"][b, 0]
        sample_maps = {
            "xs_in": np.ascontiguousarray(inp["x_sample"][b]),
            "st_in": np.ascontiguousarray(np.stack([inp["state_ssd"][b, 0, :, 4 * gg:4 * gg + 4].reshape(2, 256, 128) for gg in range(4)])),
            "featsS": featsS, "tnS": tnS,
            "w1tab": w1tab, "w1inv": w1inv, "w128": w128, "twid": twid,
            "x_ext": xe, "blkmask": bmk, "w_in_o_pairs": w_in_o_pairs,
            "ckT_in": np.ascontiguousarray(ck.reshape(8, 2, 256, 64).transpose(0, 1, 3, 2).reshape(8, 128, 256)),
            "cv_in": np.ascontiguousarray(cvv.reshape(8, 2, 2, 128, 64).transpose(0, 3, 2, 1, 4).reshape(8, 128, 2, 128)),
            "strip_in": strip_in, "rm_in": np.ascontiguousarray(rm_.reshape(16, 18 * 64)),
        }
        sample_maps.update(hg_in)
        m = {
            "xp": np.ascontiguousarray(inp["x_prompt"][core * NSEQ:(core + 1) * NSEQ]),
            "cv": np.ascontiguousarray(np.stack([fm(inp["c_ctx"], 8), fm(inp["c"][b], 8)], -1)),
            "w_ada": inp["w_ada"],
            "b_ada_fm": np.ascontiguousarray(np.stack([fm(inp["b_ada"][l], 24) for l in range(2)])),
            "norm_w_fm": np.ascontiguousarray(np.stack([fm(inp["norm_w"][l], 8) for l in range(2)])),
            "w_in_e": inp["w_in_e"][0],
            "w_out_e": inp["w_out_e"][0],
            "w_in_o": inp["w_in_o"][0],
            "w_out_o": inp["w_out_o"][0],
            "conv_a_fm": conv_a_fm,
            "conv_b_fm": conv_b_fm,
            "ssd_par": ssd_par,
            "sel_c": sel.reshape(32, 32 * 128),
            "dskip_rep": dskip_rep,
            "naw_fm": fm(inp["norm_a_w"][0], 8),
            "hyb_fm": fm(inp["hy_bias"][0], 8),
            "fnw": inp["final_norm_w"],
            "featsP": featsP, "tnP": tnP,
            "hfw1": inp["hf_w1"][0], "hfw2": inp["hf_w2"][0], "hfw3": inp["hf_w3"][0], "hfpar": hfpar,
            "ndelta_fm": ndelta_fm, "dftP": dftP,
        }
        m.update(sample_maps)
        in_maps.append(m)
    res = run_bass_kernel_spmd(nc, in_maps, core_ids=list(range(8)))
    st = np.concatenate([r["o_state"] for r in res.results], 0)
    new_state = st.reshape(32, 1, 2, 16, 64, 128).astype(f32)
    y_prompt = np.concatenate([r["o_yp"] for r in res.results], 0).astype(f32)
    new_k = np.concatenate([r["o_k"] for r in res.results], 0).reshape(32, 1, 16, 256, 64).astype(f32)
    new_v = np.concatenate([r["o_v"] for r in res.results], 0).reshape(32, 1, 16, 256, 64).astype(f32)
    y_sample = np.stack([np.concatenate([res.results[4 * b_ + j]["o_ys"] for j in range(4)], 0) for b_ in range(2)]).astype(f32)
    return (y_prompt, y_sample, new_state, new_k, new_v)
```

```python
import contextlib
import math
import numpy as np
import concourse.bass as bass
import concourse.mybir as mybir
from concourse.bass_utils import run_bass_kernel_spmd

F32 = mybir.dt.float32
BF16 = mybir.dt.bfloat16
AF = mybir.ActivationFunctionType
ALU = mybir.AluOpType
AX = mybir.AxisListType

D = 1024
L_P = 256
NSEQ = 4
EPS = 1e-6
NEG = -30000.0
TWO_PI = 2.0 * math.pi
DO_PROMPT = True
DO_SAMPLE = True
STAGE_S = 99
L_S = 4096
NWIN = 9


class Prog:
    R = 8

    def __init__(self, nc):
        self.nc = nc
        self.q = {e: [] for e in ("pe", "act", "dve", "pool", "sp")}
        self.cnt = {e: 0 for e in self.q}
        self.dma_cnt = {e: 0 for e in self.q}
        self.lastw = {}
        self.readers = {}
        self.seen = {e: {} for e in self.q}
        self.nbank = 0
        self.last_tok = {e: [] for e in self.q}
        self.snaps = {}

    def _need(self, eng, tok, waits):
        if tok is None:
            return
        semkey, val, teng = tok
        if teng == eng and eng == "pe":
            return
        cur = self.seen[eng].get(semkey, 0)
        if cur >= val:
            return
        self.seen[eng][semkey] = val
        waits.append((semkey, val))
        se = self.seen[eng]
        for sk2, v2 in self.snaps.get((semkey, val), ()):
            if se.get(sk2, 0) < v2:
                se[sk2] = v2

    def op(self, eng, fn, reads=(), writes=(), dma=False):
        writes = list(writes) + [r for r in reads if isinstance(r, tuple) and r[0] == "ps" and r not in writes]
        waits = []
        for r in reads:
            self._need(eng, self.lastw.get(r), waits)
        for w in writes:
            self._need(eng, self.lastw.get(w), waits)
            for t in self.readers.get(w, ()):
                self._need(eng, t, waits)
        if dma:
            j = self.dma_cnt[eng]
            self.dma_cnt[eng] += 1
            R = 1 if eng == "pool" else self.R
            semkey = ("dma", eng, j % R)
            val = 16 * (j // R + 1)
            if j >= R:
                self._need(eng, (semkey, val - 16, "dma"), waits)
            tok = (semkey, val, "dma")
            self.last_tok[eng] = (self.last_tok[eng] + [tok])[-R:]
        else:
            self.cnt[eng] += 1
            semkey = ("eng", eng)
            tok = (semkey, self.cnt[eng], eng)
        self.q[eng].append((fn, waits, semkey, dma))
        self.snaps[(tok[0], tok[1])] = tuple(self.seen[eng].items())
        for w in writes:
            self.lastw[w] = tok
            self.readers[w] = []
        for r in reads:
            if r not in writes:
                self.readers.setdefault(r, []).append(tok)
        return tok

    def barrier(self):
        toks = []
        for e in self.q:
            if e in ("pe", "act", "dve", "pool") and self.cnt[e]:
                toks.append((("eng", e), self.cnt[e], e))
            toks += self.last_tok[e]
        for e in self.q:
            waits = []
            for t in toks:
                self._need(e, t, waits)
            if waits:
                self.q[e].append((None, waits, None, False))

    def finish(self, eng, toks):
        waits = []
        for t in toks:
            self._need(eng, t, waits)
        self.q[eng].append((None, waits, None, False))

    def emit(self):
        nc = self.nc
        semkeys = set()
        for e, lst in self.q.items():
            for fn, waits, semkey, dma in lst:
                if semkey is not None:
                    semkeys.add(semkey)
                for (sk, v) in waits:
                    semkeys.add(sk)
        sems = {}
        with contextlib.ExitStack() as st:
            for sk in sorted(semkeys, key=str):
                sems[sk] = st.enter_context(nc.semaphore("s_" + "_".join(str(x) for x in sk)))
            block = st.enter_context(nc.Block())
            engmap = {"pe": block.tensor, "act": block.scalar, "dve": block.vector,
                      "pool": block.gpsimd, "sp": block.sync}

            def make(e):
                lst = self.q[e]

                def body(eng):
                    for fn, waits, semkey, dma in lst:
                        fuse = (fn is not None) and (not dma) and len(waits) > 0 and e != "pool"
                        for (sk, v) in (waits[:-1] if fuse else waits):
                            eng.wait_ge(sems[sk], v)
                        if fn is None:
                            continue
                        n0 = nc.n_instructions()
                        ins = fn(eng)
                        if fuse:
                            assert nc.n_instructions() - n0 == 1, ("multi-instruction op cannot carry a fused wait", e)
                            ins._wait_ge(sems[waits[-1][0]], waits[-1][1])
                        ins.then_inc(sems[semkey], 16 if dma else 1)
                return body
            for e in self.q:
                if self.q[e]:
                    engmap[e](make(e))


def fm(v, nt):
    return np.ascontiguousarray(np.asarray(v, np.float32).reshape(nt, 128).T)


def hyena_feats(L):
    f32 = np.float32
    t = np.linspace(0.0, 1.0, L, dtype=f32)[:, None]
    w = (f32(2.0 * math.pi) * np.arange(L, dtype=f32)[:, None] / f32(L)).astype(f32)
    f = np.linspace(1e-4, 15, 16, dtype=f32)[None]
    fw = (f * w).astype(f32)
    feats = np.concatenate([t, np.cos(fw), -np.sin(fw)], -1).astype(f32)
    allf = np.zeros((2 * L, 33), f32)
    allt = np.zeros((2 * L,), f32)
    allf[:L] = feats
    allt[:L] = t[:, 0]
    for j in range(L + 1, 2 * L):
        allf[j] = feats[2 * L - j]
        allt[j] = t[2 * L - j, 0]
    return np.ascontiguousarray(allf.T), allt


def build_nc():
    nc = bass.Bass("TRN2", target_bir_lowering=False)
    P = Prog(nc)

    def din(name, shape, dt=F32):
        return nc.dram_tensor(name, list(shape), dt, kind="ExternalInput").ap()

    def dout(name, shape, dt=F32):
        return nc.dram_tensor(name, list(shape), dt, kind="ExternalOutput").ap()

    ARENA_BYTES = 207 * 1024
    arena = nc.alloc_sbuf_tensor("arena", [128, ARENA_BYTES // 4], F32)
    cur = [0]

    def sb(name, shape, dt=F32):
        item = 4 if dt == F32 else 2
        n = 1
        for s_ in shape[1:]:
            n *= s_
        nbytes = (n * item + 31) // 32 * 32
        off = cur[0]
        cur[0] += nbytes
        assert cur[0] <= ARENA_BYTES, (name, cur[0])
        v = arena[:, off // 4:(off + nbytes) // 4]
        if dt != F32:
            v = v.bitcast(dt)
        v = v[0:shape[0], 0:n]
        if len(shape) == 3:
            v = v.rearrange("p (a b) -> p a b", b=shape[2])
        elif len(shape) == 4:
            v = v.rearrange("p (a b c) -> p a b c", b=shape[2], c=shape[3])
        return v

    def mm(out, lhsT, rhs, start, stop, reads, writes):
        P.op("pe", lambda e: e.matmul(out, lhsT=lhsT, rhs=rhs, start=start, stop=stop), reads, writes)

    def tr(out, in_, ident, reads, writes):
        P.op("pe", lambda e: e.transpose(out=out, in_=in_, identity=ident), reads, writes)

    def act(out, in_, func, reads, writes, bias=None, scale=None, accum=None):
        kw = {}
        if bias is not None:
            kw["bias"] = bias
        if scale is not None:
            kw["scale"] = scale
        if accum is not None:
            kw["accum_out"] = accum
        P.op("act", lambda e: e.activation(out=out, in_=in_, func=func, **kw), reads, writes)

    def tt(eng, out, in0, in1, op, reads, writes):
        P.op(eng, lambda e: e.tensor_tensor(out=out, in0=in0, in1=in1, op=op), reads, writes)

    def ts(eng, out, in0, s1, s2, op0, op1, reads, writes):
        if s2 is None:
            P.op(eng, lambda e: e.tensor_scalar(out=out, in0=in0, scalar1=s1, scalar2=None, op0=op0), reads, writes)
        else:
            P.op(eng, lambda e: e.tensor_scalar(out=out, in0=in0, scalar1=s1, scalar2=s2, op0=op0, op1=op1), reads, writes)

    def stt(eng, out, in0, scalar, in1, op0, op1, reads, writes):
        P.op(eng, lambda e: e.scalar_tensor_tensor(out=out, in0=in0, scalar=scalar, in1=in1, op0=op0, op1=op1), reads, writes)

    def cp(eng, out, in_, reads, writes):
        if eng == "act":
            P.op("act", lambda e: e.activation(out=out, in_=in_, func=AF.Copy), reads, writes)
        else:
            P.op(eng, lambda e: e.tensor_copy(out=out, in_=in_), reads, writes)

    def dma(eng, out, in_, reads, writes):
        return P.op(eng, lambda e: e.dma_start(out=out, in_=in_), reads, writes, dma=True)

    def memset(out, val, writes, eng="pool"):
        P.op(eng, lambda e: e.memset(out, val), (), writes)

    xp = din("xp", [NSEQ, L_P, D])
    cv = din("cv", [128, 8, 2])
    w_ada = din("w_ada", [2, D, 3 * D])
    b_ada_fm = din("b_ada_fm", [2, 128, 24])
    norm_w_fm = din("norm_w_fm", [2, 128, 8])
    w_in_e = din("w_in_e", [D, 6688])
    w_out_e = din("w_out_e", [2 * D, D])
    w_in_o = din("w_in_o", [D, 4 * D])
    w_out_o = din("w_out_o", [D, D])
    conv_a_fm = din("conv_a_fm", [128, 12, 4])
    conv_b_fm = din("conv_b_fm", [128, 24, 4])
    ssd_par = din("ssd_par", [32, 2])
    sel_c = din("sel_c", [32, 32 * 128])
    dskip_rep = din("dskip_rep", [2, D])
    naw_fm = din("naw_fm", [128, 8])
    hyb_fm = din("hyb_fm", [128, 8])
    fnw = din("fnw", [D])
    featsP = din("featsP", [33, 2 * L_P])
    tnP = din("tnP", [2 * L_P])
    hfw1 = din("hfw1", [33, 64])
    hfw2 = din("hfw2", [64, 64])
    hfw3 = din("hfw3", [64, 2 * D])
    hfpar = din("hfpar", [64, 3])
    ndelta_fm = din("ndelta_fm", [128, 8])
    dftP = din("dftP", [4, 512, 512])

    xs_in = din("xs_in", [L_S, D])
    w_in_es4 = din("w_in_es", [4, D, 1800])
    conva_s_fm4 = din("conva_s_fm", [4, 128, 4, 4])
    convb_s_fm4 = din("convb_s_fm", [4, 128, 6, 4])
    ssd_par_s4 = din("ssd_par_s", [4, 8, 2])
    dskip_s4 = din("dskip_s", [4, 2, 256])
    st_in4 = din("st_in", [4, 2, 256, 128])
    featsS = din("featsS", [33, 2 * L_S])
    tnS = din("tnS", [2 * L_S])
    hfw3_s4 = din("hfw3_s", [4, 64, 512])
    hyp_s_fm4 = din("hyp_s_fm", [4, 128, 2, 2])
    w1tab = din("w1tab", [64, 128])
    w1inv = din("w1inv", [128, 32])
    w128 = din("w128", [3, 128, 128])
    twid = din("twid", [2, 128, 64])
    x_ext = din("x_ext", [2048, D])
    blkmask = din("blkmask", [16])
    w_in_o_pairs = din("w_in_o_pairs", [8, D, 512])
    ckT_in = din("ckT_in", [8, 128, 256])
    cv_in = din("cv_in", [8, 128, 2, 128])
    strip_in = din("strip_in", [8, 128, 19 * 64])
    rm_in = din("rm_in", [16, 18 * 64])
    y_all_dbg = None
    x1_d = nc.dram_tensor("x1_d", [8, 128, 2048], F32, kind="Internal").ap()
    h2_d = nc.dram_tensor("h2_d", [64, 2 * L_S], BF16, kind="Internal").ap()
    hTw_d = nc.dram_tensor("hTw_d", [NWIN, 8, 128, 512], BF16, kind="Internal").ap()
    o_ys = dout("o_ys", [1024, D])
    zs_d = nc.dram_tensor("zs_d", [2, 128, L_S], BF16, kind="Internal").ap()
    gs_d = nc.dram_tensor("gs_d", [2, 128, L_S], BF16, kind="Internal").ap()
    x0_d = nc.dram_tensor("x0_d", [2, 128, L_S], BF16, kind="Internal").ap()
    dt_d = nc.dram_tensor("dt_d", [8, L_S], F32, kind="Internal").ap()
    y_all = nc.dram_tensor("y_all", [2 * D, L_S], BF16, kind="Internal").ap()
    o_dbg = None

    o_state = dout("o_state", [NSEQ, 2, 1024, 128])
    o_yp = dout("o_yp", [NSEQ, L_P, D])
    o_k = dout("o_k", [NSEQ, 16, L_P, 64])
    o_v = dout("o_v", [NSEQ, 16, L_P, 64])
    out_toks = []

    ps = [nc.alloc_psum_tensor(f"ps{i}", [128, 512], F32) for i in range(8)]

    def bank():
        i = P.nbank % 8
        P.nbank += 1
        return i

    def psb(b):
        return ps[b][:].bitcast(BF16)

    identf = sb("identf", [128, 128])
    identb = sb("identb", [128, 128], BF16)
    onesf = sb("onesf", [128, 128])
    maskF = sb("maskF", [128, 128])
    maskB = sb("maskB", [128, 128])
    triLE = sb("triLE", [128, 128])
    triGE = sb("triGE", [128, 128])
    zer = sb("zer", [128, 128])
    memset(zer, 0.0, ["zer"])
    memset(onesf, 1.0, ["onesf"])

    def aff(out, in_, pattern, cmp_op, fill, cm, reads, writes):
        P.op("pool", lambda e: e.affine_select(out=out, in_=in_, pattern=pattern, compare_op=cmp_op, fill=fill, base=0,
                                               channel_multiplier=cm), reads, writes)
    aff(identf, zer, [[-1, 128]], ALU.not_equal, 1.0, 1, ["zer"], ["identf"])
    aff(maskF, zer, [[1, 128]], ALU.is_ge, NEG, -1, ["zer"], ["maskF"])
    aff(maskB, zer, [[-1, 128]], ALU.is_ge, NEG, 1, ["zer"], ["maskB"])
    aff(triLE, onesf, [[1, 128]], ALU.is_ge, 0.0, -1, ["onesf"], ["triLE"])
    aff(triGE, onesf, [[-1, 128]], ALU.is_ge, 0.0, 1, ["onesf"], ["triGE"])
    cp("dve", identb, identf, ["identf"], ["identb"])

    sel = sb("sel", [32, 32, 128])
    dma("sp", sel, sel_c.rearrange("k (j m) -> k j m", m=128), [], ["sel"])
    conva = sb("conva", [128, 12, 4])
    dma("sp", conva, conv_a_fm, [], ["conva"])
    convb = sb("convb", [128, 24, 4])
    dma("sp", convb, conv_b_fm, [], ["convb"])
    ssdp = sb("ssdp", [32, 2])
    dma("sp", ssdp, ssd_par, [], ["ssdp"])
    aneg = sb("aneg", [32, 1])
    act(aneg, ssdp[:, 1:2], AF.Exp, ["ssdp"], ["aneg"])
    ts("dve", aneg, aneg, -1.0, None, ALU.mult, None, ["aneg"], ["aneg"])
    naw = sb("naw", [128, 8])
    dma("sp", naw, naw_fm, [], ["naw"])
    hyb = sb("hyb", [128, 8])
    dma("sp", hyb, hyb_fm, [], ["hyb"])
    dsk = sb("dsk", [128, D])
    dsk2 = sb("dsk2", [128, D])
    dma("sp", dsk, dskip_rep[0].partition_broadcast(128), [], ["dsk"])
    dma("sp", dsk2, dskip_rep[1].partition_broadcast(128), [], ["dsk2"])
    tt("dve", dsk, dsk, dsk2, ALU.add, ["dsk", "dsk2"], ["dsk"])
    fnw_bc = dsk2
    dma("sp", fnw_bc, fnw.partition_broadcast(128), ["dsk"], ["dsk2"])

    cvt = sb("cvt", [128, 8, 2])
    cvs = sb("cvs", [128, 8, 2], BF16)
    dma("sp", cvt, cv, [], ["cvt"])
    act(cvs, cvt, AF.Silu, ["cvt"], ["cvs"])
    bada = sb("bada", [128, 2, 24])
    nw = sb("nw", [128, 2, 8])
    dma("sp", bada, b_ada_fm.rearrange("l p j -> p l j"), [], ["bada"])
    dma("sp", nw, norm_w_fm.rearrange("l p j -> p l j"), [], ["nw"])
    mod = sb("mod", [128, 2, 24, 2])
    modA = sb("modA", [128, 2, 8, 2])
    persist_mark = cur[0]
    wa = sb("wa", [128, 8, 3 * D], BF16)
    for l in range(2):
        for half in range(2):
            dma("pool", wa[:, half * 4:(half + 1) * 4, :],
                w_ada[l, half * 512:(half + 1) * 512, :].rearrange("(kc p) n -> p kc n", p=128), [], [("wa", half)])
        b = bank()
        for j in range(24):
            for kc in range(8):
                mm(ps[b][:, j * 2:(j + 1) * 2], wa[:, kc, j * 128:(j + 1) * 128], cvs[:, kc, :], kc == 0, kc == 7,
                   [("wa", kc // 4), "cvs"], [("ps", b)])
        tt("dve", mod[:, l], ps[b][:, 0:48].rearrange("p (j c) -> p j c", c=2),
           bada[:, l, :].unsqueeze(2).to_broadcast([128, 24, 2]), ALU.add, [("ps", b), "bada"], [("mod", l)])
        stt("dve", modA[:, l], mod[:, l, 8:16, :], 1.0, nw[:, l, :].unsqueeze(2).to_broadcast([128, 8, 2]),
            ALU.add, ALU.mult, [("mod", l), "nw"], [("modA", l)])
    P.barrier()
    cur[0] = persist_mark

    prompt_mark = cur[0]
    dft = sb("dft", [128, 4, 4, 512], BF16)
    for t_ in range(4):
        dma("pool", dft[:, t_], dftP[t_].rearrange("(c p) n -> p c n", p=128), [], [("dft", t_)])
    Kre = sb("Kre", [128, 4, 1024], BF16)
    Kim = sb("Kim", [128, 4, 1024], BF16)
    filt_mark = cur[0]
    if True:
        n2 = 2 * L_P
        fe = sb("fe", [33, n2])
        tnb = sb("tnb", [128, n2])
        w1s = sb("w1s", [33, 64])
        w2s = sb("w2s", [64, 64])
        w3s = sb("w3s", [64, 2 * D])
        hp_ = sb("hp_", [64, 3])
        hsc = sb("hsc", [64, 4])
        ndl = sb("ndl", [128, 8])
        h1 = sb("h1", [64, n2])
        h2 = sb("h2", [64, n2])
        kT = sb("kT", [128, 8, n2], BF16)
        dec = sb("dec", [128, n2])
        ktok = sb("ktok", [128, 4, 1024], BF16)
        dma("sp", fe, featsP, [], ["fe"])
        dma("sp", tnb, tnP.partition_broadcast(128), [], ["tnb"])
        dma("sp", w1s, hfw1, [], ["w1s"])
        dma("sp", w2s, hfw2, [], ["w2s"])
        dma("sp", w3s, hfw3, [], ["w3s"])
        dma("sp", hp_, hfpar, [], ["hp_"])
        dma("sp", ndl, ndelta_fm, [], ["ndl"])
        ts("dve", hsc[:, 0:1], hp_[:, 2:3], 1.0 / TWO_PI, None, ALU.mult, None, ["hp_"], ["hsc"])
        for j in range(2):
            tt("dve", hsc[:, 1 + j:2 + j], hp_[:, j:j + 1], hsc[:, 0:1], ALU.mult, ["hp_", "hsc"], ["hsc"])

        MAGIC = 12582912.0
        kk = sb("kk", [64, n2])

        def sin_layer(dst, src_w, src_x, j):
            b = bank()
            mm(ps[b][0:64, 0:n2], src_w, src_x, True, True, ["w1s", "w2s", "fe", "h1"], [("ps", b)])
            ts("dve", dst, ps[b][0:64, 0:n2], hsc[:, 0:1], hsc[:, 1 + j:2 + j], ALU.mult, ALU.add, [("ps", b), "hsc"], ["hl"])
            ts("dve", kk, dst, MAGIC, None, ALU.add, None, ["hl"], ["kk"])
            ts("dve", kk, kk, -MAGIC, None, ALU.add, None, ["kk"], ["kk"])
            tt("dve", dst, dst, kk, ALU.subtract, ["hl", "kk"], ["hl"])
            ts("dve", dst, dst, TWO_PI, None, ALU.mult, None, ["hl"], ["hl"])
            ts("dve", dst, dst, math.pi, -math.pi, ALU.min, ALU.max, ["hl"], ["hl"])
            act(dst, dst, AF.Sin, ["hl"], ["h1", "h2"])
        sin_layer(h1, w1s, fe, 0)
        sin_layer(h2, w2s, h1, 1)
        for ct in range(8):
            b = bank()
            mm(ps[b][:, 0:L_P], w3s[:, ct * 128:(ct + 1) * 128], h2[:, 0:L_P], True, True, ["w3s", "h2"], [("ps", b)])
            mm(ps[b][:, L_P:n2], w3s[:, D + ct * 128:D + (ct + 1) * 128], h2[:, L_P:n2], True, True, ["w3s", "h2"], [("ps", b)])
            act(dec, tnb, AF.Exp, ["tnb", "ndl"], ["dec"], scale=ndl[:, ct:ct + 1])
            tt("dve", kT[:, ct, :], ps[b][:, 0:n2], dec, ALU.mult, [("ps", b), "dec"], ["kT"])
        memset(kT[:, :, L_P:L_P + 1], 0.0, ["kT"], eng="dve")
        for dc in range(4):
            for half in range(2):
                b = bank()
                for j in range(4):
                    ct = half * 4 + j
                    tr(psb(b)[:, j * 128:(j + 1) * 128], kT[:, ct, dc * 128:(dc + 1) * 128], identb, ["kT", "identb"], [("ps", b)])
                cp("act", ktok[:, dc, half * 512:(half + 1) * 512], psb(b)[:, 0:512], [("ps", b)], ["ktok"])
        for ft in range(4):
            for half in range(2):
                for ri in range(2):
                    b = bank()
                    for dc in range(4):
                        mm(ps[b][:, :], dft[:, ri, dc, ft * 128:(ft + 1) * 128], ktok[:, dc, half * 512:(half + 1) * 512],
                           dc == 0, dc == 3, [("dft", ri), "ktok"], [("ps", b)])
                    cp("act", (Kim if ri else Kre)[:, ft, half * 512:(half + 1) * 512], ps[b][:, :],
                       [("ps", b)], ["Kim" if ri else "Kre"])
    P.barrier()
    cur[0] = filt_mark

    xin = sb("xin", [128, 2, D])
    xT = sb("xT", [128, 8, L_P])
    scr8 = sb("scr8", [128, 8, L_P])
    rstd = sb("rstd", [128, L_P])
    hT = sb("hT", [128, 8, L_P], BF16)
    wbuf = [sb(f"wbuf{i}", [128, 8, 512], BF16) for i in range(2)]
    tmpc = [sb(f"tmpc{i}", [128, L_P]) for i in range(2)]
    sm = sb("sm", [128, 8])
    l0_mark = cur[0]
    xbcT = sb("xbcT", [128, 12, L_P], BF16)
    dtraw = sb("dtraw", [32, L_P])
    dtT = sb("dtT", [32, L_P])
    adtT = sb("adtT", [32, L_P])
    dt_tok = sb("dt_tok", [128, 2, 32])
    adt_tok = sb("adt_tok", [128, 2, 32])
    cs_tok = sb("cs_tok", [128, 2, 32])
    ncs_tok = sb("ncs_tok", [128, 2, 32])
    dec_tok = sb("dec_tok", [128, 2, 32])
    ecs_tok = sb("ecs_tok", [128, 2, 32])
    etot = sb("etot", [128, 2, 32])
    dtdec_tok = sb("dtdec_tok", [128, 2, 32])
    csT = sb("csT", [32, 2, 128])
    x_tok = sb("x_tok", [128, 2, 1024], BF16)
    B_tok = sb("B_tok", [128, 2, 2, 128], BF16)
    xdt = sb("xdt", [128, 2, 2, 1024], BF16)
    xdd = sb("xdd", [128, 2, 2, 1024], BF16)
    GT = sb("GT", [128, 2, 2, 128])
    t1 = [sb(f"t1_{i}", [128, 4, 128]) for i in range(1)]
    t2 = [sb(f"t2_{i}", [128, 4, 128]) for i in range(1)]
    MT = [sb(f"MT_{i}", [128, 4, 128], BF16) for i in range(2)]
    y_tok = sb("y_tok", [128, 2, 1024])
    ytmp = sb("ytmp", [128, 512])
    ST = [sb(f"ST{d}", [128, 1024]) for d in range(2)]
    STb = [sb(f"STb{d}", [128, 1024], BF16) for d in range(2)]
    stout = xin.rearrange("p c d -> p (c d)")[:, 0:1024].rearrange("p (a n) -> p a n", n=128)
    ssd_end = cur[0]
    u16 = sb("u16", [128, 24, L_P], BF16)
    zs = sb("zs", [128, 8, L_P], BF16)
    gs = sb("gs", [128, 8, L_P], BF16)
    ygT = sb("ygT", [128, 8, L_P], BF16)
    ybT = sb("ybT", [128, 8, L_P], BF16)
    rstdy = sb("rstdy", [128, L_P])
    l0_end = cur[0]
    cur[0] = l0_mark
    uvb = sb("uvb", [128, 8, L_P], BF16)
    uv_tok = sb("uv_tok", [128, 2, 1024], BF16)
    Xs = [sb(f"Xs{i}", [128, 512]) for i in range(2)]
    ta = [sb(f"ta{i}", [128, 512]) for i in range(2)]
    Yre = sb("Yre", [128, 4, 1024], BF16)
    Yim = sb("Yim", [128, 4, 1024], BF16)
    assert cur[0] <= ssd_end
    cur[0] = l0_mark
    qT = sb("qT", [128, 8, L_P], BF16)
    QBD = sb("QBD", [128, 8, 4, 128], BF16)
    kTf = sb("kTf", [128, 8, L_P])
    vTf = sb("vTf", [128, 8, L_P])
    kTb = sb("kTb", [128, 8, L_P], BF16)
    v_tok = sb("v_tok", [128, 2, 1024], BF16)
    gs1 = sb("gs1", [128, 8, L_P], BF16)
    ogT = sb("ogT", [128, 8, L_P], BF16)
    Pm = [sb(f"Pm{i}", [128, L_P], BF16) for i in range(2)]
    PT = [sb(f"PT{i}", [128, 2, 128], BF16) for i in range(2)]
    assert cur[0] <= l0_end
    cur[0] = l0_end
    print("SBUF bytes used per partition:", cur[0])

    gcount = [0]

    def load_w(wsrc, row0, col0, ncols):
        g = gcount[0]
        gcount[0] += 1
        wb = wbuf[g % 2]
        dma("pool", wb[:, :, 0:ncols], wsrc[row0:row0 + 1024, col0:col0 + ncols].rearrange("(kc p) n -> p kc n", p=128),
            [], [("wbuf", g % 2)])
        return wb, ("wbuf", g % 2)

    def proj_tile(wb, wkey, off, m, src, srckey, evac, b=None):
        if b is None:
            b = bank()
        for kc in range(8):
            mm(ps[b][0:m, 0:L_P], wb[:, kc, off:off + m], src[:, kc, :], kc == 0, kc == 7, [wkey, srckey], [("ps", b)])
        if evac is not None:
            evac(b)
        return b

    def norm_mod(layer):
        act(scr8, xT, AF.Square, ["xT"], ["scr8"])
        b = bank()
        for ft in range(8):
            mm(ps[b][:, 0:L_P], onesf, scr8[:, ft, :], ft == 0, ft == 7, ["onesf", "scr8"], [("ps", b)])
        ts("dve", rstd, ps[b][:, 0:L_P], 1.0 / D, EPS, ALU.mult, ALU.add, [("ps", b)], ["rstd"])
        act(rstd, rstd, AF.Sqrt, ["rstd"], ["rstd"])
        P.op("dve", lambda e: e.reciprocal(out=rstd, in_=rstd), ["rstd"], ["rstd"])
        for ft in range(8):
            stt("dve", scr8[:, ft, :], xT[:, ft, :], modA[:, layer, ft, 0:1], rstd, ALU.mult, ALU.mult,
                ["xT", ("modA", layer), "rstd"], ["scr8"])
            act(hT[:, ft, :], scr8[:, ft, :], AF.Identity, ["scr8", ("mod", layer)], ["hT"], bias=mod[:, layer, ft, 0:1], scale=1.0)

    def conv_evac(b, dst, cw, i, silu, key):
        tc_ = tmpc[i % 2]
        tk = ("tmpc", i % 2)
        ts("dve", tc_, ps[b][:, 0:L_P], cw[:, i, 1:2], cw[:, i, 3:4], ALU.mult, ALU.add, [("ps", b), "conva", "convb"], [tk])
        stt("dve", tc_[:, 1:L_P], ps[b][:, 0:L_P - 1], cw[:, i, 0:1], tc_[:, 1:L_P], ALU.mult, ALU.add, [("ps", b), tk], [tk])
        stt("dve", tc_[:, 0:L_P - 1], ps[b][:, 1:L_P], cw[:, i, 2:3], tc_[:, 0:L_P - 1], ALU.mult, ALU.add, [("ps", b), tk], [tk])
        if silu:
            act(dst, tc_, AF.Silu, [tk], [key])
        else:
            cp("act", dst, tc_, [tk], [key])

    def seq_body(s):
        dma("sp", xin, xp[s].rearrange("(tt p) d -> p tt d", p=128), [], ["xin"])
        for ft in range(8):
            b = bank()
            for tt_ in range(2):
                tr(ps[b][:, tt_ * 128:(tt_ + 1) * 128], xin[:, tt_, ft * 128:(ft + 1) * 128], identf, ["xin", "identf"], [("ps", b)])
            cp("act", xT[:, ft, :], ps[b][:, 0:L_P], [("ps", b)], ["xT"])
        norm_mod(0)

        for gi in range(2):
            wb, wk = load_w(w_in_e, 0, gi * 512, 512)
            for j in range(4):
                i = gi * 4 + j
                proj_tile(wb, wk, j * 128, 128, hT, "hT",
                          lambda b, i=i: act(zs[:, i, :], ps[b][:, 0:L_P], AF.Silu, [("ps", b)], ["zs"]))
        for gi in range(3):
            wb, wk = load_w(w_in_e, 0, 1024 + gi * 512, 512)
            for j in range(4):
                i = gi * 4 + j
                proj_tile(wb, wk, j * 128, 128, hT, "hT", lambda b, i=i: conv_evac(b, xbcT[:, i, :], conva, i, True, ("xbcT", i)))
        wb, wk = load_w(w_in_e, 0, 2560, 32)
        proj_tile(wb, wk, 0, 32, hT, "hT", lambda b: cp("act", dtraw, ps[b][0:32, 0:L_P], [("ps", b)], ["dtraw"]))
        for gi in range(6):
            wb, wk = load_w(w_in_e, 0, 2592 + gi * 512, 512)
            for j in range(4):
                i = gi * 4 + j
                proj_tile(wb, wk, j * 128, 128, hT, "hT", lambda b, i=i: conv_evac(b, u16[:, i, :], convb, i, False, "u16"))
        for gi in range(2):
            wb, wk = load_w(w_in_e, 0, 5664 + gi * 512, 512)
            for j in range(4):
                i = gi * 4 + j
                proj_tile(wb, wk, j * 128, 128, hT, "hT",
                          lambda b, i=i: act(gs[:, i, :], ps[b][:, 0:L_P], AF.Silu, [("ps", b)], ["gs"]))

        act(dtT, dtraw, AF.Exp, ["dtraw", "ssdp"], ["dtT"], bias=ssdp[:, 0:1], scale=1.0)
        act(dtT, dtT, AF.Ln, ["dtT"], ["dtT"], bias=1.0, scale=1.0)
        ts("dve", adtT, dtT, aneg[:, 0:1], None, ALU.mult, None, ["dtT", "aneg"], ["adtT"])
        b = bank()
        for c in range(2):
            tr(ps[b][:, c * 32:(c + 1) * 32], dtT[:, c * 128:(c + 1) * 128], identf[0:32, 0:32], ["dtT", "identf"], [("ps", b)])
            tr(ps[b][:, 64 + c * 32:64 + (c + 1) * 32], adtT[:, c * 128:(c + 1) * 128], identf[0:32, 0:32], ["adtT", "identf"], [("ps", b)])
        cp("dve", dt_tok, ps[b][:, 0:64].rearrange("p (c j) -> p c j", j=32), [("ps", b)], ["dt_tok"])
        cp("dve", adt_tok, ps[b][:, 64:128].rearrange("p (c j) -> p c j", j=32), [("ps", b)], ["adt_tok"])
        b = bank()
        b2 = bank()
        for c in range(2):
            mm(ps[b][:, c * 32:c * 32 + 16], triLE, adt_tok[:, c, 0:16], True, True, ["triLE", "adt_tok"], [("ps", b)])
            mm(ps[b][:, c * 32 + 16:c * 32 + 32], triGE, adt_tok[:, c, 16:32], True, True, ["triGE", "adt_tok"], [("ps", b)])
            mm(ps[b2][:, c * 32:(c + 1) * 32], onesf, adt_tok[:, c, :], True, True, ["onesf", "adt_tok"], [("ps", b2)])
        cp("dve", cs_tok, ps[b][:, 0:64].rearrange("p (c j) -> p c j", j=32), [("ps", b)], ["cs_tok"])
        ts("dve", ncs_tok, cs_tok, -1.0, None, ALU.mult, None, ["cs_tok"], ["ncs_tok"])
        act(etot, ps[b2][:, 0:64].rearrange("p (c j) -> p c j", j=32), AF.Exp, [("ps", b2)], ["etot"])
        tt("dve", dec_tok, ps[b2][:, 0:64].rearrange("p (c j) -> p c j", j=32), cs_tok, ALU.subtract, [("ps", b2), "cs_tok"], ["dec_tok"])
        act(dec_tok, dec_tok, AF.Exp, ["dec_tok"], ["dec_tok"])
        act(ecs_tok, cs_tok, AF.Exp, ["cs_tok"], ["ecs_tok"])
        tt("dve", dtdec_tok, dt_tok, dec_tok, ALU.mult, ["dt_tok", "dec_tok"], ["dtdec_tok"])
        b = bank()
        for c in range(2):
            tr(ps[b][0:32, c * 128:(c + 1) * 128], cs_tok[:, c, :], identf, ["cs_tok", "identf"], [("ps", b)])
        cp("dve", csT, ps[b][0:32, 0:256].rearrange("p (c j) -> p c j", j=128), [("ps", b)], ["csT"])

        for c in range(2):
            for half in range(2):
                b = bank()
                for j in range(4):
                    i = half * 4 + j
                    tr(psb(b)[:, j * 128:(j + 1) * 128], xbcT[:, i, c * 128:(c + 1) * 128], identb, [("xbcT", i), "identb"], [("ps", b)])
                cp("act", x_tok[:, c, half * 512:(half + 1) * 512], psb(b)[:, 0:512], [("ps", b)], ["x_tok"])
            b = bank()
            for g2 in range(2):
                tr(psb(b)[:, g2 * 128:(g2 + 1) * 128], xbcT[:, 8 + g2, c * 128:(c + 1) * 128], identb, [("xbcT", 8 + g2), "identb"], [("ps", b)])
            cp("act", B_tok[:, c].rearrange("p g n -> p (g n)"), psb(b)[:, 0:256], [("ps", b)], ["B_tok"])
            b = bank()
            for g2 in range(2):
                mm(ps[b][:, g2 * 128:(g2 + 1) * 128], xbcT[:, 8 + g2, c * 128:(c + 1) * 128], xbcT[:, 10 + g2, c * 128:(c + 1) * 128],
                   True, True, [("xbcT", 8 + g2), ("xbcT", 10 + g2)], [("ps", b)])
            cp("dve", GT[:, c].rearrange("p g n -> p (g n)"), ps[b][:, 0:256], [("ps", b)], ["GT"])
            for d in range(2):
                tt("dve", xdt[:, c, d].rearrange("p (h q) -> p h q", q=64), x_tok[:, c].rearrange("p (h q) -> p h q", q=64),
                   dt_tok[:, c, d * 16:(d + 1) * 16].unsqueeze(2).to_broadcast([128, 16, 64]), ALU.mult, ["x_tok", "dt_tok"], ["xdt"])
                tt("dve", xdd[:, c, d].rearrange("p (h q) -> p h q", q=64), x_tok[:, c].rearrange("p (h q) -> p h q", q=64),
                   dtdec_tok[:, c, d * 16:(d + 1) * 16].unsqueeze(2).to_broadcast([128, 16, 64]), ALU.mult, ["x_tok", "dtdec_tok"], ["xdd"])

        for c in range(2):
            tt("dve", y_tok[:, c, :], x_tok[:, c, :], dsk, ALU.mult, ["x_tok", "dsk"], ["y_tok"])
        def p_prep(d, c, hg, k):
            mask = maskF if d == 0 else maskB
            b = bank()
            for j in range(4):
                h = hg * 4 + j
                mm(ps[b][:, j * 128:(j + 1) * 128], sel[:, d * 16 + h, :], csT[:, c, :], True, True, ["sel", "csT"], [("ps", b)])
            tt("dve", t1[0], ps[b][:, :].rearrange("p (j l) -> p j l", l=128), mask.unsqueeze(1).to_broadcast([128, 4, 128]),
               ALU.add, [("ps", b), "maskF", "maskB"], [("t1", 0)])
            for j in range(4):
                h = hg * 4 + j
                act(t2[0][:, j, :], t1[0][:, j, :], AF.Exp, [("t1", 0), "ncs_tok"], [("t2", 0)],
                    bias=ncs_tok[:, c, d * 16 + h:d * 16 + h + 1], scale=1.0)
            g2 = hg // 2
            tt("dve", MT[k], t2[0], GT[:, c, g2, :].unsqueeze(1).to_broadcast([128, 4, 128]), ALU.mult,
               [("t2", 0), "GT"], [("MT", k)])

        byb = [None]

        def p_consume(d, ci, c, hg, k):
            g2 = hg // 2
            if hg % 2 == 0:
                byb[0] = bank()
            by = byb[0]
            for j in range(4):
                h = hg * 4 + j
                hs = (h % 8) * 64
                mm(ps[by][:, hs:hs + 64], MT[k][:, j, :], xdt[:, c, d, h * 64:(h + 1) * 64], True, True,
                   [("MT", k), "xdt"], [("ps", by)])
            if hg % 2 == 1:
                tt("dve", y_tok[:, c, g2 * 512:(g2 + 1) * 512], y_tok[:, c, g2 * 512:(g2 + 1) * 512], ps[by][:, :], ALU.add,
                   ["y_tok", ("ps", by)], ["y_tok"])
            if hg != 3:
                return
            if ci > 0:
                for g2 in range(2):
                    b = bank()
                    mm(ps[b][:, :], xbcT[:, 10 + g2, c * 128:(c + 1) * 128], STb[d][:, g2 * 512:(g2 + 1) * 512], True, True,
                       [("xbcT", 10 + g2), ("STb", d)], [("ps", b)])
                    tt("dve", ytmp.rearrange("p (h q) -> p h q", q=64), ps[b][:, :].rearrange("p (h q) -> p h q", q=64),
                       ecs_tok[:, c, d * 16 + g2 * 8:d * 16 + g2 * 8 + 8].unsqueeze(2).to_broadcast([128, 8, 64]), ALU.mult,
                       [("ps", b), "ecs_tok"], ["ytmp"])
                    tt("dve", y_tok[:, c, g2 * 512:(g2 + 1) * 512], y_tok[:, c, g2 * 512:(g2 + 1) * 512], ytmp, ALU.add,
                       ["y_tok", "ytmp"], ["y_tok"])
            for g2 in range(2):
                b = bank()
                mm(ps[b][:, :], B_tok[:, c, g2, :], xdd[:, c, d, g2 * 512:(g2 + 1) * 512], True, True, ["B_tok", "xdd"], [("ps", b)])
                sl = ST[d][:, g2 * 512:(g2 + 1) * 512]
                tt("dve", sl.rearrange("p (h q) -> p h q", q=64), sl.rearrange("p (h q) -> p h q", q=64),
                   etot[:, c, d * 16 + g2 * 8:d * 16 + g2 * 8 + 8].unsqueeze(2).to_broadcast([128, 8, 64]), ALU.mult,
                   [("ST", d), "etot"], [("ST", d)])
                tt("dve", sl, sl, ps[b][:, :], ALU.add, [("ST", d), ("ps", b)], [("ST", d)])
            if ci == 0:
                cp("act", STb[d], ST[d], [("ST", d)], [("STb", d)])

        for d in range(2):
            memset(ST[d], 0.0, [("ST", d)])
            memset(STb[d], 0.0, [("STb", d)])
        p_items = []
        for d in range(2):
            for ci, c in enumerate([0, 1] if d == 0 else [1, 0]):
                for hg in range(4):
                    p_items.append((d, ci, c, hg))
        p_prep(p_items[0][0], p_items[0][2], p_items[0][3], 0)
        for d in range(2):
            for i_ in range(8 * d, 8 * d + 8):
                d_, ci, c, hg = p_items[i_]
                if i_ + 1 < len(p_items):
                    n_ = p_items[i_ + 1]
                    p_prep(n_[0], n_[2], n_[3], (i_ + 1) % 2)
                p_consume(d_, ci, c, hg, i_ % 2)
            for half in range(2):
                b = bank()
                for j in range(4):
                    i = half * 4 + j
                    tr(ps[b][:, j * 128:(j + 1) * 128], ST[d][:, i * 128:(i + 1) * 128], identf, [("ST", d), "identf"], [("ps", b)])
                cp("act", stout[:, half * 4:(half + 1) * 4, :].rearrange("p a n -> p (a n)"), ps[b][:, :], [("ps", b)], ["xin"])
            out_toks.append(dma("sp", o_state[s, d].rearrange("(a p) n -> p a n", p=128), stout, ["xin"], [("o_state", s, d)]))

        for c in range(2):
            for half in range(2):
                b = bank()
                for j in range(4):
                    ft = half * 4 + j
                    tr(ps[b][:, j * 128:(j + 1) * 128], y_tok[:, c, ft * 128:(ft + 1) * 128], identf, ["y_tok", "identf"], [("ps", b)])
                tt("dve", scr8[:, half * 4:(half + 1) * 4, c * 128:(c + 1) * 128], ps[b][:, :].rearrange("p (j l) -> p j l", l=128),
                   zs[:, half * 4:(half + 1) * 4, c * 128:(c + 1) * 128], ALU.mult, [("ps", b), "zs"], ["scr8"])
        for ft in range(8):
            act(ygT[:, ft, :], scr8[:, ft, :], AF.Copy, ["scr8", "naw"], ["ygT"], scale=naw[:, ft:ft + 1])
        act(scr8, scr8, AF.Square, ["scr8"], ["scr8"])
        b = bank()
        for ft in range(8):
            mm(ps[b][:, 0:L_P], onesf, scr8[:, ft, :], ft == 0, ft == 7, ["onesf", "scr8"], [("ps", b)])
        ts("dve", rstdy, ps[b][:, 0:L_P], 1.0 / D, EPS, ALU.mult, ALU.add, [("ps", b)], ["rstdy"])
        act(rstdy, rstdy, AF.Sqrt, ["rstdy"], ["rstdy"])
        P.op("dve", lambda e: e.reciprocal(out=rstdy, in_=rstdy), ["rstdy"], ["rstdy"])

        P.barrier()
        tt("dve", uvb, u16[:, 16:24, :], u16[:, 8:16, :], ALU.mult, ["u16"], ["uvb"])
        for c in range(2):
            for half in range(2):
                b = bank()
                for j in range(4):
                    ct = half * 4 + j
                    tr(psb(b)[:, j * 128:(j + 1) * 128], uvb[:, ct, c * 128:(c + 1) * 128], identb, ["uvb", "identb"], [("ps", b)])
                cp("act", uv_tok[:, c, half * 512:(half + 1) * 512], psb(b)[:, 0:512], [("ps", b)], ["uv_tok"])
        for ft in range(4):
            for half in range(2):
                bre, bim = bank(), bank()
                for ri, bb in ((0, bre), (1, bim)):
                    for c in range(2):
                        mm(ps[bb][:, :], dft[:, ri, c, ft * 128:(ft + 1) * 128], uv_tok[:, c, half * 512:(half + 1) * 512],
                           c == 0, c == 1, [("dft", ri), "uv_tok"], [("ps", bb)])
                cp("act", Xs[0], ps[bre][:, :], [("ps", bre)], [("Xs", 0)])
                cp("act", Xs[1], ps[bim][:, :], [("ps", bim)], [("Xs", 1)])
                kr = Kre[:, ft, half * 512:(half + 1) * 512]
                ki = Kim[:, ft, half * 512:(half + 1) * 512]
                yr = Yre[:, ft, half * 512:(half + 1) * 512]
                yi = Yim[:, ft, half * 512:(half + 1) * 512]
                tt("dve", ta[0], Xs[0], kr, ALU.mult, [("Xs", 0), "Kre"], [("ta", 0)])
                tt("dve", ta[1], Xs[1], ki, ALU.mult, [("Xs", 1), "Kim"], [("ta", 1)])
                tt("dve", yr, ta[0], ta[1], ALU.subtract, [("ta", 0), ("ta", 1)], ["Yre"])
                tt("dve", ta[1], Xs[0], ki, ALU.mult, [("Xs", 0), "Kim"], [("ta", 1)])
                tt("dve", ta[0], Xs[1], kr, ALU.mult, [("Xs", 1), "Kre"], [("ta", 0)])
                tt("dve", yi, ta[0], ta[1], ALU.add, [("ta", 0), ("ta", 1)], ["Yim"])
        for ct in range(8):
            b = bank()
            for ft in range(4):
                mm(ps[b][:, 0:L_P], Yre[:, ft, ct * 128:(ct + 1) * 128], dft[:, 2, ft, 0:L_P], ft == 0, False, ["Yre", ("dft", 2)], [("ps", b)])
            for ft in range(4):
                mm(ps[b][:, 0:L_P], Yim[:, ft, ct * 128:(ct + 1) * 128], dft[:, 3, ft, 0:L_P], False, ft == 3, ["Yim", ("dft", 3)], [("ps", b)])
            tc_ = tmpc[ct % 2]
            tk = ("tmpc", ct % 2)
            stt("dve", tc_, uvb[:, ct, :], hyb[:, ct:ct + 1], ps[b][:, 0:L_P], ALU.mult, ALU.add, ["uvb", "hyb", ("ps", b)], [tk])
            tt("dve", tc_, tc_, u16[:, ct, :], ALU.mult, [tk, "u16"], [tk])
            tt("dve", ybT[:, ct, :], tc_, gs[:, ct, :], ALU.mult, [tk, "gs"], ["ybT"])

        for nh in range(2):
            wa_, wka = load_w(w_out_e, 0, nh * 512, 512)
            wb_, wkb = load_w(w_out_e, 1024, nh * 512, 512)
            for j in range(4):
                ot = nh * 4 + j
                ba = proj_tile(wa_, wka, j * 128, 128, ygT, "ygT", None)
                bb = proj_tile(wb_, wkb, j * 128, 128, ybT, "ybT", None)
                tc_ = tmpc[ot % 2]
                tk = ("tmpc", ot % 2)
                tt("dve", tc_, ps[ba][:, 0:L_P], rstdy, ALU.mult, [("ps", ba), "rstdy"], [tk])
                tt("dve", tc_, tc_, ps[bb][:, 0:L_P], ALU.add, [tk, ("ps", bb)], [tk])
                stt("dve", xT[:, ot, :], tc_, mod[:, 0, 16 + ot, 0:1], xT[:, ot, :], ALU.mult, ALU.add, [tk, ("mod", 0), "xT"], ["xT"])

        P.barrier()
        memset(QBD, 0.0, ["QBD"])
        norm_mod(1)
        for seg in range(4):
            for gi in range(2):
                wb, wk = load_w(w_in_o, 0, seg * 1024 + gi * 512, 512)
                for j in range(4):
                    i = gi * 4 + j
                    if seg == 0:
                        ev = lambda b, i=i: cp("act", qT[:, i, :], ps[b][:, 0:L_P], [("ps", b)], ["qT"])
                    elif seg == 1:
                        def ev(b, i=i):
                            cp("act", kTf[:, i, :], ps[b][:, 0:L_P], [("ps", b)], ["kTf"])
                            cp("dve", kTb[:, i, :], ps[b][:, 0:L_P], [("ps", b)], ["kTb"])
                    elif seg == 2:
                        ev = lambda b, i=i: cp("act", vTf[:, i, :], ps[b][:, 0:L_P], [("ps", b)], ["vTf"])
                    else:
                        ev = lambda b, i=i: act(gs1[:, i, :], ps[b][:, 0:L_P], AF.Silu, [("ps", b)], ["gs1"])
                    proj_tile(wb, wk, j * 128, 128, hT, "hT", ev)
        for which, src, odram in ((0, kTf, o_k), (1, vTf, o_v)):
            for c in range(2):
                for half in range(2):
                    b = bank()
                    for j in range(4):
                        ft = half * 4 + j
                        tr(ps[b][:, j * 128:(j + 1) * 128], src[:, ft, c * 128:(c + 1) * 128], identf, ["kTf", "vTf", "identf"], [("ps", b)])
                    cp("act", xin[:, c, half * 512:(half + 1) * 512], ps[b][:, :], [("ps", b)], ["xin"])
                    if which == 1:
                        cp("dve", v_tok[:, c, half * 512:(half + 1) * 512], ps[b][:, :], [("ps", b)], ["v_tok"])
            for c in range(2):
                out_toks.append(dma("sp", odram[s][:, c * 128:(c + 1) * 128, :].rearrange("h p e -> p h e"),
                                    xin[:, c, :].rearrange("p (h e) -> p h e", e=64), ["xin"], [("o_kv", s, which, c)]))
        for hp2 in range(8):
            cp("dve", QBD[0:64, hp2, :, 0:64], qT[0:64, hp2, :].rearrange("p (qb q) -> p qb q", q=64), ["qT"], ["QBD"])
            cp("dve", QBD[64:128, hp2, :, 64:128], qT[64:128, hp2, :].rearrange("p (qb q) -> p qb q", q=64), ["qT"], ["QBD"])
        def a_S1(hp2, qb):
            b = bank()
            mm(ps[b][:, 0:L_P], QBD[:, hp2, qb, :], kTb[:, hp2, :], True, True, ["QBD", "kTb"], [("ps", b)])
            return b

        def a_S2(b, k):
            P.op("dve", lambda e, b=b: e.tensor_reduce(out=sm[:, 0:1], in_=ps[b][:, 0:L_P], axis=AX.X, op=ALU.max, negate=True),
                 [("ps", b)], ["sm0"])
            ts("dve", sm[:, 1:2], sm[:, 0:1], 0.125, None, ALU.mult, None, ["sm0"], ["sm1"])
            act(Pm[k], ps[b][:, 0:L_P], AF.Exp, [("ps", b), "sm1"], [("Pm", k), "sm2"], bias=sm[:, 1:2], scale=0.125, accum=sm[:, 2:3])
            P.op("dve", lambda e: e.reciprocal(out=sm[:, 3:4], in_=sm[:, 2:3]), ["sm2"], ["sm3"])
            ts("dve", Pm[k], Pm[k], sm[:, 3:4], None, ALU.mult, None, [("Pm", k), "sm3"], [("Pm", k)])

        def a_S3(hp2, qb, k):
            b2 = bank()
            for kt in range(2):
                tr(psb(b2)[:, kt * 128:(kt + 1) * 128], Pm[k][:, kt * 128:(kt + 1) * 128], identb, [("Pm", k), "identb"], [("ps", b2)])
            cp("act", PT[k].rearrange("p a q -> p (a q)"), psb(b2)[:, 0:256], [("ps", b2)], [("PT", k)])
            b3 = bank()
            for kt in range(2):
                mm(ps[b3][:, 0:128], v_tok[:, kt, hp2 * 128:(hp2 + 1) * 128], PT[k][:, kt, :], kt == 0, kt == 1,
                   ["v_tok", ("PT", k)], [("ps", b3)])
            tt("dve", ogT[0:64, hp2, qb * 64:(qb + 1) * 64], ps[b3][0:64, 0:64], gs1[0:64, hp2, qb * 64:(qb + 1) * 64], ALU.mult,
               [("ps", b3), "gs1"], ["ogT"])
            tt("dve", ogT[64:128, hp2, qb * 64:(qb + 1) * 64], ps[b3][64:128, 64:128], gs1[64:128, hp2, qb * 64:(qb + 1) * 64], ALU.mult,
               [("ps", b3), "gs1"], ["ogT"])

        a_items = [(hp2, qb) for hp2 in range(8) for qb in range(4)]
        nb_ = a_S1(*a_items[0])
        for i_, (hp2, qb) in enumerate(a_items):
            a_S2(nb_, i_ % 2)
            if i_ + 1 < len(a_items):
                nb_ = a_S1(*a_items[i_ + 1])
            a_S3(hp2, qb, i_ % 2)
        for nh in range(2):
            wb, wk = load_w(w_out_o, 0, nh * 512, 512)
            for j in range(4):
                ot = nh * 4 + j
                bo = proj_tile(wb, wk, j * 128, 128, ogT, "ogT", None)
                stt("dve", xT[:, ot, :], ps[bo][:, 0:L_P], mod[:, 1, 16 + ot, 0:1], xT[:, ot, :], ALU.mult, ALU.add,
                    [("ps", bo), ("mod", 1), "xT"], ["xT"])

        for c in range(2):
            for half in range(2):
                b = bank()
                for j in range(4):
                    ft = half * 4 + j
                    tr(ps[b][:, j * 128:(j + 1) * 128], xT[:, ft, c * 128:(c + 1) * 128], identf, ["xT", "identf"], [("ps", b)])
                cp("act", xin[:, c, half * 512:(half + 1) * 512], ps[b][:, :], [("ps", b)], ["xin"])
            act(scr8.rearrange("p a b -> p (a b)")[:, 0:1024], xin[:, c, :], AF.Square, ["xin"], ["scr8", "sm4"], accum=sm[:, 4:5])
            ts("dve", sm[:, 5:6], sm[:, 4:5], 1.0 / D, EPS, ALU.mult, ALU.add, ["sm4"], ["sm5"])
            act(sm[:, 5:6], sm[:, 5:6], AF.Sqrt, ["sm5"], ["sm5"])
            P.op("dve", lambda e: e.reciprocal(out=sm[:, 6:7], in_=sm[:, 5:6]), ["sm5"], ["sm6"])
            stt("dve", xin[:, c, :], xin[:, c, :], sm[:, 6:7], fnw_bc, ALU.mult, ALU.mult, ["xin", "sm6", "dsk2"], ["xin"])
        out_toks.append(dma("sp", o_yp[s].rearrange("(c p) d -> p c d", p=128), xin, ["xin"], [("o_yp", s)]))
        P.barrier()

    if DO_PROMPT:
        for s in range(NSEQ):
            seq_body(s)
    P.barrier()

    def sample_layer0(hg):
        w_in_es, conva_s_fm, convb_s_fm = w_in_es4[hg], conva_s_fm4[hg], convb_s_fm4[hg]
        ssd_par_s, dskip_s, st_in = ssd_par_s4[hg], dskip_s4[hg], st_in4[hg]
        cur[0] = prompt_mark
        onesb = sb("onesb", [128, 128], BF16)
        cp("dve", onesb, onesf, ["onesf"], ["onesb"])
        conva_s = sb("conva_s", [128, 4, 4])
        convb_s = sb("convb_s", [128, 6, 4])
        ssdp_s = sb("ssdp_s", [8, 2])
        aneg_s = sb("aneg_s", [8, 1])
        dsk_s = sb("dsk_s", [128, 256])
        dsk_s2 = sb("dsk_s2", [128, 256])
        dma("sp", conva_s, conva_s_fm, [], ["conva_s"])
        dma("sp", convb_s, convb_s_fm, [], ["convb_s"])
        dma("sp", ssdp_s, ssd_par_s, [], ["ssdp_s"])
        dma("sp", dsk_s, dskip_s[0].partition_broadcast(128), [], ["dsk_s"])
        dma("sp", dsk_s2, dskip_s[1].partition_broadcast(128), [], ["dsk_s2"])
        tt("dve", dsk_s, dsk_s, dsk_s2, ALU.add, ["dsk_s", "dsk_s2"], ["dsk_s"])
        act(aneg_s, ssdp_s[:, 1:2], AF.Exp, ["ssdp_s"], ["aneg_s"])
        ts("dve", aneg_s, aneg_s, -1.0, None, ALU.mult, None, ["aneg_s"], ["aneg_s"])
        uvb_s = sb("uvb_s", [128, 2, L_S], BF16)
        hy_mark = cur[0]
        xbcT_s = sb("xbcT_s", [128, 4, L_S], BF16)
        ph_mark = cur[0]
        w_s = sb("w_s", [128, 8, 1800], BF16)
        dma("pool", w_s, w_in_es.rearrange("(kc p) n -> p kc n", p=128), [], ["w_s"])
        xin_s = sb("xin_s", [128, 4, D])
        xT_w2 = [sb(f"xT_w{i}", [128, 8, 512]) for i in range(2)]
        sqb2 = [sb(f"sqb{i}", [128, 8, 512], BF16) for i in range(2)]
        hT_w2 = [sb(f"hT_w{i}", [128, 8, 512], BF16) for i in range(2)]
        rstd_w = sb("rstd_w", [128, 512])
        tmpw = [sb(f"tmpw{i}", [128, 512]) for i in range(2)]
        stg = [sb(f"stg{i}", [128, 512], BF16) for i in range(3)]
        x1w = sb("x1w", [128, 2, 512], BF16)
        vw = sb("vw", [128, 2, 512], BF16)
        dtst = sb("dtst", [8, 512])
        nst = [0]

        def conv_w(b, dst, cw, i, silu, key, n0, nv):
            tc_ = tmpw[i % 2]
            tk = ("tmpw", i % 2)
            ts("dve", tc_, ps[b][:, :], cw[:, i, 1:2], cw[:, i, 3:4], ALU.mult, ALU.add, [("ps", b), "conva_s", "convb_s"], [tk])
            stt("dve", tc_[:, 1:512], ps[b][:, 0:511], cw[:, i, 0:1], tc_[:, 1:512], ALU.mult, ALU.add, [("ps", b), tk], [tk])
            stt("dve", tc_[:, 0:511], ps[b][:, 1:512], cw[:, i, 2:3], tc_[:, 0:511], ALU.mult, ALU.add, [("ps", b), tk], [tk])
            if silu:
                act(dst, tc_[:, 1:1 + nv], AF.Silu, [tk], [key])
            else:
                cp("act", dst, tc_[:, 1:1 + nv], [tk], [key])

        def prep_w(k):
            xT_w, sqb, hT_w = xT_w2[k % 2], sqb2[k % 2], hT_w2[k % 2]
            kx, ks, kh = ("xT_w", k % 2), ("sqb", k % 2), ("hT_w", k % 2)
            t0 = 510 * k - 1
            n0 = 510 * k
            if hg > 0:
                dma("sp", hT_w, hTw_d[k].rearrange("a p t -> p a t"), [("hTw_d", k)], [kh])
                return
            for tt_ in range(4):
                lo = max(t0 + 128 * tt_, 0)
                hi = min(t0 + 128 * tt_ + 128, L_S)
                if hi - lo < 128:
                    memset(xin_s[:, tt_, :], 0.0, ["xin_s"])
                if hi > lo:
                    p0 = lo - (t0 + 128 * tt_)
                    dma("sp", xin_s[p0:p0 + (hi - lo), tt_, :], xs_in[lo:hi, :], [], ["xin_s"])
            for ft in range(8):
                b = bank()
                for tt_ in range(4):
                    tr(ps[b][:, tt_ * 128:(tt_ + 1) * 128], xin_s[:, tt_, ft * 128:(ft + 1) * 128], identf, ["xin_s", "identf"], [("ps", b)])
                cp("act", xT_w[:, ft, :], ps[b][:, :], [("ps", b)], [kx])
            act(sqb, xT_w, AF.Square, [kx], [ks])
            b = bank()
            for ft in range(8):
                mm(ps[b][:, :], onesb, sqb[:, ft, :], ft == 0, ft == 7, ["onesb", ks], [("ps", b)])
            ts("dve", rstd_w, ps[b][:, :], 1.0 / D, EPS, ALU.mult, ALU.add, [("ps", b)], ["rstd_w"])
            act(rstd_w, rstd_w, AF.Sqrt, ["rstd_w"], ["rstd_w"])
            P.op("dve", lambda e: e.reciprocal(out=rstd_w, in_=rstd_w), ["rstd_w"], ["rstd_w"])
            for ft in range(8):
                stt("dve", xT_w[:, ft, :], xT_w[:, ft, :], modA[:, 0, ft, 1:2], rstd_w, ALU.mult, ALU.mult,
                    [kx, ("modA", 0), "rstd_w"], [kx])
                act(hT_w[:, ft, :], xT_w[:, ft, :], AF.Identity, [kx, ("mod", 0)], [kh], bias=mod[:, 0, ft, 1:2], scale=1.0)
            if k == 0:
                memset(hT_w[:, :, 0:1], 0.0, [kh], eng="dve")
            if n0 + 511 > L_S:
                memset(hT_w[:, :, L_S - t0:512], 0.0, [kh], eng="dve")
            dma("sp", hTw_d[k].rearrange("a p t -> p a t"), hT_w, [kh], [("hTw_d", k)])

        def inproj_w(k):
            hT_w = hT_w2[k % 2]
            kh = ("hT_w", k % 2)
            n0 = 510 * k
            nv = min(510, L_S - n0)
            for i in range(14):
                b = bank()
                for kc in range(8):
                    mm(ps[b][:, :], w_s[:, kc, i * 128:(i + 1) * 128], hT_w[:, kc, :], kc == 0, kc == 7, ["w_s", kh], [("ps", b)])
                if i in (0, 1, 12, 13):
                    j = nst[0] % 3
                    nst[0] += 1
                    act(stg[j], ps[b][:, :], AF.Silu, [("ps", b)], [("stg", j)])
                    dst = (zs_d if i < 2 else gs_d)[i % 2 if i < 2 else i - 12, :, n0:n0 + nv]
                    dma("sp", dst, stg[j][:, 1:1 + nv], [("stg", j)], ["zs_d" if i < 2 else "gs_d"])
                elif i in (2, 3, 4, 5):
                    conv_w(b, xbcT_s[:, i - 2, n0:n0 + nv], conva_s, i - 2, True, "xbcT_s", n0, nv)
                elif i in (6, 7):
                    j = nst[0] % 3
                    nst[0] += 1
                    conv_w(b, stg[j][:, 1:1 + nv], convb_s, i - 6, False, ("stg", j), n0, nv)
                    dma("sp", x0_d[i - 6, :, n0:n0 + nv], stg[j][:, 1:1 + nv], [("stg", j)], ["x0_d"])
                elif i in (8, 9):
                    conv_w(b, x1w[:, i - 8, 0:nv], convb_s, i - 6, False, "x1w", n0, nv)
                else:
                    conv_w(b, vw[:, i - 10, 0:nv], convb_s, i - 6, False, "vw", n0, nv)
            tt("dve", uvb_s[:, :, n0:n0 + nv], vw[:, :, 0:nv], x1w[:, :, 0:nv], ALU.mult, ["vw", "x1w"], ["uvb_s"])
            b = bank()
            for kc in range(8):
                mm(ps[b][0:8, :], w_s[:, kc, 1792:1800], hT_w[:, kc, :], kc == 0, kc == 7, ["w_s", kh], [("ps", b)])
            cp("act", dtst, ps[b][0:8, :], [("ps", b)], ["dtst"])
            dma("sp", dt_d[:, n0:n0 + nv], dtst[:, 1:1 + nv], ["dtst"], ["dt_d"])
        prep_w(0)
        for k in range(NWIN):
            if k + 1 < NWIN:
                prep_w(k + 1)
            inproj_w(k)
        P.barrier()
        cur[0] = ph_mark

        NCH = L_S // 128
        x_tok_s = sb("x_tok_s", [128, NCH, 256], BF16)
        B_tok_s = sb("B_tok_s", [128, NCH, 128], BF16)
        GT_s = sb("GT_s", [128, NCH, 128])
        dt_tok_s = sb("dt_tok_s", [128, NCH, 8])
        adt_tok_s = sb("adt_tok_s", [128, NCH, 8])
        cs_tok_s = sb("cs_tok_s", [128, NCH, 8])
        ncs_tok_s = sb("ncs_tok_s", [128, NCH, 8])
        dec_tok_s = sb("dec_tok_s", [128, NCH, 8])
        ecs_tok_s = sb("ecs_tok_s", [128, NCH, 8])
        etot_s = sb("etot_s", [128, NCH, 8])
        dtdec_tok_s = sb("dtdec_tok_s", [128, NCH, 8])
        y_tok_s = sb("y_tok_s", [128, NCH, 256])
        dtp = sb("dtp", [8, 1024])
        adtp = sb("adtp", [8, 1024])
        csT_r = [sb(f"csT_r{i}", [8, 128]) for i in range(2)]
        xdt_r = [sb(f"xdt_r{i}", [128, 256], BF16) for i in range(2)]
        xdd_r = [sb(f"xdd_r{i}", [128, 256], BF16) for i in range(2)]
        t1s = sb("t1s", [128, 4, 128])
        t2s = sb("t2s", [128, 4, 128])
        MTs = [sb(f"MTs{i}", [128, 4, 128], BF16) for i in range(2)]
        ytmp_s = sb("ytmp_s", [128, 256])
        STs = [sb(f"STs{d}", [128, 256]) for d in range(2)]
        STbs = [sb(f"STbs{d}", [128, 256], BF16) for d in range(2)]
        stld = sb("stld", [128, 2, 128])
        for q4 in range(4):
            dma("sp", dtp, dt_d[:, q4 * 1024:(q4 + 1) * 1024], ["dt_d"], ["dtp"])
            act(dtp, dtp, AF.Exp, ["dtp", "ssdp_s"], ["dtp"], bias=ssdp_s[:, 0:1], scale=1.0)
            act(dtp, dtp, AF.Ln, ["dtp"], ["dtp"], bias=1.0, scale=1.0)
            ts("dve", adtp, dtp, aneg_s[:, 0:1], None, ALU.mult, None, ["dtp", "aneg_s"], ["adtp"])
            b = bank()
            for c8 in range(8):
                tr(ps[b][:, c8 * 8:(c8 + 1) * 8], dtp[:, c8 * 128:(c8 + 1) * 128], identf[0:8, 0:8], ["dtp", "identf"], [("ps", b)])
                tr(ps[b][:, 64 + c8 * 8:64 + (c8 + 1) * 8], adtp[:, c8 * 128:(c8 + 1) * 128], identf[0:8, 0:8], ["adtp", "identf"], [("ps", b)])
            cp("dve", dt_tok_s[:, q4 * 8:(q4 + 1) * 8, :], ps[b][:, 0:64].rearrange("p (c j) -> p c j", j=8), [("ps", b)], ["dt_tok_s"])
            cp("dve", adt_tok_s[:, q4 * 8:(q4 + 1) * 8, :], ps[b][:, 64:128].rearrange("p (c j) -> p c j", j=8), [("ps", b)], ["adt_tok_s"])
        adt_v = adt_tok_s.rearrange("p c (d j) -> p c d j", d=2)
        b = bank()
        b2 = bank()
        b3 = bank()
        mm(ps[b][:, 0:NCH * 4], triLE, adt_v[:, :, 0, :], True, True, ["triLE", "adt_tok_s"], [("ps", b)])
        mm(ps[b2][:, 0:NCH * 4], triGE, adt_v[:, :, 1, :], True, True, ["triGE", "adt_tok_s"], [("ps", b2)])
        mm(ps[b3][:, 0:NCH * 8], onesf, adt_tok_s, True, True, ["onesf", "adt_tok_s"], [("ps", b3)])
        cs_v = cs_tok_s.rearrange("p c (d j) -> p c d j", d=2)
        cp("dve", cs_v[:, :, 0, :], ps[b][:, 0:NCH * 4].rearrange("p (c j) -> p c j", j=4), [("ps", b)], ["cs_tok_s"])
        cp("dve", cs_v[:, :, 1, :], ps[b2][:, 0:NCH * 4].rearrange("p (c j) -> p c j", j=4), [("ps", b2)], ["cs_tok_s"])
        ts("dve", ncs_tok_s, cs_tok_s, -1.0, None, ALU.mult, None, ["cs_tok_s"], ["ncs_tok_s"])
        act(etot_s, ps[b3][:, 0:NCH * 8].rearrange("p (c j) -> p c j", j=8), AF.Exp, [("ps", b3)], ["etot_s"])
        tt("dve", dec_tok_s, ps[b3][:, 0:NCH * 8].rearrange("p (c j) -> p c j", j=8), cs_tok_s, ALU.subtract, [("ps", b3), "cs_tok_s"], ["dec_tok_s"])
        act(dec_tok_s, dec_tok_s, AF.Exp, ["dec_tok_s"], ["dec_tok_s"])
        act(ecs_tok_s, cs_tok_s, AF.Exp, ["cs_tok_s"], ["ecs_tok_s"])
        tt("dve", dtdec_tok_s, dt_tok_s, dec_tok_s, ALU.mult, ["dt_tok_s", "dec_tok_s"], ["dtdec_tok_s"])
        for c in range(NCH):
            b = bank()
            pb = psb(b)
            for j in range(3):
                tr(pb[:, j * 128:(j + 1) * 128], xbcT_s[:, j, c * 128:(c + 1) * 128], identb, ["xbcT_s", "identb"], [("ps", b)])
            cp("act", x_tok_s[:, c, :], pb[:, 0:256], [("ps", b)], ["x_tok_s"])
            cp("dve", B_tok_s[:, c, :], pb[:, 256:384], [("ps", b)], ["B_tok_s"])
            b = bank()
            mm(ps[b][:, 0:128], xbcT_s[:, 2, c * 128:(c + 1) * 128], xbcT_s[:, 3, c * 128:(c + 1) * 128], True, True, ["xbcT_s"], [("ps", b)])
            cp("act", GT_s[:, c, :], ps[b][:, 0:128], [("ps", b)], ["GT_s"])
        tt("dve", y_tok_s, x_tok_s, dsk_s.unsqueeze(1).to_broadcast([128, NCH, 256]), ALU.mult, ["x_tok_s", "dsk_s"], ["y_tok_s"])
        def load_state(d):
            dma("sp", stld, st_in[d].rearrange("(a p) n -> p a n", p=128), [], ["stld"])
            b = bank()
            for a2 in range(2):
                tr(ps[b][:, a2 * 128:(a2 + 1) * 128], stld[:, a2, :], identf, ["stld", "identf"], [("ps", b)])
            cp("dve", STs[d], ps[b][:, 0:256], [("ps", b)], [("STs", d)])
            cp("act", STbs[d], ps[b][:, 0:256], [("ps", b)], [("STbs", d)])

        def prep(d, c, k):
            mask = maskF if d == 0 else maskB
            b = bank()
            tr(ps[b][0:8, 0:128], cs_tok_s[:, c, :], identf, ["cs_tok_s", "identf"], [("ps", b)])
            cp("act", csT_r[k], ps[b][0:8, 0:128], [("ps", b)], [("csT_r", k)])
            tt("dve", xdt_r[k].rearrange("p (h q) -> p h q", q=64), x_tok_s[:, c, :].rearrange("p (h q) -> p h q", q=64),
               dt_tok_s[:, c, d * 4:(d + 1) * 4].unsqueeze(2).to_broadcast([128, 4, 64]), ALU.mult, ["x_tok_s", "dt_tok_s"], [("xdt_r", k)])
            tt("dve", xdd_r[k].rearrange("p (h q) -> p h q", q=64), x_tok_s[:, c, :].rearrange("p (h q) -> p h q", q=64),
               dtdec_tok_s[:, c, d * 4:(d + 1) * 4].unsqueeze(2).to_broadcast([128, 4, 64]), ALU.mult, ["x_tok_s", "dtdec_tok_s"], [("xdd_r", k)])
            b = bank()
            for j in range(4):
                mm(ps[b][:, j * 128:(j + 1) * 128], sel[0:8, d * 4 + j, :], csT_r[k], True, True, ["sel", ("csT_r", k)], [("ps", b)])
            tt("dve", t1s, ps[b][:, :].rearrange("p (j l) -> p j l", l=128), mask.unsqueeze(1).to_broadcast([128, 4, 128]),
               ALU.add, [("ps", b), "maskF", "maskB"], ["t1s"])
            for j in range(4):
                act(t2s[:, j, :], t1s[:, j, :], AF.Exp, ["t1s", "ncs_tok_s"], ["t2s"],
                    bias=ncs_tok_s[:, c, d * 4 + j:d * 4 + j + 1], scale=1.0)
            tt("dve", MTs[k], t2s, GT_s[:, c, :].unsqueeze(1).to_broadcast([128, 4, 128]), ALU.mult, ["t2s", "GT_s"], [("MTs", k)])

        def consume(d, c, k):
            by = bank()
            for j in range(4):
                mm(ps[by][:, j * 64:(j + 1) * 64], MTs[k][:, j, :], xdt_r[k][:, j * 64:(j + 1) * 64], True, True,
                   [("MTs", k), ("xdt_r", k)], [("ps", by)])
            mm(ps[by][:, 256:512], xbcT_s[:, 3, c * 128:(c + 1) * 128], STbs[d], True, True, ["xbcT_s", ("STbs", d)], [("ps", by)])
            tt("dve", ytmp_s.rearrange("p (h q) -> p h q", q=64), ps[by][:, 256:512].rearrange("p (h q) -> p h q", q=64),
               ecs_tok_s[:, c, d * 4:(d + 1) * 4].unsqueeze(2).to_broadcast([128, 4, 64]), ALU.mult, [("ps", by), "ecs_tok_s"], ["ytmp_s"])
            tt("dve", ytmp_s, ytmp_s, ps[by][:, 0:256], ALU.add, ["ytmp_s", ("ps", by)], ["ytmp_s"])
            tt("dve", y_tok_s[:, c, :], y_tok_s[:, c, :], ytmp_s, ALU.add, ["y_tok_s", "ytmp_s"], ["y_tok_s"])
            b = bank()
            mm(ps[b][:, 0:256], B_tok_s[:, c, :], xdd_r[k], True, True, ["B_tok_s", ("xdd_r", k)], [("ps", b)])
            tt("dve", STs[d].rearrange("p (h q) -> p h q", q=64), STs[d].rearrange("p (h q) -> p h q", q=64),
               etot_s[:, c, d * 4:(d + 1) * 4].unsqueeze(2).to_broadcast([128, 4, 64]), ALU.mult, [("STs", d), "etot_s"], [("STs", d)])
            tt("dve", STs[d], STs[d], ps[b][:, 0:256], ALU.add, [("STs", d), ("ps", b)], [("STs", d)])
            cp("act", STbs[d], STs[d], [("STs", d)], [("STbs", d)])

        iters = [(0, c) for c in range(NCH)] + [(1, c) for c in range(NCH - 1, -1, -1)]
        load_state(0)
        load_state(1)
        prep(iters[0][0], iters[0][1], 0)
        for i_, (d, c) in enumerate(iters):
            if i_ + 1 < len(iters):
                prep(iters[i_ + 1][0], iters[i_ + 1][1], (i_ + 1) % 2)
            consume(d, c, i_ % 2)
        zs_s = sb("zs_s", [128, 2, 512], BF16)
        ygs = sb("ygs", [128, 2, 512], BF16)
        ydbg = sb("ydbg", [128, 2, 512])
        for blk in range(8):
            dma("sp", zs_s, zs_d[:, :, blk * 512:(blk + 1) * 512].rearrange("a p t -> p a t"), ["zs_d"], ["zs_s"])
            for ft in range(2):
                b = bank()
                for c4 in range(4):
                    c = blk * 4 + c4
                    tr(ps[b][:, c4 * 128:(c4 + 1) * 128], y_tok_s[:, c, ft * 128:(ft + 1) * 128], identf, ["y_tok_s", "identf"], [("ps", b)])
                tt("dve", ygs[:, ft, :], ps[b][:, :], zs_s[:, ft, :], ALU.mult, [("ps", b), "zs_s"], ["ygs"])
                if STAGE_S == 1:
                    tt("dve", ydbg[:, ft, :], ps[b][:, :], zs_s[:, ft, :], ALU.mult, [("ps", b), "zs_s"], ["ydbg"])
            dma("sp", y_all[256 * hg:256 * hg + 256, blk * 512:(blk + 1) * 512].rearrange("(a p) t -> p a t", p=128), ygs, ["ygs"], ["y_all"])
            if STAGE_S == 1:
                out_toks.append(dma("sp", o_dbg[:, blk * 512:(blk + 1) * 512].rearrange("(a p) t -> p a t", p=128), ydbg, ["ydbg"], [("o_dbg", blk)]))
        P.barrier()
        return uvb_s, hy_mark


    def sample_hyena(uvb_s, ph_mark, hg):
        hfw3_s, hyp_s_fm = hfw3_s4[hg], hyp_s_fm4[hg]
        cur[0] = ph_mark
        n2 = 2 * L_S
        w1t = sb("w1t", [64, 128], BF16)
        w1i = sb("w1i", [128, 32], BF16)
        wtab = sb("wtab", [128, 3, 128], BF16)
        tw = sb("tw", [128, 2, 64])
        dma("pool", w1t, w1tab, [], ["w1t"])
        dma("pool", w1i, w1inv, [], ["w1i"])
        dma("pool", wtab, w128.rearrange("t p f -> p t f"), [], ["wtab"])
        dma("sp", tw, twid.rearrange("t p f -> p t f"), [], ["tw"])
        w1s_ = sb("w1s_", [33, 64])
        w2s_ = sb("w2s_", [64, 64])
        w3s_ = sb("w3s_", [64, 512], BF16)
        hp2 = sb("hp2", [64, 3])
        hsc2 = sb("hsc2", [64, 4])
        hyp = sb("hyp", [128, 2, 2])
        dma("sp", w1s_, hfw1, [], ["w1s_"])
        dma("sp", w2s_, hfw2, [], ["w2s_"])
        dma("pool", w3s_, hfw3_s, [], ["w3s_"])
        dma("sp", hp2, hfpar, [], ["hp2"])
        dma("sp", hyp, hyp_s_fm, [], ["hyp"])
        ts("dve", hsc2[:, 0:1], hp2[:, 2:3], 1.0 / TWO_PI, None, ALU.mult, None, ["hp2"], ["hsc2"])
        for j in range(2):
            tt("dve", hsc2[:, 1 + j:2 + j], hp2[:, j:j + 1], hsc2[:, 0:1], ALU.mult, ["hp2", "hsc2"], ["hsc2"])
        MAGIC = 12582912.0
        feb = sb("feb", [33, 512])
        tnb2 = sb("tnb2", [128, 512])
        hA = sb("hA", [64, 512])
        hB = sb("hB", [64, 512])
        kk2 = sb("kk2", [64, 512])
        dec2 = sb("dec2", [128, 512])
        kT_f = sb("kT_f", [128, n2], BF16)
        tnb2x = [tnb2, sb("tnb2b", [128, 512])]
        hBx = [sb("hBa", [64, 512], BF16), sb("hBb", [64, 512], BF16)]
        dec2x = [dec2, sb("dec2b", [128, 512])]
        lay = sb("lay", [128, 128 * 64], BF16)
        Gb = sb("Gb", [128, 64 * 128], BF16)
        Gp = sb("Gp", [128, 2, 4096], BF16)
        tq = [sb(f"tq{i}", [128, 1024]) for i in range(2)]
        Ksp = sb("Ksp", [128, 2, 4096], BF16)
        Xe = sb("Xe", [128, 2, 512])
        Ysp = sb("Ysp", [128, 2, 4096], BF16)
        y1 = Ksp.rearrange("p t n -> p (t n)")[0:32, :]
        ycv = sb("ycv", [64, L_S], BF16)
        x0h = sb("x0h", [64, L_S // 2], BF16)
        gsh = sb("gsh", [64, L_S // 2], BF16)
        ybh = sb("ybh", [64, L_S // 2], BF16)

        def sin_l(dst, src_w, src_x, j, keys):
            b = bank()
            mm(ps[b][0:64, :], src_w, src_x, True, True, keys, [("ps", b)])
            ts("dve", dst, ps[b][0:64, :], hsc2[:, 0:1], hsc2[:, 1 + j:2 + j], ALU.mult, ALU.add, [("ps", b), "hsc2"], ["hl2"])
            ts("dve", kk2, dst, MAGIC, None, ALU.add, None, ["hl2"], ["kk2"])
            ts("dve", kk2, kk2, -MAGIC, None, ALU.add, None, ["kk2"], ["kk2"])
            tt("dve", dst, dst, kk2, ALU.subtract, ["hl2", "kk2"], ["hl2"])
            ts("dve", dst, dst, TWO_PI, None, ALU.mult, None, ["hl2"], ["hl2"])
            ts("dve", dst, dst, math.pi, -math.pi, ALU.min, ALU.max, ["hl2"], ["hl2"])
            act(dst, dst, AF.Sin, ["hl2"], ["hA", "hB"])

        def fwd_half(srcT, hoff, NB, srckey):
            layv = lay[0:NB, :].rearrange("p (a c) -> p a c", c=64)
            for a0 in range(0, 128, 16):
                b = bank()
                pb = psb(b)
                for j in range(16):
                    a = a0 + j
                    tr(pb[0:NB, j * 64:(j + 1) * 64], srcT[hoff:hoff + 64, a:NB * 128:128], identb[hoff:hoff + 64, hoff:hoff + 64],
                       [srckey, "identb"], [("ps", b)])
                cp("act", layv[:, a0:a0 + 16, :], pb[0:NB, 0:1024].rearrange("p (a c) -> p a c", c=64),
                   [("ps", b)], ["lay"])
            Gv = Gb.rearrange("p (c f) -> p c f", f=128)
            for c0 in range(0, 64, 4):
                b = bank()
                for j in range(4):
                    mm(ps[b][:, j * 128:(j + 1) * 128], layv[:, :, c0 + j], w1t[0:NB, :], True, True, ["lay", "w1t"], [("ps", b)])
                cp("act", Gv[:, c0:c0 + 4, :], ps[b][:, :].rearrange("p (c f) -> p c f", f=128), [("ps", b)], ["Gb"])
            for hf in range(4):
                gre = Gv[:, hf * 16:(hf + 1) * 16, 0:64]
                gim = Gv[:, hf * 16:(hf + 1) * 16, 64:128]
                tre = tw[:, 0, :].unsqueeze(1).to_broadcast([128, 16, 64])
                tim = tw[:, 1, :].unsqueeze(1).to_broadcast([128, 16, 64])
                q0 = tq[0].rearrange("p (c f) -> p c f", f=64)
                q1 = tq[1].rearrange("p (c f) -> p c f", f=64)
                ore = Gp[:, 0, hf * 1024:(hf + 1) * 1024].rearrange("p (c f) -> p c f", f=64)
                oim = Gp[:, 1, hf * 1024:(hf + 1) * 1024].rearrange("p (c f) -> p c f", f=64)
                tt("dve", q0, gre, tre, ALU.mult, ["Gb", "tw"], [("tq", 0)])
                tt("dve", q1, gim, tim, ALU.mult, ["Gb", "tw"], [("tq", 1)])
                tt("dve", ore, q0, q1, ALU.subtract, [("tq", 0), ("tq", 1)], ["Gp"])
                tt("dve", q1, gre, tim, ALU.mult, ["Gb", "tw"], [("tq", 1)])
                tt("dve", q0, gim, tre, ALU.mult, ["Gb", "tw"], [("tq", 0)])
                tt("dve", oim, q0, q1, ALU.add, [("tq", 0), ("tq", 1)], ["Gp"])

        def stageB(ch, src, srckey, inverse):
            bre, bim = bank(), bank()
            sl = slice(ch * 512, (ch + 1) * 512)
            s_a, s_b = (2, 1) if inverse else (1, 2)
            mm(ps[bre][:, :], wtab[:, 0, :], src[:, 0, sl], True, False, ["wtab", srckey], [("ps", bre)])
            mm(ps[bre][:, :], wtab[:, s_a, :], src[:, 1, sl], False, True, ["wtab", srckey], [("ps", bre)])
            mm(ps[bim][:, :], wtab[:, 0, :], src[:, 1, sl], True, False, ["wtab", srckey], [("ps", bim)])
            mm(ps[bim][:, :], wtab[:, s_b, :], src[:, 0, sl], False, True, ["wtab", srckey], [("ps", bim)])
            return bre, bim

        for ct in range(2):
            for blk in range(n2 // 512):
                c0 = blk * 512
                kq = blk % 2
                tnb_, hB_, dec_ = tnb2x[kq], hBx[kq], dec2x[kq]
                dma("sp", tnb_, tnS[c0:c0 + 512].partition_broadcast(128), [], [("tnb2", kq)])
                if hg == 0 and ct == 0:
                    dma("sp", feb, featsS[:, c0:c0 + 512], [], ["feb"])
                    sin_l(hA, w1s_, feb, 0, ["w1s_", "feb"])
                    sin_l(hB, w2s_, hA, 1, ["w2s_", "hA"])
                    cp("act", hB_, hB, ["hA", "hB"], [("hBx", kq)])
                    dma("sp", h2_d[:, c0:c0 + 512], hB_, [("hBx", kq)], [("h2_d", blk)])
                else:
                    dma("sp", hB_, h2_d[:, c0:c0 + 512], [("h2_d", blk)], [("hBx", kq)])
                b = bank()
                wcol = (0 if c0 < L_S else 256) + ct * 128
                mm(ps[b][:, :], w3s_[:, wcol:wcol + 128], hB_, True, True, ["w3s_", "hB", ("hBx", kq)], [("ps", b)])
                act(dec_, tnb_, AF.Exp, [("tnb2", kq), "hyp"], [("dec2", kq)], scale=hyp[:, ct, 0:1])
                tt("dve", kT_f[:, c0:c0 + 512], ps[b][:, :], dec_, ALU.mult, [("ps", b), ("dec2", kq)], ["kT_f"])
            memset(kT_f[:, L_S:L_S + 1], 0.0, ["kT_f"], eng="dve")
            ts("dve", kT_f[:, 0:1], kT_f[:, 0:1], hyp[:, ct, 1:2], None, ALU.add, None, ["kT_f", "hyp"], ["kT_f"])
            for half in range(2):
                hoff = half * 64
                fwd_half(kT_f, hoff, 64, "kT_f")
                for ch in range(8):
                    bre, bim = stageB(ch, Gp, "Gp", False)
                    cp("act", Ksp[:, 0, ch * 512:(ch + 1) * 512], ps[bre][:, :], [("ps", bre)], ["Ksp"])
                    cp("dve", Ksp[:, 1, ch * 512:(ch + 1) * 512], ps[bim][:, :], [("ps", bim)], ["Ksp"])
                fwd_half(uvb_s[:, ct, :], hoff, 32, "uvb_s")
                for ch in range(8):
                    bre, bim = stageB(ch, Gp, "Gp", False)
                    sl = slice(ch * 512, (ch + 1) * 512)
                    cp("act", Xe[:, 0, :], ps[bre][:, :], [("ps", bre)], [("Xe", 0)])
                    cp("act", Xe[:, 1, :], ps[bim][:, :], [("ps", bim)], [("Xe", 1)])
                    qa = tq[0][:, 0:512]
                    qb_ = tq[1][:, 0:512]
                    tt("dve", qa, Xe[:, 0, :], Ksp[:, 0, sl], ALU.mult, [("Xe", 0), "Ksp"], [("tq", 0)])
                    tt("dve", qb_, Xe[:, 1, :], Ksp[:, 1, sl], ALU.mult, [("Xe", 1), "Ksp"], [("tq", 1)])
                    tt("dve", Ysp[:, 0, sl], qa, qb_, ALU.subtract, [("tq", 0), ("tq", 1)], ["Ysp"])
                    tt("dve", qb_, Xe[:, 0, :], Ksp[:, 1, sl], ALU.mult, [("Xe", 0), "Ksp"], [("tq", 1)])
                    tt("dve", qa, Xe[:, 1, :], Ksp[:, 0, sl], ALU.mult, [("Xe", 1), "Ksp"], [("tq", 0)])
                    tt("dve", Ysp[:, 1, sl], qa, qb_, ALU.add, [("tq", 0), ("tq", 1)], ["Ysp"])
                Vv = Gb.rearrange("p (c f) -> p c f", f=128)
                for ch in range(8):
                    bre, bim = stageB(ch, Ysp, "Ysp", True)
                    csl = slice(ch * 8, (ch + 1) * 8)
                    cp("act", Xe[:, 0, :], ps[bre][:, :], [("ps", bre)], [("Xe", 0)])
                    cp("act", Xe[:, 1, :], ps[bim][:, :], [("ps", bim)], [("Xe", 1)])
                    vre = Xe[:, 0, :].rearrange("p (c f) -> p c f", f=64)
                    vim = Xe[:, 1, :].rearrange("p (c f) -> p c f", f=64)
                    tre = tw[:, 0, :].unsqueeze(1).to_broadcast([128, 8, 64])
                    tim = tw[:, 1, :].unsqueeze(1).to_broadcast([128, 8, 64])
                    qa = tq[0][:, 0:512].rearrange("p (c f) -> p c f", f=64)
                    qb_ = tq[1][:, 0:512].rearrange("p (c f) -> p c f", f=64)
                    tt("dve", qa, vre, tre, ALU.mult, [("Xe", 0), "tw"], [("tq", 0)])
                    tt("dve", qb_, vim, tim, ALU.mult, [("Xe", 1), "tw"], [("tq", 1)])
                    tt("dve", Vv[:, csl, 0:64], qa, qb_, ALU.add, [("tq", 0), ("tq", 1)], ["Gb"])
                    tt("dve", qb_, vre, tim, ALU.mult, [("Xe", 0), "tw"], [("tq", 1)])
                    tt("dve", qa, vim, tre, ALU.mult, [("Xe", 1), "tw"], [("tq", 0)])
                    tt("dve", Vv[:, csl, 64:128], qa, qb_, ALU.subtract, [("tq", 0), ("tq", 1)], ["Gb"])
                Tl = lay.rearrange("p (c a) -> p c a", a=128)
                for c0 in range(0, 64, 8):
                    b = bank()
                    pb = psb(b)
                    for j in range(8):
                        tr(pb[:, j * 128:(j + 1) * 128], Vv[:, c0 + j, :], identb, ["Gb", "identb"], [("ps", b)])
                    cp("act", Tl[:, c0:c0 + 8, :], pb[:, 0:1024].rearrange("p (c a) -> p c a", a=128),
                       [("ps", b)], ["lay"])
                ycv3 = ycv.rearrange("p (i a) -> p i a", a=128)
                for a0 in range(0, 128, 16):
                    b = bank()
                    for j in range(16):
                        mm(ps[b][0:64, j * 32:(j + 1) * 32], Tl[:, :, a0 + j], w1i, True, True, ["lay", "w1i"], [("ps", b)])
                    cp("act", ycv3[:, :, a0:a0 + 16], ps[b][0:64, 0:512].rearrange("p (a i) -> p i a", i=32), [("ps", b)], ["ycv"])
                r0 = ct * 128 + hoff
                for tq_ in range(2):
                    tsl = slice(tq_ * 2048, (tq_ + 1) * 2048)
                    dma("sp", x0h, x0_d[ct, hoff:hoff + 64, tsl], ["x0_d"], ["x0h"])
                    dma("sp", gsh, gs_d[ct, hoff:hoff + 64, tsl], ["gs_d"], ["gsh"])
                    tt("dve", ybh, ycv[:, tsl], x0h, ALU.mult, ["ycv", "x0h"], ["ybh"])
                    tt("dve", ybh, ybh, gsh, ALU.mult, ["ybh", "gsh"], ["ybh"])
                    dma("sp", y_all[D + 256 * hg + r0:D + 256 * hg + r0 + 64, tsl], ybh, ["ybh"], ["y_all"])
                    if STAGE_S == 2:
                        out_toks.append(dma("sp", o_dbg.bitcast(BF16)[r0:r0 + 64, tsl], ybh, ["ybh"], [("o_dbg", r0, tq_)]))

        P.barrier()


    def sample_tail():
        P.barrier()
        cur[0] = prompt_mark
        NE = 2048
        onesb2 = sb("onesb2", [128, 128], BF16)
        cp("dve", onesb2, onesf, ["onesf"], ["onesb2"])
        bm = sb("bm", [128, 16])
        dma("sp", bm, blkmask.partition_broadcast(128), [], ["bm"])
        hT1x = sb("hT1x", [128, 8, NE], BF16)
        ogT_own = sb("ogT_own", [128, 8, 1024], BF16)
        t1_mark = cur[0]
        wo = sb("wo", [128, 16, D], BF16)
        for half in range(2):
            dma("pool", wo[:, half * 8:(half + 1) * 8, :], w_out_e[half * 1024:(half + 1) * 1024, :].rearrange("(kc p) n -> p kc n", p=128),
                [], [("wo", half)])
        if y_all_dbg is not None:
            stgd = sb("stgd", [128, L_S], BF16)
            for rt in range(16):
                dma("pool", stgd, y_all_dbg[rt * 128:(rt + 1) * 128, :], [], ["stgd"])
                dma("sp", y_all[rt * 128:(rt + 1) * 128, :], stgd, ["stgd"], ["y_all"])
        cand = [sb(f"cand{i}", [128, 4, 512], BF16) for i in range(2)]
        ygx = sb("ygx", [128, 16, 512], BF16)
        xin_t = [sb(f"xin_t{i}", [128, D]) for i in range(2)]
        xT_b = sb("xT_b", [128, 8, 512])
        sq_b = sb("sq_b", [128, 8, 512], BF16)
        rstd_b = sb("rstd_b", [128, 512])
        rstdy_b = sb("rstdy_b", [128, 512])
        tmp_b = [sb(f"tmp_b{i}", [128, 512]) for i in range(2)]
        for kb in range(4):
            piece = 0 if kb == 0 else (2 if kb == 3 else 1)
            off = 512 if kb in (0, 2) else 0
            for rt in range(16):
                cd = cand[rt % 2]
                ck = ("cand", rt % 2)
                dma("sp", cd, y_all[rt * 128:(rt + 1) * 128, :].rearrange("p (j t) -> p j t", t=1024)[:, :, off:off + 512], ["y_all"], [ck])
                for jj in range(4):
                    m = bm[:, piece * 4 + jj:piece * 4 + jj + 1]
                    if jj == 0:
                        ts("dve", ygx[:, rt, :], cd[:, jj, :], m, None, ALU.mult, None, [ck, "bm"], ["ygx"])
                    else:
                        stt("dve", ygx[:, rt, :], cd[:, jj, :], m, ygx[:, rt, :], ALU.mult, ALU.add, [ck, "bm", "ygx"], ["ygx"])
            for ft in range(8):
                pass
            bks = [bank() for _ in range(8)]
            for tt_ in range(4):
                xt = xin_t[tt_ % 2]
                xk = ("xin_t", tt_ % 2)
                dma("sp", xt, x_ext[kb * 512 + tt_ * 128:kb * 512 + (tt_ + 1) * 128, :], [], [xk])
                for ft in range(8):
                    tr(ps[bks[ft]][:, tt_ * 128:(tt_ + 1) * 128], xt[:, ft * 128:(ft + 1) * 128], identf, [xk, "identf"], [("ps", bks[ft])])
            for ft in range(8):
                cp("act", xT_b[:, ft, :], ps[bks[ft]][:, :], [("ps", bks[ft])], ["xT_b"])
            act(sq_b, ygx[:, 0:8, :], AF.Square, ["ygx"], ["sq_b"])
            b = bank()
            for ft in range(8):
                mm(ps[b][:, :], onesb2, sq_b[:, ft, :], ft == 0, ft == 7, ["onesb2", "sq_b"], [("ps", b)])
            ts("dve", rstdy_b, ps[b][:, :], 1.0 / D, EPS, ALU.mult, ALU.add, [("ps", b)], ["rstdy_b"])
            act(rstdy_b, rstdy_b, AF.Sqrt, ["rstdy_b"], ["rstdy_b"])
            P.op("dve", lambda e: e.reciprocal(out=rstdy_b, in_=rstdy_b), ["rstdy_b"], ["rstdy_b"])
            for ft in range(8):
                act(ygx[:, ft, :], ygx[:, ft, :], AF.Copy, ["ygx", "naw"], ["ygx"], scale=naw[:, ft:ft + 1])
            for ot in range(8):
                ba, bb = bank(), bank()
                for kc in range(8):
                    mm(ps[ba][:, :], wo[:, kc, ot * 128:(ot + 1) * 128], ygx[:, kc, :], kc == 0, kc == 7, [("wo", 0), "ygx"], [("ps", ba)])
                for kc in range(8):
                    mm(ps[bb][:, :], wo[:, 8 + kc, ot * 128:(ot + 1) * 128], ygx[:, 8 + kc, :], kc == 0, kc == 7, [("wo", 1), "ygx"], [("ps", bb)])
                tb = tmp_b[ot % 2]
                tk = ("tmp_b", ot % 2)
                tt("dve", tb, ps[ba][:, :], rstdy_b, ALU.mult, [("ps", ba), "rstdy_b"], [tk])
                tt("dve", tb, tb, ps[bb][:, :], ALU.add, [tk, ("ps", bb)], [tk])
                stt("dve", xT_b[:, ot, :], tb, mod[:, 0, 16 + ot, 1:2], xT_b[:, ot, :], ALU.mult, ALU.add, [tk, ("mod", 0), "xT_b"], ["xT_b"])
            dma("sp", x1_d[:, :, kb * 512:(kb + 1) * 512].rearrange("a p t -> p a t"), xT_b, ["xT_b"], ["x1_d"])
            act(sq_b, xT_b, AF.Square, ["xT_b"], ["sq_b"])
            b = bank()
            for ft in range(8):
                mm(ps[b][:, :], onesb2, sq_b[:, ft, :], ft == 0, ft == 7, ["onesb2", "sq_b"], [("ps", b)])
            ts("dve", rstd_b, ps[b][:, :], 1.0 / D, EPS, ALU.mult, ALU.add, [("ps", b)], ["rstd_b"])
            act(rstd_b, rstd_b, AF.Sqrt, ["rstd_b"], ["rstd_b"])
            P.op("dve", lambda e: e.reciprocal(out=rstd_b, in_=rstd_b), ["rstd_b"], ["rstd_b"])
            for ft in range(8):
                stt("dve", xT_b[:, ft, :], xT_b[:, ft, :], modA[:, 1, ft, 1:2], rstd_b, ALU.mult, ALU.mult,
                    ["xT_b", ("modA", 1), "rstd_b"], ["xT_b"])
                act(hT1x[:, ft, kb * 512:(kb + 1) * 512], xT_b[:, ft, :], AF.Identity, ["xT_b", ("mod", 1)], ["hT1x"],
                    bias=mod[:, 1, ft, 1:2], scale=1.0)
        P.barrier()
        cur[0] = t1_mark
        wp = [sb(f"wp{i}", [128, 8, 512], BF16) for i in range(2)]
        qT_p = sb("qT_p", [128, NE], BF16)
        kT_p = sb("kT_p", [128, NE], BF16)
        vT_p = sb("vT_p", [128, NE], BF16)
        gs_p = sb("gs_p", [128, NE], BF16)
        v_tok_p = sb("v_tok_p", [128, 16, 128], BF16)
        ckT_p = sb("ckT_p", [128, 256], BF16)
        cv_p = sb("cv_p", [128, 2, 128], BF16)
        strip = sb("strip", [128, 19 * 64])
        rm = sb("rm", [16, 18 * 64], BF16)
        dma("pool", rm, rm_in, [], ["rm"])
        selb = sb("selb", [16, 16, 128], BF16)
        cp("dve", selb, sel[0:16, 0:16, :], ["sel"], ["selb"])
        qbd = [sb(f"qbd{i}", [128, 128], BF16) for i in range(2)]
        sc = [sb(f"sc{i}", [128, 1408]) for i in range(2)]
        Pn = [sb(f"Pn{i}", [128, 1408], BF16) for i in range(2)]
        PTn = [sb(f"PTn{i}", [128, 11, 128], BF16) for i in range(2)]
        sms = sb("sms", [128, 8])
        for i in range(2):
            memset(qbd[i], 0.0, [("qbd", i)])
        itn = [0]
        for hp2 in range(8):
            w_ = wp[hp2 % 2]
            wk_ = ("wp", hp2 % 2)
            dma("pool", w_, w_in_o_pairs[hp2].rearrange("(kc p) n -> p kc n", p=128), [], [wk_])
            dma("pool", ckT_p, ckT_in[hp2], [], ["ckT_p"])
            dma("pool", cv_p, cv_in[hp2], [], ["cv_p"])
            dma("sp", strip, strip_in[hp2], [], ["strip"])
            for seg, dst_, key_ in ((0, qT_p, "qT_p"), (1, kT_p, "kT_p"), (2, vT_p, "vT_p"), (3, gs_p, "gs_p")):
                for kb in ((1, 2) if seg in (0, 3) else range(4)):
                    b = bank()
                    for kc in range(8):
                        mm(ps[b][:, :], w_[:, kc, seg * 128:(seg + 1) * 128], hT1x[:, kc, kb * 512:(kb + 1) * 512], kc == 0, kc == 7,
                           [wk_, "hT1x"], [("ps", b)])
                    if seg == 3:
                        act(dst_[:, kb * 512:(kb + 1) * 512], ps[b][:, :], AF.Silu, [("ps", b)], [key_])
                    else:
                        cp("act", dst_[:, kb * 512:(kb + 1) * 512], ps[b][:, :], [("ps", b)], [key_])
            for t4 in range(4):
                b = bank()
                pb = psb(b)
                for j in range(4):
                    tl = t4 * 4 + j
                    tr(pb[:, j * 128:(j + 1) * 128], vT_p[:, tl * 128:(tl + 1) * 128], identb, ["vT_p", "identb"], [("ps", b)])
                cp("act", v_tok_p[:, t4 * 4:(t4 + 1) * 4, :].rearrange("p a d -> p (a d)"), pb[:, 0:512], [("ps", b)], ["v_tok_p"])
            def S1(rho, k):
                q0 = 512 + 64 * rho
                nt = 9 if rho % 2 == 0 else 8
                st_ = rho // 2 if rho % 2 == 0 else (rho + 1) // 2
                nk = nt * 128
                cp("dve", qbd[k][0:64, 0:64], qT_p[0:64, q0:q0 + 64], ["qT_p"], [("qbd", k)])
                cp("dve", qbd[k][64:128, 64:128], qT_p[64:128, q0:q0 + 64], ["qT_p"], [("qbd", k)])
                banks = []
                for c0 in range(0, nk, 512):
                    cn = min(512, nk - c0)
                    b = bank()
                    banks.append((b, c0, cn))
                    mm(ps[b][:, 0:cn], qbd[k], kT_p[:, st_ * 128 + c0:st_ * 128 + c0 + cn], True, False, [("qbd", k), "kT_p"], [("ps", b)])
                    mm(ps[b][:, 0:cn], selb[:, rho, :], rm[:, c0:c0 + cn], False, True, ["selb", "rm"], [("ps", b)])
                bc = bank()
                mm(ps[bc][:, 0:256], qbd[k], ckT_p, True, True, [("qbd", k), "ckT_p"], [("ps", bc)])
                return banks, bc

            def S2(rho, k, banks, bc):
                nt = 9 if rho % 2 == 0 else 8
                nk = nt * 128
                srow0 = 0 if rho % 2 == 0 else 1
                for (b, c0, cn) in banks:
                    stt("dve", sc[k][:, c0:c0 + cn], ps[b][:, 0:cn], 0.125, strip[:, srow0 * 64 + c0:srow0 * 64 + c0 + cn], ALU.mult, ALU.add,
                        [("ps", b), "strip"], [("sc", k)])
                act(sc[k][:, nk:nk + 256], ps[bc][:, 0:256], AF.Copy, [("ps", bc)], [("sc", k)], scale=0.125)
                tot = nk + 256
                P.op("dve", lambda e, k=k, tot=tot: e.tensor_reduce(out=sms[:, 0:1], in_=sc[k][:, 0:tot], axis=AX.X, op=ALU.max, negate=True),
                     [("sc", k)], ["sms0"])
                act(Pn[k][:, 0:tot], sc[k][:, 0:tot], AF.Exp, [("sc", k), "sms0"], [("Pn", k), "sms2"], bias=sms[:, 0:1], scale=1.0, accum=sms[:, 2:3])
                P.op("dve", lambda e: e.reciprocal(out=sms[:, 3:4], in_=sms[:, 2:3]), ["sms2"], ["sms3"])
                ts("dve", Pn[k][:, 0:tot], Pn[k][:, 0:tot], sms[:, 3:4], None, ALU.mult, None, [("Pn", k), "sms3"], [("Pn", k)])

            def S3(rho, k):
                q0 = 512 + 64 * rho
                nt = 9 if rho % 2 == 0 else 8
                st_ = rho // 2 if rho % 2 == 0 else (rho + 1) // 2
                ntt = nt + 2
                for g0 in range(0, ntt, 8):
                    gn = min(8, ntt - g0)
                    b = bank()
                    pb = psb(b)
                    for j in range(gn):
                        tr(pb[:, j * 128:(j + 1) * 128], Pn[k][:, (g0 + j) * 128:(g0 + j + 1) * 128], identb, [("Pn", k), "identb"], [("ps", b)])
                    cp("act", PTn[k][:, g0:g0 + gn, :].rearrange("p a q -> p (a q)"), pb[:, 0:gn * 128], [("ps", b)], [("PTn", k)])
                bo = bank()
                for j in range(ntt):
                    lhs = v_tok_p[:, st_ + j, :] if j < nt else cv_p[:, j - nt, :]
                    mm(ps[bo][:, 0:128], lhs, PTn[k][:, j, :], j == 0, j == ntt - 1, ["v_tok_p", "cv_p", ("PTn", k)], [("ps", bo)])
                tt("dve", ogT_own[0:64, hp2, 64 * rho:64 * rho + 64], ps[bo][0:64, 0:64], gs_p[0:64, q0:q0 + 64], ALU.mult,
                   [("ps", bo), "gs_p"], ["ogT_own"])
                tt("dve", ogT_own[64:128, hp2, 64 * rho:64 * rho + 64], ps[bo][64:128, 64:128], gs_p[64:128, q0:q0 + 64], ALU.mult,
                   [("ps", bo), "gs_p"], ["ogT_own"])

            nxt = S1(0, itn[0] % 2)
            for rho in range(16):
                k = itn[0] % 2
                itn[0] += 1
                S2(rho, k, *nxt)
                if rho + 1 < 16:
                    nxt = S1(rho + 1, itn[0] % 2)
                S3(rho, k)
        P.barrier()
        cur[0] = t1_mark
        woo = sb("woo", [128, 8, D], BF16)
        dma("pool", woo, w_out_o.rearrange("(kc p) n -> p kc n", p=128), [], ["woo"])
        x1o = sb("x1o", [128, 8, 512])
        ytk = sb("ytk", [128, 4, D])
        sq3 = sb("sq3", [128, D])
        sms3 = sb("sms3", [128, 8])
        for kb in range(2):
            dma("sp", x1o, x1_d[:, :, 512 + kb * 512:512 + (kb + 1) * 512].rearrange("a p t -> p a t"), ["x1_d"], ["x1o"])
            for ot in range(8):
                b = bank()
                for kc in range(8):
                    mm(ps[b][:, :], woo[:, kc, ot * 128:(ot + 1) * 128], ogT_own[:, kc, kb * 512:(kb + 1) * 512], kc == 0, kc == 7,
                       ["woo", "ogT_own"], [("ps", b)])
                stt("dve", x1o[:, ot, :], ps[b][:, :], mod[:, 1, 16 + ot, 1:2], x1o[:, ot, :], ALU.mult, ALU.add,
                    [("ps", b), ("mod", 1), "x1o"], ["x1o"])
            for c in range(4):
                for half in range(2):
                    b = bank()
                    for j in range(4):
                        ft = half * 4 + j
                        tr(ps[b][:, j * 128:(j + 1) * 128], x1o[:, ft, c * 128:(c + 1) * 128], identf, ["x1o", "identf"], [("ps", b)])
                    cp("act", ytk[:, c, half * 512:(half + 1) * 512], ps[b][:, :], [("ps", b)], ["ytk"])
                act(sq3, ytk[:, c, :], AF.Square, ["ytk"], ["sq3", "sms4"], accum=sms3[:, 4:5])
                ts("dve", sms3[:, 5:6], sms3[:, 4:5], 1.0 / D, EPS, ALU.mult, ALU.add, ["sms4"], ["sms5"])
                act(sms3[:, 5:6], sms3[:, 5:6], AF.Sqrt, ["sms5"], ["sms5"])
                P.op("dve", lambda e: e.reciprocal(out=sms3[:, 6:7], in_=sms3[:, 5:6]), ["sms5"], ["sms6"])
                stt("dve", ytk[:, c, :], ytk[:, c, :], sms3[:, 6:7], fnw_bc, ALU.mult, ALU.mult, ["ytk", "sms6", "dsk2"], ["ytk"])
            out_toks.append(dma("sp", o_ys[kb * 512:(kb + 1) * 512, :].rearrange("(c p) d -> p c d", p=128), ytk, ["ytk"], [("o_ys", kb)]))

    NHG = 4
    if DO_SAMPLE:
        for hg in range(NHG):
            uvb_s_, phm_ = sample_layer0(hg)
            if STAGE_S >= 2:
                sample_hyena(uvb_s_, phm_, hg)
        if STAGE_S >= 3:
            sample_tail()

    P.finish("sp", out_toks)
    P.emit()
    return nc


def kernel(**inp):
    f32 = np.float32
    inp = {k: np.asarray(v) for k, v in inp.items()}
    nc = build_nc()
    sel = np.zeros((32, 32, 128), f32)
    for j in range(32):
        sel[j, j, :] = 1.0
    conv_a = np.concatenate([inp["conv_a_w"][0], inp["conv_a_b"][0][None]], 0)
    conv_a_fm = np.ascontiguousarray(conv_a.reshape(4, 12, 128).transpose(2, 1, 0))
    conv_b = np.concatenate([inp["conv_b_w"][0], inp["conv_b_b"][0][None]], 0)
    conv_b_fm = np.ascontiguousarray(conv_b.reshape(4, 24, 128).transpose(2, 1, 0))
    ssd_par = np.ascontiguousarray(np.stack([inp["dt_bias"][0].reshape(32), inp["a_log"][0].reshape(32)], 1))
    dskip_rep = np.ascontiguousarray(np.repeat(inp["d_skip"][0], 64, axis=1))
    featsP, tnP = hyena_feats(L_P)
    deltas = np.linspace(math.log(1e-2) / 0.3, math.log(1e-2) / 1.5, D, dtype=f32)
    ndelta_fm = fm(-np.abs(deltas), 8)
    n = 2 * L_P
    tt_ = np.arange(n, dtype=np.float64)
    ang = 2.0 * np.pi * np.outer(tt_, tt_) / n
    dftP = np.stack([np.cos(ang), -np.sin(ang), np.cos(ang) / n, -np.sin(ang) / n]).astype(f32)
    hfpar = np.ascontiguousarray(np.stack([inp["hf_b1"][0], inp["hf_b2"][0], inp["hf_freq"][0]], 1))
    featsS, tnS = hyena_feats(L_S)
    ii = np.arange(64, dtype=np.float64)
    w1tab = np.concatenate([np.cos(2 * np.pi * np.outer(ii, ii) / 64), -np.sin(2 * np.pi * np.outer(ii, ii) / 64)], 1).astype(f32)
    i32 = np.arange(32, dtype=np.float64)
    w1inv = np.concatenate([np.cos(2 * np.pi * np.outer(ii, i32) / 64), -np.sin(2 * np.pi * np.outer(ii, i32) / 64)], 0).astype(np.float64)
    w1inv = (w1inv / (2 * L_S)).astype(f32)
    aa = np.arange(128, dtype=np.float64)
    th = 2 * np.pi * np.outer(aa, aa) / 128
    w128 = np.stack([np.cos(th), np.sin(th), -np.sin(th)]).astype(f32)
    tht = 2 * np.pi * np.outer(aa, ii) / (2 * L_S)
    twid = np.stack([np.cos(tht), -np.sin(tht)]).astype(f32)
    WE = inp["w_in_e"][0]
    W3 = inp["hf_w3"][0]

    def hg_slices(g):
        g2 = g // 2
        dtc = [2560 + d_ * 16 + 4 * g + j for d_ in range(2) for j in range(4)]
        r256 = np.arange(256 * g, 256 * g + 256)
        r128 = np.arange(128 * g2, 128 * g2 + 128)
        cols = np.concatenate([r256, 1024 + r256, 2048 + r128, 2304 + r128, 2592 + r256, 3616 + r256, 4640 + r256, 5664 + r256,
                               np.array(dtc)])
        ca_cols = np.concatenate([r256, 1024 + r128, 1280 + r128])
        cb_cols = np.concatenate([r256, 1024 + r256, 2048 + r256])
        return dict(
            w_in_es=WE[:, cols],
            conva_s_fm=conv_a[:, ca_cols].reshape(4, 4, 128).transpose(2, 1, 0),
            convb_s_fm=conv_b[:, cb_cols].reshape(4, 6, 128).transpose(2, 1, 0),
            ssd_par_s=np.stack([inp["dt_bias"][0][:, 4 * g:4 * g + 4].reshape(8), inp["a_log"][0][:, 4 * g:4 * g + 4].reshape(8)], 1),
            dskip_s=np.repeat(inp["d_skip"][0][:, 4 * g:4 * g + 4], 64, axis=1),
            hfw3_s=np.concatenate([W3[:, r256], W3[:, 1024 + r256]], 1),
            hyp_s_fm=np.stack([fm(-np.abs(deltas)[r256], 2), fm(inp["hy_bias"][0][r256], 2)], -1),
        )
    hgs = [hg_slices(g) for g in range(4)]
    hg_in = {k: np.ascontiguousarray(np.stack([h[k] for h in hgs]).astype(f32)) for k in hgs[0]}
    WO = inp["w_in_o"][0]
    w_in_o_pairs = np.ascontiguousarray(np.stack([
        np.concatenate([WO[:, seg * 1024 + hp * 128:seg * 1024 + hp * 128 + 128] for seg in range(4)], 1) for hp in range(8)]))
    rpb = inp["rpb"][0]
    qc_ = np.arange(64)[:, None]
    kc_ = np.arange(64)[None, :]
    col0 = np.clip(qc_ - 8, 0, 48)
    col_in = (kc_ >= col0) & (kc_ < col0 + 16)
    dcidx = np.clip(kc_ - qc_ + 15, 0, 30)
    strip_all = np.full((16, 64, 19, 64), NEG, f32)
    for sr in range(15):
        strip_all[:, :, sr + 1, :] = np.where(col_in[None], rpb[:, sr][:, dcidx], f32(NEG))
    strip_in = np.ascontiguousarray(strip_all.reshape(8, 128, 19 * 64))
    in_maps = []
    for core in range(8):
        b = core // 4
        g = core % 4
        j_ = g
        xe = np.zeros((2048, D), f32)
        lo, hi = 1024 * j_ - 512, 1024 * j_ + 1536
        slo, shi = max(lo, 0), min(hi, L_S)
        xe[slo - lo:shi - lo] = inp["x_sample"][b][slo:shi]
        bmk = np.zeros((16,), f32)
        if j_ - 1 >= 0:
            bmk[0 + j_ - 1] = 1.0
        bmk[4 + j_] = 1.0
        if j_ + 1 <= 3:
            bmk[8 + j_ + 1] = 1.0
        rm_ = np.full((16, 18, 64), NEG, f32)
        for rho in range(16):
            r = 16 * j_ + rho
            row0 = min(max(r - 4, 0), 56)
            for ip in range(18):
                sr = ip - 1 if rho % 2 == 0 else ip
                kr = r + sr - 7
                if 0 <= sr <= 14 and row0 <= kr < row0 + 8:
                    rm_[rho, ip, :] = 0.0
        ck = inp["cache_k"][b, 0]
        cvv = inp["cache_v"][b, 0]
        sample_maps = {
            "xs_in": np.ascontiguousarray(inp["x_sample"][b]),
            "st_in": np.ascontiguousarray(np.stack([inp["state_ssd"][b, 0, :, 4 * gg:4 * gg + 4].reshape(2, 256, 128) for gg in range(4)])),
            "featsS": featsS, "tnS": tnS,
            "w1tab": w1tab, "w1inv": w1inv, "w128": w128, "twid": twid,
            "x_ext": xe, "blkmask": bmk, "w_in_o_pairs": w_in_o_pairs,
            "ckT_in": np.ascontiguousarray(ck.reshape(8, 2, 256, 64).transpose(0, 1, 3, 2).reshape(8, 128, 256)),
            "cv_in": np.ascontiguousarray(cvv.reshape(8, 2, 2, 128, 64).transpose(0, 3, 2, 1, 4).reshape(8, 128, 2, 128)),
            "strip_in": strip_in, "rm_in": np.ascontiguousarray(rm_.reshape(16, 18 * 64)),
        }
        sample_maps.update(hg_in)
        m = {
            "xp": np.ascontiguousarray(inp["x_prompt"][core * NSEQ:(core + 1) * NSEQ]),
            "cv": np.ascontiguousarray(np.stack([fm(inp["c_ctx"], 8), fm(inp["c"][b], 8)], -1)),
            "w_ada": inp["w_ada"],
            "b_ada_fm": np.ascontiguousarray(np.stack([fm(inp["b_ada"][l], 24) for l in range(2)])),
            "norm_w_fm": np.ascontiguousarray(np.stack([fm(inp["norm_w"][l], 8) for l in range(2)])),
            "w_in_e": inp["w_in_e"][0],
            "w_out_e": inp["w_out_e"][0],
            "w_in_o": inp["w_in_o"][0],
            "w_out_o": inp["w_out_o"][0],
            "conv_a_fm": conv_a_fm,
            "conv_b_fm": conv_b_fm,
            "ssd_par": ssd_par,
            "sel_c": sel.reshape(32, 32 * 128),
            "dskip_rep": dskip_rep,
            "naw_fm": fm(inp["norm_a_w"][0], 8),
            "hyb_fm": fm(inp["hy_bias"][0], 8),
            "fnw": inp["final_norm_w"],
            "featsP": featsP, "tnP": tnP,
            "hfw1": inp["hf_w1"][0], "hfw2": inp["hf_w2"][0], "hfw3": inp["hf_w3"][0], "hfpar": hfpar,
            "ndelta_fm": ndelta_fm, "dftP": dftP,
        }
        m.update(sample_maps)
        in_maps.append(m)
    res = run_bass_kernel_spmd(nc, in_maps, core_ids=list(range(8)))
    st = np.concatenate([r["o_state"] for r in res.results], 0)
    new_state = st.reshape(32, 1, 2, 16, 64, 128).astype(f32)
    y_prompt = np.concatenate([r["o_yp"] for r in res.results], 0).astype(f32)
    new_k = np.concatenate([r["o_k"] for r in res.results], 0).reshape(32, 1, 16, 256, 64).astype(f32)
    new_v = np.concatenate([r["o_v"] for r in res.results], 0).reshape(32, 1, 16, 256, 64).astype(f32)
    y_sample = np.stack([np.concatenate([res.results[4 * b_ + j]["o_ys"] for j in range(4)], 0) for b_ in range(2)]).astype(f32)
    return (y_prompt, y_sample, new_state, new_k, new_v)
```

```python
import contextlib
import math
import numpy as np
import concourse.bass as bass
import concourse.mybir as mybir
from concourse.bass_utils import run_bass_kernel_spmd

F32 = mybir.dt.float32
BF16 = mybir.dt.bfloat16
AF = mybir.ActivationFunctionType
ALU = mybir.AluOpType
AX = mybir.AxisListType

D = 1024
L_P = 256
NSEQ = 4
EPS = 1e-6
NEG = -30000.0
TWO_PI = 2.0 * math.pi
DO_PROMPT = True
DO_SAMPLE = True
STAGE_S = 99
L_S = 4096
NWIN = 9


class Prog:
    R = 8

    def __init__(self, nc):
        self.nc = nc
        self.q = {e: [] for e in ("pe", "act", "dve", "pool", "sp")}
        self.cnt = {e: 0 for e in self.q}
        self.dma_cnt = {e: 0 for e in self.q}
        self.lastw = {}
        self.readers = {}
        self.seen = {e: {} for e in self.q}
        self.nbank = 0
        self.last_tok = {e: [] for e in self.q}
        self.snaps = {}

    def _need(self, eng, tok, waits):
        if tok is None:
            return
        semkey, val, teng = tok
        if teng == eng and eng == "pe":
            return
        cur = self.seen[eng].get(semkey, 0)
        if cur >= val:
            return
        self.seen[eng][semkey] = val
        waits.append((semkey, val))
        se = self.seen[eng]
        for sk2, v2 in self.snaps.get((semkey, val), ()):
            if se.get(sk2, 0) < v2:
                se[sk2] = v2

    def op(self, eng, fn, reads=(), writes=(), dma=False):
        writes = list(writes) + [r for r in reads if isinstance(r, tuple) and r[0] == "ps" and r not in writes]
        waits = []
        for r in reads:
            self._need(eng, self.lastw.get(r), waits)
        for w in writes:
            self._need(eng, self.lastw.get(w), waits)
            for t in self.readers.get(w, ()):
                self._need(eng, t, waits)
        if dma:
            j = self.dma_cnt[eng]
            self.dma_cnt[eng] += 1
            R = 1 if eng == "pool" else self.R
            semkey = ("dma", eng, j % R)
            val = 16 * (j // R + 1)
            if j >= R:
                self._need(eng, (semkey, val - 16, "dma"), waits)
            tok = (semkey, val, "dma")
            self.last_tok[eng] = (self.last_tok[eng] + [tok])[-R:]
        else:
            self.cnt[eng] += 1
            semkey = ("eng", eng)
            tok = (semkey, self.cnt[eng], eng)
        self.q[eng].append((fn, waits, semkey, dma))
        self.snaps[(tok[0], tok[1])] = tuple(self.seen[eng].items())
        for w in writes:
            self.lastw[w] = tok
            self.readers[w] = []
        for r in reads:
            if r not in writes:
                self.readers.setdefault(r, []).append(tok)
        return tok

    def barrier(self):
        toks = []
        for e in self.q:
            if e in ("pe", "act", "dve", "pool") and self.cnt[e]:
                toks.append((("eng", e), self.cnt[e], e))
            toks += self.last_tok[e]
        for e in self.q:
            waits = []
            for t in toks:
                self._need(e, t, waits)
            if waits:
                self.q[e].append((None, waits, None, False))

    def finish(self, eng, toks):
        waits = []
        for t in toks:
            self._need(eng, t, waits)
        self.q[eng].append((None, waits, None, False))

    def emit(self):
        nc = self.nc
        semkeys = set()
        for e, lst in self.q.items():
            for fn, waits, semkey, dma in lst:
                if semkey is not None:
                    semkeys.add(semkey)
                for (sk, v) in waits:
                    semkeys.add(sk)
        sems = {}
        with contextlib.ExitStack() as st:
            for sk in sorted(semkeys, key=str):
                sems[sk] = st.enter_context(nc.semaphore("s_" + "_".join(str(x) for x in sk)))
            block = st.enter_context(nc.Block())
            engmap = {"pe": block.tensor, "act": block.scalar, "dve": block.vector,
                      "pool": block.gpsimd, "sp": block.sync}

            def make(e):
                lst = self.q[e]

                def body(eng):
                    for fn, waits, semkey, dma in lst:
                        fuse = (fn is not None) and (not dma) and len(waits) > 0 and e != "pool"
                        for (sk, v) in (waits[:-1] if fuse else waits):
                            eng.wait_ge(sems[sk], v)
                        if fn is None:
                            continue
                        n0 = nc.n_instructions()
                        ins = fn(eng)
                        if fuse:
                            assert nc.n_instructions() - n0 == 1, ("multi-instruction op cannot carry a fused wait", e)
                            ins._wait_ge(sems[waits[-1][0]], waits[-1][1])
                        ins.then_inc(sems[semkey], 16 if dma else 1)
                return body
            for e in self.q:
                if self.q[e]:
                    engmap[e](make(e))


def fm(v, nt):
    return np.ascontiguousarray(np.asarray(v, np.float32).reshape(nt, 128).T)


def hyena_feats(L):
    f32 = np.float32
    t = np.linspace(0.0, 1.0, L, dtype=f32)[:, None]
    w = (f32(2.0 * math.pi) * np.arange(L, dtype=f32)[:, None] / f32(L)).astype(f32)
    f = np.linspace(1e-4, 15, 16, dtype=f32)[None]
    fw = (f * w).astype(f32)
    feats = np.concatenate([t, np.cos(fw), -np.sin(fw)], -1).astype(f32)
    allf = np.zeros((2 * L, 33), f32)
    allt = np.zeros((2 * L,), f32)
    allf[:L] = feats
    allt[:L] = t[:, 0]
    for j in range(L + 1, 2 * L):
        allf[j] = feats[2 * L - j]
        allt[j] = t[2 * L - j, 0]
    return np.ascontiguousarray(allf.T), allt


def build_nc():
    nc = bass.Bass("TRN2", target_bir_lowering=False)
    P = Prog(nc)

    def din(name, shape, dt=F32):
        return nc.dram_tensor(name, list(shape), dt, kind="ExternalInput").ap()

    def dout(name, shape, dt=F32):
        return nc.dram_tensor(name, list(shape), dt, kind="ExternalOutput").ap()

    ARENA_BYTES = 207 * 1024
    arena = nc.alloc_sbuf_tensor("arena", [128, ARENA_BYTES // 4], F32)
    cur = [0]

    def sb(name, shape, dt=F32):
        item = 4 if dt == F32 else 2
        n = 1
        for s_ in shape[1:]:
            n *= s_
        nbytes = (n * item + 31) // 32 * 32
        off = cur[0]
        cur[0] += nbytes
        assert cur[0] <= ARENA_BYTES, (name, cur[0])
        v = arena[:, off // 4:(off + nbytes) // 4]
        if dt != F32:
            v = v.bitcast(dt)
        v = v[0:shape[0], 0:n]
        if len(shape) == 3:
            v = v.rearrange("p (a b) -> p a b", b=shape[2])
        elif len(shape) == 4:
            v = v.rearrange("p (a b c) -> p a b c", b=shape[2], c=shape[3])
        return v

    def mm(out, lhsT, rhs, start, stop, reads, writes):
        P.op("pe", lambda e: e.matmul(out, lhsT=lhsT, rhs=rhs, start=start, stop=stop), reads, writes)

    def tr(out, in_, ident, reads, writes):
        P.op("pe", lambda e: e.transpose(out=out, in_=in_, identity=ident), reads, writes)

    def act(out, in_, func, reads, writes, bias=None, scale=None, accum=None):
        kw = {}
        if bias is not None:
            kw["bias"] = bias
        if scale is not None:
            kw["scale"] = scale
        if accum is not None:
            kw["accum_out"] = accum
        P.op("act", lambda e: e.activation(out=out, in_=in_, func=func, **kw), reads, writes)

    def tt(eng, out, in0, in1, op, reads, writes):
        P.op(eng, lambda e: e.tensor_tensor(out=out, in0=in0, in1=in1, op=op), reads, writes)

    def ts(eng, out, in0, s1, s2, op0, op1, reads, writes):
        if s2 is None:
            P.op(eng, lambda e: e.tensor_scalar(out=out, in0=in0, scalar1=s1, scalar2=None, op0=op0), reads, writes)
        else:
            P.op(eng, lambda e: e.tensor_scalar(out=out, in0=in0, scalar1=s1, scalar2=s2, op0=op0, op1=op1), reads, writes)

    def stt(eng, out, in0, scalar, in1, op0, op1, reads, writes):
        P.op(eng, lambda e: e.scalar_tensor_tensor(out=out, in0=in0, scalar=scalar, in1=in1, op0=op0, op1=op1), reads, writes)

    def cp(eng, out, in_, reads, writes):
        if eng == "act":
            P.op("act", lambda e: e.activation(out=out, in_=in_, func=AF.Copy), reads, writes)
        else:
            P.op(eng, lambda e: e.tensor_copy(out=out, in_=in_), reads, writes)

    def dma(eng, out, in_, reads, writes):
        return P.op(eng, lambda e: e.dma_start(out=out, in_=in_), reads, writes, dma=True)

    def memset(out, val, writes, eng="pool"):
        P.op(eng, lambda e: e.memset(out, val), (), writes)

    xp = din("xp", [NSEQ, L_P, D])
    cv = din("cv", [128, 8, 2])
    w_ada = din("w_ada", [2, D, 3 * D])
    b_ada_fm = din("b_ada_fm", [2, 128, 24])
    norm_w_fm = din("norm_w_fm", [2, 128, 8])
    w_in_e = din("w_in_e", [D, 6688])
    w_out_e = din("w_out_e", [2 * D, D])
    w_in_o = din("w_in_o", [D, 4 * D])
    w_out_o = din("w_out_o", [D, D])
    conv_a_fm = din("conv_a_fm", [128, 12, 4])
    conv_b_fm = din("conv_b_fm", [128, 24, 4])
    ssd_par = din("ssd_par", [32, 2])
    sel_c = din("sel_c", [32, 32 * 128])
    dskip_rep = din("dskip_rep", [2, D])
    naw_fm = din("naw_fm", [128, 8])
    hyb_fm = din("hyb_fm", [128, 8])
    fnw = din("fnw", [D])
    featsP = din("featsP", [33, 2 * L_P])
    tnP = din("tnP", [2 * L_P])
    hfw1 = din("hfw1", [33, 64])
    hfw2 = din("hfw2", [64, 64])
    hfw3 = din("hfw3", [64, 2 * D])
    hfpar = din("hfpar", [64, 3])
    ndelta_fm = din("ndelta_fm", [128, 8])
    dftP = din("dftP", [4, 512, 512])

    xs_in = din("xs_in", [L_S, D])
    w_in_es4 = din("w_in_es", [4, D, 1800])
    conva_s_fm4 = din("conva_s_fm", [4, 128, 4, 4])
    convb_s_fm4 = din("convb_s_fm", [4, 128, 6, 4])
    ssd_par_s4 = din("ssd_par_s", [4, 8, 2])
    dskip_s4 = din("dskip_s", [4, 2, 256])
    st_in4 = din("st_in", [4, 2, 256, 128])
    featsS = din("featsS", [33, 2 * L_S])
    tnS = din("tnS", [2 * L_S])
    hfw3_s4 = din("hfw3_s", [4, 64, 512])
    hyp_s_fm4 = din("hyp_s_fm", [4, 128, 2, 2])
    w1tab = din("w1tab", [64, 128])
    w1inv = din("w1inv", [128, 32])
    w128 = din("w128", [3, 128, 128])
    twid = din("twid", [2, 128, 64])
    x_ext = din("x_ext", [2048, D])
    blkmask = din("blkmask", [16])
    w_in_o_pairs = din("w_in_o_pairs", [8, D, 512])
    ckT_in = din("ckT_in", [8, 128, 256])
    cv_in = din("cv_in", [8, 128, 2, 128])
    strip_in = din("strip_in", [8, 128, 19 * 64])
    rm_in = din("rm_in", [16, 18 * 64])
    y_all_dbg = None
    x1_d = nc.dram_tensor("x1_d", [8, 128, 2048], F32, kind="Internal").ap()
    h2_d = nc.dram_tensor("h2_d", [64, 2 * L_S], F32, kind="Internal").ap()
    hTw_d = nc.dram_tensor("hTw_d", [NWIN, 8, 128, 512], BF16, kind="Internal").ap()
    o_ys = dout("o_ys", [1024, D])
    zs_d = nc.dram_tensor("zs_d", [2, 128, L_S], BF16, kind="Internal").ap()
    gs_d = nc.dram_tensor("gs_d", [2, 128, L_S], BF16, kind="Internal").ap()
    x0_d = nc.dram_tensor("x0_d", [2, 128, L_S], BF16, kind="Internal").ap()
    dt_d = nc.dram_tensor("dt_d", [8, L_S], F32, kind="Internal").ap()
    y_all = nc.dram_tensor("y_all", [2 * D, L_S], BF16, kind="Internal").ap()
    o_dbg = None

    o_state = dout("o_state", [NSEQ, 2, 1024, 128])
    o_yp = dout("o_yp", [NSEQ, L_P, D])
    o_k = dout("o_k", [NSEQ, 16, L_P, 64])
    o_v = dout("o_v", [NSEQ, 16, L_P, 64])
    out_toks = []

    ps = [nc.alloc_psum_tensor(f"ps{i}", [128, 512], F32) for i in range(8)]

    def bank():
        i = P.nbank % 8
        P.nbank += 1
        return i

    def psb(b):
        return ps[b][:].bitcast(BF16)

    identf = sb("identf", [128, 128])
    identb = sb("identb", [128, 128], BF16)
    onesf = sb("onesf", [128, 128])
    maskF = sb("maskF", [128, 128])
    maskB = sb("maskB", [128, 128])
    triLE = sb("triLE", [128, 128])
    triGE = sb("triGE", [128, 128])
    zer = sb("zer", [128, 128])
    memset(zer, 0.0, ["zer"])
    memset(onesf, 1.0, ["onesf"])

    def aff(out, in_, pattern, cmp_op, fill, cm, reads, writes):
        P.op("pool", lambda e: e.affine_select(out=out, in_=in_, pattern=pattern, compare_op=cmp_op, fill=fill, base=0,
                                               channel_multiplier=cm), reads, writes)
    aff(identf, zer, [[-1, 128]], ALU.not_equal, 1.0, 1, ["zer"], ["identf"])
    aff(maskF, zer, [[1, 128]], ALU.is_ge, NEG, -1, ["zer"], ["maskF"])
    aff(maskB, zer, [[-1, 128]], ALU.is_ge, NEG, 1, ["zer"], ["maskB"])
    aff(triLE, onesf, [[1, 128]], ALU.is_ge, 0.0, -1, ["onesf"], ["triLE"])
    aff(triGE, onesf, [[-1, 128]], ALU.is_ge, 0.0, 1, ["onesf"], ["triGE"])
    cp("dve", identb, identf, ["identf"], ["identb"])

    sel = sb("sel", [32, 32, 128])
    dma("sp", sel, sel_c.rearrange("k (j m) -> k j m", m=128), [], ["sel"])
    conva = sb("conva", [128, 12, 4])
    dma("sp", conva, conv_a_fm, [], ["conva"])
    convb = sb("convb", [128, 24, 4])
    dma("sp", convb, conv_b_fm, [], ["convb"])
    ssdp = sb("ssdp", [32, 2])
    dma("sp", ssdp, ssd_par, [], ["ssdp"])
    aneg = sb("aneg", [32, 1])
    act(aneg, ssdp[:, 1:2], AF.Exp, ["ssdp"], ["aneg"])
    ts("dve", aneg, aneg, -1.0, None, ALU.mult, None, ["aneg"], ["aneg"])
    naw = sb("naw", [128, 8])
    dma("sp", naw, naw_fm, [], ["naw"])
    hyb = sb("hyb", [128, 8])
    dma("sp", hyb, hyb_fm, [], ["hyb"])
    dsk = sb("dsk", [128, D])
    dsk2 = sb("dsk2", [128, D])
    dma("sp", dsk, dskip_rep[0].partition_broadcast(128), [], ["dsk"])
    dma("sp", dsk2, dskip_rep[1].partition_broadcast(128), [], ["dsk2"])
    tt("dve", dsk, dsk, dsk2, ALU.add, ["dsk", "dsk2"], ["dsk"])
    fnw_bc = dsk2
    dma("sp", fnw_bc, fnw.partition_broadcast(128), ["dsk"], ["dsk2"])

    cvt = sb("cvt", [128, 8, 2])
    cvs = sb("cvs", [128, 8, 2], BF16)
    dma("sp", cvt, cv, [], ["cvt"])
    act(cvs, cvt, AF.Silu, ["cvt"], ["cvs"])
    bada = sb("bada", [128, 2, 24])
    nw = sb("nw", [128, 2, 8])
    dma("sp", bada, b_ada_fm.rearrange("l p j -> p l j"), [], ["bada"])
    dma("sp", nw, norm_w_fm.rearrange("l p j -> p l j"), [], ["nw"])
    mod = sb("mod", [128, 2, 24, 2])
    modA = sb("modA", [128, 2, 8, 2])
    persist_mark = cur[0]
    wa = sb("wa", [128, 8, 3 * D], BF16)
    for l in range(2):
        for half in range(2):
            dma("pool", wa[:, half * 4:(half + 1) * 4, :],
                w_ada[l, half * 512:(half + 1) * 512, :].rearrange("(kc p) n -> p kc n", p=128), [], [("wa", half)])
        b = bank()
        for j in range(24):
            for kc in range(8):
                mm(ps[b][:, j * 2:(j + 1) * 2], wa[:, kc, j * 128:(j + 1) * 128], cvs[:, kc, :], kc == 0, kc == 7,
                   [("wa", kc // 4), "cvs"], [("ps", b)])
        tt("dve", mod[:, l], ps[b][:, 0:48].rearrange("p (j c) -> p j c", c=2),
           bada[:, l, :].unsqueeze(2).to_broadcast([128, 24, 2]), ALU.add, [("ps", b), "bada"], [("mod", l)])
        stt("dve", modA[:, l], mod[:, l, 8:16, :], 1.0, nw[:, l, :].unsqueeze(2).to_broadcast([128, 8, 2]),
            ALU.add, ALU.mult, [("mod", l), "nw"], [("modA", l)])
    P.barrier()
    cur[0] = persist_mark

    prompt_mark = cur[0]
    dft = sb("dft", [128, 4, 4, 512], BF16)
    for t_ in range(4):
        dma("pool", dft[:, t_], dftP[t_].rearrange("(c p) n -> p c n", p=128), [], [("dft", t_)])
    Kre = sb("Kre", [128, 4, 1024], BF16)
    Kim = sb("Kim", [128, 4, 1024], BF16)
    filt_mark = cur[0]
    if True:
        n2 = 2 * L_P
        fe = sb("fe", [33, n2])
        tnb = sb("tnb", [128, n2])
        w1s = sb("w1s", [33, 64])
        w2s = sb("w2s", [64, 64])
        w3s = sb("w3s", [64, 2 * D])
        hp_ = sb("hp_", [64, 3])
        hsc = sb("hsc", [64, 4])
        ndl = sb("ndl", [128, 8])
        h1 = sb("h1", [64, n2])
        h2 = sb("h2", [64, n2])
        kT = sb("kT", [128, 8, n2], BF16)
        dec = sb("dec", [128, n2])
        ktok = sb("ktok", [128, 4, 1024], BF16)
        dma("sp", fe, featsP, [], ["fe"])
        dma("sp", tnb, tnP.partition_broadcast(128), [], ["tnb"])
        dma("sp", w1s, hfw1, [], ["w1s"])
        dma("sp", w2s, hfw2, [], ["w2s"])
        dma("sp", w3s, hfw3, [], ["w3s"])
        dma("sp", hp_, hfpar, [], ["hp_"])
        dma("sp", ndl, ndelta_fm, [], ["ndl"])
        ts("dve", hsc[:, 0:1], hp_[:, 2:3], 1.0 / TWO_PI, None, ALU.mult, None, ["hp_"], ["hsc"])
        for j in range(2):
            tt("dve", hsc[:, 1 + j:2 + j], hp_[:, j:j + 1], hsc[:, 0:1], ALU.mult, ["hp_", "hsc"], ["hsc"])

        MAGIC = 12582912.0
        kk = sb("kk", [64, n2])

        def sin_layer(dst, src_w, src_x, j):
            b = bank()
            mm(ps[b][0:64, 0:n2], src_w, src_x, True, True, ["w1s", "w2s", "fe", "h1"], [("ps", b)])
            ts("dve", dst, ps[b][0:64, 0:n2], hsc[:, 0:1], hsc[:, 1 + j:2 + j], ALU.mult, ALU.add, [("ps", b), "hsc"], ["hl"])
            ts("dve", kk, dst, MAGIC, None, ALU.add, None, ["hl"], ["kk"])
            ts("dve", kk, kk, -MAGIC, None, ALU.add, None, ["kk"], ["kk"])
            tt("dve", dst, dst, kk, ALU.subtract, ["hl", "kk"], ["hl"])
            ts("dve", dst, dst, TWO_PI, None, ALU.mult, None, ["hl"], ["hl"])
            ts("dve", dst, dst, math.pi, -math.pi, ALU.min, ALU.max, ["hl"], ["hl"])
            act(dst, dst, AF.Sin, ["hl"], ["h1", "h2"])
        sin_layer(h1, w1s, fe, 0)
        sin_layer(h2, w2s, h1, 1)
        for ct in range(8):
            b = bank()
            mm(ps[b][:, 0:L_P], w3s[:, ct * 128:(ct + 1) * 128], h2[:, 0:L_P], True, True, ["w3s", "h2"], [("ps", b)])
            mm(ps[b][:, L_P:n2], w3s[:, D + ct * 128:D + (ct + 1) * 128], h2[:, L_P:n2], True, True, ["w3s", "h2"], [("ps", b)])
            act(dec, tnb, AF.Exp, ["tnb", "ndl"], ["dec"], scale=ndl[:, ct:ct + 1])
            tt("dve", kT[:, ct, :], ps[b][:, 0:n2], dec, ALU.mult, [("ps", b), "dec"], ["kT"])
        memset(kT[:, :, L_P:L_P + 1], 0.0, ["kT"], eng="dve")
        for dc in range(4):
            for half in range(2):
                b = bank()
                for j in range(4):
                    ct = half * 4 + j
                    tr(psb(b)[:, j * 128:(j + 1) * 128], kT[:, ct, dc * 128:(dc + 1) * 128], identb, ["kT", "identb"], [("ps", b)])
                cp("act", ktok[:, dc, half * 512:(half + 1) * 512], psb(b)[:, 0:512], [("ps", b)], ["ktok"])
        for ft in range(4):
            for half in range(2):
                for ri in range(2):
                    b = bank()
                    for dc in range(4):
                        mm(ps[b][:, :], dft[:, ri, dc, ft * 128:(ft + 1) * 128], ktok[:, dc, half * 512:(half + 1) * 512],
                           dc == 0, dc == 3, [("dft", ri), "ktok"], [("ps", b)])
                    cp("act", (Kim if ri else Kre)[:, ft, half * 512:(half + 1) * 512], ps[b][:, :],
                       [("ps", b)], ["Kim" if ri else "Kre"])
    P.barrier()
    cur[0] = filt_mark

    xin = sb("xin", [128, 2, D])
    xT = sb("xT", [128, 8, L_P])
    scr8 = sb("scr8", [128, 8, L_P])
    rstd = sb("rstd", [128, L_P])
    hT = sb("hT", [128, 8, L_P], BF16)
    wbuf = [sb(f"wbuf{i}", [128, 8, 512], BF16) for i in range(2)]
    tmpc = [sb(f"tmpc{i}", [128, L_P]) for i in range(2)]
    sm = sb("sm", [128, 8])
    l0_mark = cur[0]
    xbcT = sb("xbcT", [128, 12, L_P], BF16)
    dtraw = sb("dtraw", [32, L_P])
    dtT = sb("dtT", [32, L_P])
    adtT = sb("adtT", [32, L_P])
    dt_tok = sb("dt_tok", [128, 2, 32])
    adt_tok = sb("adt_tok", [128, 2, 32])
    cs_tok = sb("cs_tok", [128, 2, 32])
    ncs_tok = sb("ncs_tok", [128, 2, 32])
    dec_tok = sb("dec_tok", [128, 2, 32])
    ecs_tok = sb("ecs_tok", [128, 2, 32])
    etot = sb("etot", [128, 2, 32])
    dtdec_tok = sb("dtdec_tok", [128, 2, 32])
    csT = sb("csT", [32, 2, 128])
    x_tok = sb("x_tok", [128, 2, 1024], BF16)
    B_tok = sb("B_tok", [128, 2, 2, 128], BF16)
    xdt = sb("xdt", [128, 2, 2, 1024], BF16)
    xdd = sb("xdd", [128, 2, 2, 1024], BF16)
    GT = sb("GT", [128, 2, 2, 128])
    t1 = [sb(f"t1_{i}", [128, 4, 128]) for i in range(1)]
    t2 = [sb(f"t2_{i}", [128, 4, 128]) for i in range(1)]
    MT = [sb(f"MT_{i}", [128, 4, 128], BF16) for i in range(2)]
    y_tok = sb("y_tok", [128, 2, 1024])
    ytmp = sb("ytmp", [128, 512])
    ST = [sb(f"ST{d}", [128, 1024]) for d in range(2)]
    STb = [sb(f"STb{d}", [128, 1024], BF16) for d in range(2)]
    stout = xin.rearrange("p c d -> p (c d)")[:, 0:1024].rearrange("p (a n) -> p a n", n=128)
    ssd_end = cur[0]
    u16 = sb("u16", [128, 24, L_P], BF16)
    zs = sb("zs", [128, 8, L_P], BF16)
    gs = sb("gs", [128, 8, L_P], BF16)
    ygT = sb("ygT", [128, 8, L_P], BF16)
    ybT = sb("ybT", [128, 8, L_P], BF16)
    rstdy = sb("rstdy", [128, L_P])
    l0_end = cur[0]
    cur[0] = l0_mark
    uvb = sb("uvb", [128, 8, L_P], BF16)
    uv_tok = sb("uv_tok", [128, 2, 1024], BF16)
    Xs = [sb(f"Xs{i}", [128, 512]) for i in range(2)]
    ta = [sb(f"ta{i}", [128, 512]) for i in range(2)]
    Yre = sb("Yre", [128, 4, 1024], BF16)
    Yim = sb("Yim", [128, 4, 1024], BF16)
    assert cur[0] <= ssd_end
    cur[0] = l0_mark
    qT = sb("qT", [128, 8, L_P], BF16)
    QBD = sb("QBD", [128, 8, 4, 128], BF16)
    kTf = sb("kTf", [128, 8, L_P])
    vTf = sb("vTf", [128, 8, L_P])
    kTb = sb("kTb", [128, 8, L_P], BF16)
    v_tok = sb("v_tok", [128, 2, 1024], BF16)
    gs1 = sb("gs1", [128, 8, L_P], BF16)
    ogT = sb("ogT", [128, 8, L_P], BF16)
    Pm = [sb(f"Pm{i}", [128, L_P], BF16) for i in range(2)]
    PT = [sb(f"PT{i}", [128, 2, 128], BF16) for i in range(2)]
    assert cur[0] <= l0_end
    cur[0] = l0_end
    print("SBUF bytes used per partition:", cur[0])

    gcount = [0]
    gseq = [0, 0]
    wsc = nc.dram_tensor("wsc", [28, 128, 8, 512], BF16, kind="Internal").ap()

    def load_w(wsrc, row0, col0, ncols):
        g = gcount[0]
        gcount[0] += 1
        idx = gseq[0]
        gseq[0] += 1
        wb = wbuf[g % 2]
        key = ("wbuf", g % 2)
        if gseq[1] == 0:
            dma("pool", wb[:, :, 0:ncols], wsrc[row0:row0 + 1024, col0:col0 + ncols].rearrange("(kc p) n -> p kc n", p=128),
                [], [key])
            dma("sp", wsc[idx][:, :, 0:ncols], wb[:, :, 0:ncols], [key], [("wsc", idx)])
        else:
            dma("sp", wb[:, :, 0:ncols], wsc[idx][:, :, 0:ncols], [("wsc", idx)], [key])
        return wb, key

    def proj_tile(wb, wkey, off, m, src, srckey, evac, b=None):
        if b is None:
            b = bank()
        for kc in range(8):
            mm(ps[b][0:m, 0:L_P], wb[:, kc, off:off + m], src[:, kc, :], kc == 0, kc == 7, [wkey, srckey], [("ps", b)])
        if evac is not None:
            evac(b)
        return b

    def norm_mod(layer):
        act(scr8, xT, AF.Square, ["xT"], ["scr8"])
        b = bank()
        for ft in range(8):
            mm(ps[b][:, 0:L_P], onesf, scr8[:, ft, :], ft == 0, ft == 7, ["onesf", "scr8"], [("ps", b)])
        ts("dve", rstd, ps[b][:, 0:L_P], 1.0 / D, EPS, ALU.mult, ALU.add, [("ps", b)], ["rstd"])
        act(rstd, rstd, AF.Sqrt, ["rstd"], ["rstd"])
        P.op("dve", lambda e: e.reciprocal(out=rstd, in_=rstd), ["rstd"], ["rstd"])
        for ft in range(8):
            stt("dve", scr8[:, ft, :], xT[:, ft, :], modA[:, layer, ft, 0:1], rstd, ALU.mult, ALU.mult,
                ["xT", ("modA", layer), "rstd"], ["scr8"])
            act(hT[:, ft, :], scr8[:, ft, :], AF.Identity, ["scr8", ("mod", layer)], ["hT"], bias=mod[:, layer, ft, 0:1], scale=1.0)

    def conv_evac(b, dst, cw, i, silu, key):
        tc_ = tmpc[i % 2]
        tk = ("tmpc", i % 2)
        ts("dve", tc_, ps[b][:, 0:L_P], cw[:, i, 1:2], cw[:, i, 3:4], ALU.mult, ALU.add, [("ps", b), "conva", "convb"], [tk])
        stt("dve", tc_[:, 1:L_P], ps[b][:, 0:L_P - 1], cw[:, i, 0:1], tc_[:, 1:L_P], ALU.mult, ALU.add, [("ps", b), tk], [tk])
        stt("dve", tc_[:, 0:L_P - 1], ps[b][:, 1:L_P], cw[:, i, 2:3], tc_[:, 0:L_P - 1], ALU.mult, ALU.add, [("ps", b), tk], [tk])
        if silu:
            act(dst, tc_, AF.Silu, [tk], [key])
        else:
            cp("act", dst, tc_, [tk], [key])

    def seq_body(s):
        gseq[0] = 0
        gseq[1] = s
        dma("sp", xin, xp[s].rearrange("(tt p) d -> p tt d", p=128), [], ["xin"])
        for ft in range(8):
            b = bank()
            for tt_ in range(2):
                tr(ps[b][:, tt_ * 128:(tt_ + 1) * 128], xin[:, tt_, ft * 128:(ft + 1) * 128], identf, ["xin", "identf"], [("ps", b)])
            cp("act", xT[:, ft, :], ps[b][:, 0:L_P], [("ps", b)], ["xT"])
        norm_mod(0)

        for gi in range(2):
            wb, wk = load_w(w_in_e, 0, gi * 512, 512)
            for j in range(4):
                i = gi * 4 + j
                proj_tile(wb, wk, j * 128, 128, hT, "hT",
                          lambda b, i=i: act(zs[:, i, :], ps[b][:, 0:L_P], AF.Silu, [("ps", b)], ["zs"]))
        for gi in range(3):
            wb, wk = load_w(w_in_e, 0, 1024 + gi * 512, 512)
            for j in range(4):
                i = gi * 4 + j
                proj_tile(wb, wk, j * 128, 128, hT, "hT", lambda b, i=i: conv_evac(b, xbcT[:, i, :], conva, i, True, ("xbcT", i)))
        wb, wk = load_w(w_in_e, 0, 2560, 32)
        proj_tile(wb, wk, 0, 32, hT, "hT", lambda b: cp("act", dtraw, ps[b][0:32, 0:L_P], [("ps", b)], ["dtraw"]))
        for gi in range(6):
            wb, wk = load_w(w_in_e, 0, 2592 + gi * 512, 512)
            for j in range(4):
                i = gi * 4 + j
                proj_tile(wb, wk, j * 128, 128, hT, "hT", lambda b, i=i: conv_evac(b, u16[:, i, :], convb, i, False, "u16"))
        for gi in range(2):
            wb, wk = load_w(w_in_e, 0, 5664 + gi * 512, 512)
            for j in range(4):
                i = gi * 4 + j
                proj_tile(wb, wk, j * 128, 128, hT, "hT",
                          lambda b, i=i: act(gs[:, i, :], ps[b][:, 0:L_P], AF.Silu, [("ps", b)], ["gs"]))

        act(dtT, dtraw, AF.Exp, ["dtraw", "ssdp"], ["dtT"], bias=ssdp[:, 0:1], scale=1.0)
        act(dtT, dtT, AF.Ln, ["dtT"], ["dtT"], bias=1.0, scale=1.0)
        ts("dve", adtT, dtT, aneg[:, 0:1], None, ALU.mult, None, ["dtT", "aneg"], ["adtT"])
        b = bank()
        for c in range(2):
            tr(ps[b][:, c * 32:(c + 1) * 32], dtT[:, c * 128:(c + 1) * 128], identf[0:32, 0:32], ["dtT", "identf"], [("ps", b)])
            tr(ps[b][:, 64 + c * 32:64 + (c + 1) * 32], adtT[:, c * 128:(c + 1) * 128], identf[0:32, 0:32], ["adtT", "identf"], [("ps", b)])
        cp("dve", dt_tok, ps[b][:, 0:64].rearrange("p (c j) -> p c j", j=32), [("ps", b)], ["dt_tok"])
        cp("dve", adt_tok, ps[b][:, 64:128].rearrange("p (c j) -> p c j", j=32), [("ps", b)], ["adt_tok"])
        b = bank()
        b2 = bank()
        for c in range(2):
            mm(ps[b][:, c * 32:c * 32 + 16], triLE, adt_tok[:, c, 0:16], True, True, ["triLE", "adt_tok"], [("ps", b)])
            mm(ps[b][:, c * 32 + 16:c * 32 + 32], triGE, adt_tok[:, c, 16:32], True, True, ["triGE", "adt_tok"], [("ps", b)])
            mm(ps[b2][:, c * 32:(c + 1) * 32], onesf, adt_tok[:, c, :], True, True, ["onesf", "adt_tok"], [("ps", b2)])
        cp("dve", cs_tok, ps[b][:, 0:64].rearrange("p (c j) -> p c j", j=32), [("ps", b)], ["cs_tok"])
        ts("dve", ncs_tok, cs_tok, -1.0, None, ALU.mult, None, ["cs_tok"], ["ncs_tok"])
        act(etot, ps[b2][:, 0:64].rearrange("p (c j) -> p c j", j=32), AF.Exp, [("ps", b2)], ["etot"])
        tt("dve", dec_tok, ps[b2][:, 0:64].rearrange("p (c j) -> p c j", j=32), cs_tok, ALU.subtract, [("ps", b2), "cs_tok"], ["dec_tok"])
        act(dec_tok, dec_tok, AF.Exp, ["dec_tok"], ["dec_tok"])
        act(ecs_tok, cs_tok, AF.Exp, ["cs_tok"], ["ecs_tok"])
        tt("dve", dtdec_tok, dt_tok, dec_tok, ALU.mult, ["dt_tok", "dec_tok"], ["dtdec_tok"])
        b = bank()
        for c in range(2):
            tr(ps[b][0:32, c * 128:(c + 1) * 128], cs_tok[:, c, :], identf, ["cs_tok", "identf"], [("ps", b)])
        cp("dve", csT, ps[b][0:32, 0:256].rearrange("p (c j) -> p c j", j=128), [("ps", b)], ["csT"])

        for c in range(2):
            for half in range(2):
                b = bank()
                for j in range(4):
                    i = half * 4 + j
                    tr(psb(b)[:, j * 128:(j + 1) * 128], xbcT[:, i, c * 128:(c + 1) * 128], identb, [("xbcT", i), "identb"], [("ps", b)])
                cp("act", x_tok[:, c, half * 512:(half + 1) * 512], psb(b)[:, 0:512], [("ps", b)], ["x_tok"])
            b = bank()
            for g2 in range(2):
                tr(psb(b)[:, g2 * 128:(g2 + 1) * 128], xbcT[:, 8 + g2, c * 128:(c + 1) * 128], identb, [("xbcT", 8 + g2), "identb"], [("ps", b)])
            cp("act", B_tok[:, c].rearrange("p g n -> p (g n)"), psb(b)[:, 0:256], [("ps", b)], ["B_tok"])
            b = bank()
            for g2 in range(2):
                mm(ps[b][:, g2 * 128:(g2 + 1) * 128], xbcT[:, 8 + g2, c * 128:(c + 1) * 128], xbcT[:, 10 + g2, c * 128:(c + 1) * 128],
                   True, True, [("xbcT", 8 + g2), ("xbcT", 10 + g2)], [("ps", b)])
            cp("dve", GT[:, c].rearrange("p g n -> p (g n)"), ps[b][:, 0:256], [("ps", b)], ["GT"])
            for d in range(2):
                tt("dve", xdt[:, c, d].rearrange("p (h q) -> p h q", q=64), x_tok[:, c].rearrange("p (h q) -> p h q", q=64),
                   dt_tok[:, c, d * 16:(d + 1) * 16].unsqueeze(2).to_broadcast([128, 16, 64]), ALU.mult, ["x_tok", "dt_tok"], ["xdt"])
                tt("dve", xdd[:, c, d].rearrange("p (h q) -> p h q", q=64), x_tok[:, c].rearrange("p (h q) -> p h q", q=64),
                   dtdec_tok[:, c, d * 16:(d + 1) * 16].unsqueeze(2).to_broadcast([128, 16, 64]), ALU.mult, ["x_tok", "dtdec_tok"], ["xdd"])

        for c in range(2):
            tt("dve", y_tok[:, c, :], x_tok[:, c, :], dsk, ALU.mult, ["x_tok", "dsk"], ["y_tok"])
        def p_prep(d, c, hg, k):
            mask = maskF if d == 0 else maskB
            b = bank()
            for j in range(4):
                h = hg * 4 + j
                mm(ps[b][:, j * 128:(j + 1) * 128], sel[:, d * 16 + h, :], csT[:, c, :], True, True, ["sel", "csT"], [("ps", b)])
            tt("dve", t1[0], ps[b][:, :].rearrange("p (j l) -> p j l", l=128), mask.unsqueeze(1).to_broadcast([128, 4, 128]),
               ALU.add, [("ps", b), "maskF", "maskB"], [("t1", 0)])
            for j in range(4):
                h = hg * 4 + j
                act(t2[0][:, j, :], t1[0][:, j, :], AF.Exp, [("t1", 0), "ncs_tok"], [("t2", 0)],
                    bias=ncs_tok[:, c, d * 16 + h:d * 16 + h + 1], scale=1.0)
            g2 = hg // 2
            tt("dve", MT[k], t2[0], GT[:, c, g2, :].unsqueeze(1).to_broadcast([128, 4, 128]), ALU.mult,
               [("t2", 0), "GT"], [("MT", k)])

        byb = [None]

        def p_consume(d, ci, c, hg, k):
            g2 = hg // 2
            if hg % 2 == 0:
                byb[0] = bank()
            by = byb[0]
            for j in range(4):
                h = hg * 4 + j
                hs = (h % 8) * 64
                mm(ps[by][:, hs:hs + 64], MT[k][:, j, :], xdt[:, c, d, h * 64:(h + 1) * 64], True, True,
                   [("MT", k), "xdt"], [("ps", by)])
            if hg % 2 == 1:
                tt("dve", y_tok[:, c, g2 * 512:(g2 + 1) * 512], y_tok[:, c, g2 * 512:(g2 + 1) * 512], ps[by][:, :], ALU.add,
                   ["y_tok", ("ps", by)], ["y_tok"])
            if hg != 3:
                return
            if ci > 0:
                for g2 in range(2):
                    b = bank()
                    mm(ps[b][:, :], xbcT[:, 10 + g2, c * 128:(c + 1) * 128], STb[d][:, g2 * 512:(g2 + 1) * 512], True, True,
                       [("xbcT", 10 + g2), ("STb", d)], [("ps", b)])
                    tt("dve", ytmp.rearrange("p (h q) -> p h q", q=64), ps[b][:, :].rearrange("p (h q) -> p h q", q=64),
                       ecs_tok[:, c, d * 16 + g2 * 8:d * 16 + g2 * 8 + 8].unsqueeze(2).to_broadcast([128, 8, 64]), ALU.mult,
                       [("ps", b), "ecs_tok"], ["ytmp"])
                    tt("dve", y_tok[:, c, g2 * 512:(g2 + 1) * 512], y_tok[:, c, g2 * 512:(g2 + 1) * 512], ytmp, ALU.add,
                       ["y_tok", "ytmp"], ["y_tok"])
            for g2 in range(2):
                b = bank()
                mm(ps[b][:, :], B_tok[:, c, g2, :], xdd[:, c, d, g2 * 512:(g2 + 1) * 512], True, True, ["B_tok", "xdd"], [("ps", b)])
                sl = ST[d][:, g2 * 512:(g2 + 1) * 512]
                tt("dve", sl.rearrange("p (h q) -> p h q", q=64), sl.rearrange("p (h q) -> p h q", q=64),
                   etot[:, c, d * 16 + g2 * 8:d * 16 + g2 * 8 + 8].unsqueeze(2).to_broadcast([128, 8, 64]), ALU.mult,
                   [("ST", d), "etot"], [("ST", d)])
                tt("dve", sl, sl, ps[b][:, :], ALU.add, [("ST", d), ("ps", b)], [("ST", d)])
            if ci == 0:
                cp("act", STb[d], ST[d], [("ST", d)], [("STb", d)])

        for d in range(2):
            memset(ST[d], 0.0, [("ST", d)])
            memset(STb[d], 0.0, [("STb", d)])
        p_items = []
        for d in range(2):
            for ci, c in enumerate([0, 1] if d == 0 else [1, 0]):
                for hg in range(4):
                    p_items.append((d, ci, c, hg))
        p_prep(p_items[0][0], p_items[0][2], p_items[0][3], 0)
        for d in range(2):
            for i_ in range(8 * d, 8 * d + 8):
                d_, ci, c, hg = p_items[i_]
                if i_ + 1 < len(p_items):
                    n_ = p_items[i_ + 1]
                    p_prep(n_[0], n_[2], n_[3], (i_ + 1) % 2)
                p_consume(d_, ci, c, hg, i_ % 2)
            for half in range(2):
                b = bank()
                for j in range(4):
                    i = half * 4 + j
                    tr(ps[b][:, j * 128:(j + 1) * 128], ST[d][:, i * 128:(i + 1) * 128], identf, [("ST", d), "identf"], [("ps", b)])
                cp("act", stout[:, half * 4:(half + 1) * 4, :].rearrange("p a n -> p (a n)"), ps[b][:, :], [("ps", b)], ["xin"])
            out_toks.append(dma("sp", o_state[s, d].rearrange("(a p) n -> p a n", p=128), stout, ["xin"], [("o_state", s, d)]))

        for c in range(2):
            for half in range(2):
                b = bank()
                for j in range(4):
                    ft = half * 4 + j
                    tr(ps[b][:, j * 128:(j + 1) * 128], y_tok[:, c, ft * 128:(ft + 1) * 128], identf, ["y_tok", "identf"], [("ps", b)])
                tt("dve", scr8[:, half * 4:(half + 1) * 4, c * 128:(c + 1) * 128], ps[b][:, :].rearrange("p (j l) -> p j l", l=128),
                   zs[:, half * 4:(half + 1) * 4, c * 128:(c + 1) * 128], ALU.mult, [("ps", b), "zs"], ["scr8"])
        for ft in range(8):
            act(ygT[:, ft, :], scr8[:, ft, :], AF.Copy, ["scr8", "naw"], ["ygT"], scale=naw[:, ft:ft + 1])
        act(scr8, scr8, AF.Square, ["scr8"], ["scr8"])
        b = bank()
        for ft in range(8):
            mm(ps[b][:, 0:L_P], onesf, scr8[:, ft, :], ft == 0, ft == 7, ["onesf", "scr8"], [("ps", b)])
        ts("dve", rstdy, ps[b][:, 0:L_P], 1.0 / D, EPS, ALU.mult, ALU.add, [("ps", b)], ["rstdy"])
        act(rstdy, rstdy, AF.Sqrt, ["rstdy"], ["rstdy"])
        P.op("dve", lambda e: e.reciprocal(out=rstdy, in_=rstdy), ["rstdy"], ["rstdy"])

        P.barrier()
        tt("dve", uvb, u16[:, 16:24, :], u16[:, 8:16, :], ALU.mult, ["u16"], ["uvb"])
        for c in range(2):
            for half in range(2):
                b = bank()
                for j in range(4):
                    ct = half * 4 + j
                    tr(psb(b)[:, j * 128:(j + 1) * 128], uvb[:, ct, c * 128:(c + 1) * 128], identb, ["uvb", "identb"], [("ps", b)])
                cp("act", uv_tok[:, c, half * 512:(half + 1) * 512], psb(b)[:, 0:512], [("ps", b)], ["uv_tok"])
        for ft in range(4):
            for half in range(2):
                bre, bim = bank(), bank()
                for ri, bb in ((0, bre), (1, bim)):
                    for c in range(2):
                        mm(ps[bb][:, :], dft[:, ri, c, ft * 128:(ft + 1) * 128], uv_tok[:, c, half * 512:(half + 1) * 512],
                           c == 0, c == 1, [("dft", ri), "uv_tok"], [("ps", bb)])
                cp("act", Xs[0], ps[bre][:, :], [("ps", bre)], [("Xs", 0)])
                cp("act", Xs[1], ps[bim][:, :], [("ps", bim)], [("Xs", 1)])
                kr = Kre[:, ft, half * 512:(half + 1) * 512]
                ki = Kim[:, ft, half * 512:(half + 1) * 512]
                yr = Yre[:, ft, half * 512:(half + 1) * 512]
                yi = Yim[:, ft, half * 512:(half + 1) * 512]
                tt("dve", ta[0], Xs[0], kr, ALU.mult, [("Xs", 0), "Kre"], [("ta", 0)])
                tt("dve", ta[1], Xs[1], ki, ALU.mult, [("Xs", 1), "Kim"], [("ta", 1)])
                tt("dve", yr, ta[0], ta[1], ALU.subtract, [("ta", 0), ("ta", 1)], ["Yre"])
                tt("dve", ta[1], Xs[0], ki, ALU.mult, [("Xs", 0), "Kim"], [("ta", 1)])
                tt("dve", ta[0], Xs[1], kr, ALU.mult, [("Xs", 1), "Kre"], [("ta", 0)])
                tt("dve", yi, ta[0], ta[1], ALU.add, [("ta", 0), ("ta", 1)], ["Yim"])
        for ct in range(8):
            b = bank()
            for ft in range(4):
                mm(ps[b][:, 0:L_P], Yre[:, ft, ct * 128:(ct + 1) * 128], dft[:, 2, ft, 0:L_P], ft == 0, False, ["Yre", ("dft", 2)], [("ps", b)])
            for ft in range(4):
                mm(ps[b][:, 0:L_P], Yim[:, ft, ct * 128:(ct + 1) * 128], dft[:, 3, ft, 0:L_P], False, ft == 3, ["Yim", ("dft", 3)], [("ps", b)])
            tc_ = tmpc[ct % 2]
            tk = ("tmpc", ct % 2)
            stt("dve", tc_, uvb[:, ct, :], hyb[:, ct:ct + 1], ps[b][:, 0:L_P], ALU.mult, ALU.add, ["uvb", "hyb", ("ps", b)], [tk])
            tt("dve", tc_, tc_, u16[:, ct, :], ALU.mult, [tk, "u16"], [tk])
            tt("dve", ybT[:, ct, :], tc_, gs[:, ct, :], ALU.mult, [tk, "gs"], ["ybT"])

        for nh in range(2):
            wa_, wka = load_w(w_out_e, 0, nh * 512, 512)
            wb_, wkb = load_w(w_out_e, 1024, nh * 512, 512)
            for j in range(4):
                ot = nh * 4 + j
                ba = proj_tile(wa_, wka, j * 128, 128, ygT, "ygT", None)
                bb = proj_tile(wb_, wkb, j * 128, 128, ybT, "ybT", None)
                tc_ = tmpc[ot % 2]
                tk = ("tmpc", ot % 2)
                tt("dve", tc_, ps[ba][:, 0:L_P], rstdy, ALU.mult, [("ps", ba), "rstdy"], [tk])
                tt("dve", tc_, tc_, ps[bb][:, 0:L_P], ALU.add, [tk, ("ps", bb)], [tk])
                stt("dve", xT[:, ot, :], tc_, mod[:, 0, 16 + ot, 0:1], xT[:, ot, :], ALU.mult, ALU.add, [tk, ("mod", 0), "xT"], ["xT"])

        P.barrier()
        memset(QBD, 0.0, ["QBD"])
        norm_mod(1)
        for seg in range(4):
            for gi in range(2):
                wb, wk = load_w(w_in_o, 0, seg * 1024 + gi * 512, 512)
                for j in range(4):
                    i = gi * 4 + j
                    if seg == 0:
                        ev = lambda b, i=i: cp("act", qT[:, i, :], ps[b][:, 0:L_P], [("ps", b)], ["qT"])
                    elif seg == 1:
                        def ev(b, i=i):
                            cp("act", kTf[:, i, :], ps[b][:, 0:L_P], [("ps", b)], ["kTf"])
                            cp("dve", kTb[:, i, :], ps[b][:, 0:L_P], [("ps", b)], ["kTb"])
                    elif seg == 2:
                        ev = lambda b, i=i: cp("act", vTf[:, i, :], ps[b][:, 0:L_P], [("ps", b)], ["vTf"])
                    else:
                        ev = lambda b, i=i: act(gs1[:, i, :], ps[b][:, 0:L_P], AF.Silu, [("ps", b)], ["gs1"])
                    proj_tile(wb, wk, j * 128, 128, hT, "hT", ev)
        for which, src, odram in ((0, kTf, o_k), (1, vTf, o_v)):
            for c in range(2):
                for half in range(2):
                    b = bank()
                    for j in range(4):
                        ft = half * 4 + j
                        tr(ps[b][:, j * 128:(j + 1) * 128], src[:, ft, c * 128:(c + 1) * 128], identf, ["kTf", "vTf", "identf"], [("ps", b)])
                    cp("act", xin[:, c, half * 512:(half + 1) * 512], ps[b][:, :], [("ps", b)], ["xin"])
                    if which == 1:
                        cp("dve", v_tok[:, c, half * 512:(half + 1) * 512], ps[b][:, :], [("ps", b)], ["v_tok"])
            for c in range(2):
                out_toks.append(dma("sp", odram[s][:, c * 128:(c + 1) * 128, :].rearrange("h p e -> p h e"),
                                    xin[:, c, :].rearrange("p (h e) -> p h e", e=64), ["xin"], [("o_kv", s, which, c)]))
        for hp2 in range(8):
            cp("dve", QBD[0:64, hp2, :, 0:64], qT[0:64, hp2, :].rearrange("p (qb q) -> p qb q", q=64), ["qT"], ["QBD"])
            cp("dve", QBD[64:128, hp2, :, 64:128], qT[64:128, hp2, :].rearrange("p (qb q) -> p qb q", q=64), ["qT"], ["QBD"])
        def a_S1(hp2, qb):
            b = bank()
            mm(ps[b][:, 0:L_P], QBD[:, hp2, qb, :], kTb[:, hp2, :], True, True, ["QBD", "kTb"], [("ps", b)])
            return b

        def a_S2(b, k):
            P.op("dve", lambda e, b=b: e.tensor_reduce(out=sm[:, 0:1], in_=ps[b][:, 0:L_P], axis=AX.X, op=ALU.max, negate=True),
                 [("ps", b)], ["sm0"])
            ts("dve", sm[:, 1:2], sm[:, 0:1], 0.125, None, ALU.mult, None, ["sm0"], ["sm1"])
            act(Pm[k], ps[b][:, 0:L_P], AF.Exp, [("ps", b), "sm1"], [("Pm", k), "sm2"], bias=sm[:, 1:2], scale=0.125, accum=sm[:, 2:3])
            P.op("dve", lambda e: e.reciprocal(out=sm[:, 3:4], in_=sm[:, 2:3]), ["sm2"], ["sm3"])
            ts("dve", Pm[k], Pm[k], sm[:, 3:4], None, ALU.mult, None, [("Pm", k), "sm3"], [("Pm", k)])

        def a_S3(hp2, qb, k):
            b2 = bank()
            for kt in range(2):
                tr(psb(b2)[:, kt * 128:(kt + 1) * 128], Pm[k][:, kt * 128:(kt + 1) * 128], identb, [("Pm", k), "identb"], [("ps", b2)])
            cp("act", PT[k].rearrange("p a q -> p (a q)"), psb(b2)[:, 0:256], [("ps", b2)], [("PT", k)])
            b3 = bank()
            for kt in range(2):
                mm(ps[b3][:, 0:128], v_tok[:, kt, hp2 * 128:(hp2 + 1) * 128], PT[k][:, kt, :], kt == 0, kt == 1,
                   ["v_tok", ("PT", k)], [("ps", b3)])
            tt("dve", ogT[0:64, hp2, qb * 64:(qb + 1) * 64], ps[b3][0:64, 0:64], gs1[0:64, hp2, qb * 64:(qb + 1) * 64], ALU.mult,
               [("ps", b3), "gs1"], ["ogT"])
            tt("dve", ogT[64:128, hp2, qb * 64:(qb + 1) * 64], ps[b3][64:128, 64:128], gs1[64:128, hp2, qb * 64:(qb + 1) * 64], ALU.mult,
               [("ps", b3), "gs1"], ["ogT"])

        a_items = [(hp2, qb) for hp2 in range(8) for qb in range(4)]
        nb_ = a_S1(*a_items[0])
        for i_, (hp2, qb) in enumerate(a_items):
            a_S2(nb_, i_ % 2)
            if i_ + 1 < len(a_items):
                nb_ = a_S1(*a_items[i_ + 1])
            a_S3(hp2, qb, i_ % 2)
        for nh in range(2):
            wb, wk = load_w(w_out_o, 0, nh * 512, 512)
            for j in range(4):
                ot = nh * 4 + j
                bo = proj_tile(wb, wk, j * 128, 128, ogT, "ogT", None)
                stt("dve", xT[:, ot, :], ps[bo][:, 0:L_P], mod[:, 1, 16 + ot, 0:1], xT[:, ot, :], ALU.mult, ALU.add,
                    [("ps", bo), ("mod", 1), "xT"], ["xT"])

        for c in range(2):
            for half in range(2):
                b = bank()
                for j in range(4):
                    ft = half * 4 + j
                    tr(ps[b][:, j * 128:(j + 1) * 128], xT[:, ft, c * 128:(c + 1) * 128], identf, ["xT", "identf"], [("ps", b)])
                cp("act", xin[:, c, half * 512:(half + 1) * 512], ps[b][:, :], [("ps", b)], ["xin"])
            act(scr8.rearrange("p a b -> p (a b)")[:, 0:1024], xin[:, c, :], AF.Square, ["xin"], ["scr8", "sm4"], accum=sm[:, 4:5])
            ts("dve", sm[:, 5:6], sm[:, 4:5], 1.0 / D, EPS, ALU.mult, ALU.add, ["sm4"], ["sm5"])
            act(sm[:, 5:6], sm[:, 5:6], AF.Sqrt, ["sm5"], ["sm5"])
            P.op("dve", lambda e: e.reciprocal(out=sm[:, 6:7], in_=sm[:, 5:6]), ["sm5"], ["sm6"])
            stt("dve", xin[:, c, :], xin[:, c, :], sm[:, 6:7], fnw_bc, ALU.mult, ALU.mult, ["xin", "sm6", "dsk2"], ["xin"])
        out_toks.append(dma("sp", o_yp[s].rearrange("(c p) d -> p c d", p=128), xin, ["xin"], [("o_yp", s)]))
        P.barrier()

    if DO_PROMPT:
        for s in range(NSEQ):
            seq_body(s)
    P.barrier()

    def sample_layer0(hg):
        w_in_es, conva_s_fm, convb_s_fm = w_in_es4[hg], conva_s_fm4[hg], convb_s_fm4[hg]
        ssd_par_s, dskip_s, st_in = ssd_par_s4[hg], dskip_s4[hg], st_in4[hg]
        cur[0] = prompt_mark
        onesb = sb("onesb", [128, 128], BF16)
        cp("dve", onesb, onesf, ["onesf"], ["onesb"])
        conva_s = sb("conva_s", [128, 4, 4])
        convb_s = sb("convb_s", [128, 6, 4])
        ssdp_s = sb("ssdp_s", [8, 2])
        aneg_s = sb("aneg_s", [8, 1])
        dsk_s = sb("dsk_s", [128, 256])
        dsk_s2 = sb("dsk_s2", [128, 256])
        dma("sp", conva_s, conva_s_fm, [], ["conva_s"])
        dma("sp", convb_s, convb_s_fm, [], ["convb_s"])
        dma("sp", ssdp_s, ssd_par_s, [], ["ssdp_s"])
        dma("sp", dsk_s, dskip_s[0].partition_broadcast(128), [], ["dsk_s"])
        dma("sp", dsk_s2, dskip_s[1].partition_broadcast(128), [], ["dsk_s2"])
        tt("dve", dsk_s, dsk_s, dsk_s2, ALU.add, ["dsk_s", "dsk_s2"], ["dsk_s"])
        act(aneg_s, ssdp_s[:, 1:2], AF.Exp, ["ssdp_s"], ["aneg_s"])
        ts("dve", aneg_s, aneg_s, -1.0, None, ALU.mult, None, ["aneg_s"], ["aneg_s"])
        uvb_s = sb("uvb_s", [128, 2, L_S], BF16)
        hy_mark = cur[0]
        xbcT_s = sb("xbcT_s", [128, 4, L_S], BF16)
        ph_mark = cur[0]
        w_s = sb("w_s", [128, 8, 1800], BF16)
        dma("pool", w_s, w_in_es.rearrange("(kc p) n -> p kc n", p=128), [], ["w_s"])
        xin_s = sb("xin_s", [128, 4, D])
        xT_w2 = [sb(f"xT_w{i}", [128, 8, 512]) for i in range(2)]
        sqb2 = [sb(f"sqb{i}", [128, 8, 512], BF16) for i in range(2)]
        hT_w2 = [sb(f"hT_w{i}", [128, 8, 512], BF16) for i in range(2)]
        rstd_w = sb("rstd_w", [128, 512])
        tmpw = [sb(f"tmpw{i}", [128, 512]) for i in range(2)]
        stg = [sb(f"stg{i}", [128, 512], BF16) for i in range(3)]
        x1w = sb("x1w", [128, 2, 512], BF16)
        vw = sb("vw", [128, 2, 512], BF16)
        dtst = sb("dtst", [8, 512])
        nst = [0]

        def conv_w(b, dst, cw, i, silu, key, n0, nv):
            tc_ = tmpw[i % 2]
            tk = ("tmpw", i % 2)
            ts("dve", tc_, ps[b][:, :], cw[:, i, 1:2], cw[:, i, 3:4], ALU.mult, ALU.add, [("ps", b), "conva_s", "convb_s"], [tk])
            stt("dve", tc_[:, 1:512], ps[b][:, 0:511], cw[:, i, 0:1], tc_[:, 1:512], ALU.mult, ALU.add, [("ps", b), tk], [tk])
            stt("dve", tc_[:, 0:511], ps[b][:, 1:512], cw[:, i, 2:3], tc_[:, 0:511], ALU.mult, ALU.add, [("ps", b), tk], [tk])
            if silu:
                act(dst, tc_[:, 1:1 + nv], AF.Silu, [tk], [key])
            else:
                cp("act", dst, tc_[:, 1:1 + nv], [tk], [key])

        def prep_w(k):
            xT_w, sqb, hT_w = xT_w2[k % 2], sqb2[k % 2], hT_w2[k % 2]
            kx, ks, kh = ("xT_w", k % 2), ("sqb", k % 2), ("hT_w", k % 2)
            t0 = 510 * k - 1
            n0 = 510 * k
            if hg > 0:
                dma("sp", hT_w, hTw_d[k].rearrange("a p t -> p a t"), [("hTw_d", k)], [kh])
                return
            for tt_ in range(4):
                lo = max(t0 + 128 * tt_, 0)
                hi = min(t0 + 128 * tt_ + 128, L_S)
                if hi - lo < 128:
                    memset(xin_s[:, tt_, :], 0.0, ["xin_s"])
                if hi > lo:
                    p0 = lo - (t0 + 128 * tt_)
                    dma("sp", xin_s[p0:p0 + (hi - lo), tt_, :], xs_in[lo:hi, :], [], ["xin_s"])
            for ft in range(8):
                b = bank()
                for tt_ in range(4):
                    tr(ps[b][:, tt_ * 128:(tt_ + 1) * 128], xin_s[:, tt_, ft * 128:(ft + 1) * 128], identf, ["xin_s", "identf"], [("ps", b)])
                cp("act", xT_w[:, ft, :], ps[b][:, :], [("ps", b)], [kx])
            act(sqb, xT_w, AF.Square, [kx], [ks])
            b = bank()
            for ft in range(8):
                mm(ps[b][:, :], onesb, sqb[:, ft, :], ft == 0, ft == 7, ["onesb", ks], [("ps", b)])
            ts("dve", rstd_w, ps[b][:, :], 1.0 / D, EPS, ALU.mult, ALU.add, [("ps", b)], ["rstd_w"])
            act(rstd_w, rstd_w, AF.Sqrt, ["rstd_w"], ["rstd_w"])
            P.op("dve", lambda e: e.reciprocal(out=rstd_w, in_=rstd_w), ["rstd_w"], ["rstd_w"])
            for ft in range(8):
                stt("dve", xT_w[:, ft, :], xT_w[:, ft, :], modA[:, 0, ft, 1:2], rstd_w, ALU.mult, ALU.mult,
                    [kx, ("modA", 0), "rstd_w"], [kx])
                act(hT_w[:, ft, :], xT_w[:, ft, :], AF.Identity, [kx, ("mod", 0)], [kh], bias=mod[:, 0, ft, 1:2], scale=1.0)
            if k == 0:
                memset(hT_w[:, :, 0:1], 0.0, [kh], eng="dve")
            if n0 + 511 > L_S:
                memset(hT_w[:, :, L_S - t0:512], 0.0, [kh], eng="dve")
            dma("sp", hTw_d[k].rearrange("a p t -> p a t"), hT_w, [kh], [("hTw_d", k)])

        def inproj_w(k):
            hT_w = hT_w2[k % 2]
            kh = ("hT_w", k % 2)
            n0 = 510 * k
            nv = min(510, L_S - n0)
            for i in range(14):
                b = bank()
                for kc in range(8):
                    mm(ps[b][:, :], w_s[:, kc, i * 128:(i + 1) * 128], hT_w[:, kc, :], kc == 0, kc == 7, ["w_s", kh], [("ps", b)])
                if i in (0, 1, 12, 13):
                    j = nst[0] % 3
                    nst[0] += 1
                    act(stg[j], ps[b][:, :], AF.Silu, [("ps", b)], [("stg", j)])
                    dst = (zs_d if i < 2 else gs_d)[i % 2 if i < 2 else i - 12, :, n0:n0 + nv]
                    dma("sp", dst, stg[j][:, 1:1 + nv], [("stg", j)], ["zs_d" if i < 2 else "gs_d"])
                elif i in (2, 3, 4, 5):
                    conv_w(b, xbcT_s[:, i - 2, n0:n0 + nv], conva_s, i - 2, True, "xbcT_s", n0, nv)
                elif i in (6, 7):
                    j = nst[0] % 3
                    nst[0] += 1
                    conv_w(b, stg[j][:, 1:1 + nv], convb_s, i - 6, False, ("stg", j), n0, nv)
                    dma("sp", x0_d[i - 6, :, n0:n0 + nv], stg[j][:, 1:1 + nv], [("stg", j)], ["x0_d"])
                elif i in (8, 9):
                    conv_w(b, x1w[:, i - 8, 0:nv], convb_s, i - 6, False, "x1w", n0, nv)
                else:
                    conv_w(b, vw[:, i - 10, 0:nv], convb_s, i - 6, False, "vw", n0, nv)
            tt("dve", uvb_s[:, :, n0:n0 + nv], vw[:, :, 0:nv], x1w[:, :, 0:nv], ALU.mult, ["vw", "x1w"], ["uvb_s"])
            b = bank()
            for kc in range(8):
                mm(ps[b][0:8, :], w_s[:, kc, 1792:1800], hT_w[:, kc, :], kc == 0, kc == 7, ["w_s", kh], [("ps", b)])
            cp("act", dtst, ps[b][0:8, :], [("ps", b)], ["dtst"])
            dma("sp", dt_d[:, n0:n0 + nv], dtst[:, 1:1 + nv], ["dtst"], ["dt_d"])
        prep_w(0)
        for k in range(NWIN):
            if k + 1 < NWIN:
                prep_w(k + 1)
            inproj_w(k)
        P.barrier()
        cur[0] = ph_mark

        NCH = L_S // 128
        x_tok_s = sb("x_tok_s", [128, NCH, 256], BF16)
        B_tok_s = sb("B_tok_s", [128, NCH, 128], BF16)
        GT_s = sb("GT_s", [128, NCH, 128])
        dt_tok_s = sb("dt_tok_s", [128, NCH, 8])
        adt_tok_s = sb("adt_tok_s", [128, NCH, 8])
        cs_tok_s = sb("cs_tok_s", [128, NCH, 8])
        ncs_tok_s = sb("ncs_tok_s", [128, NCH, 8])
        dec_tok_s = sb("dec_tok_s", [128, NCH, 8])
        ecs_tok_s = sb("ecs_tok_s", [128, NCH, 8])
        etot_s = sb("etot_s", [128, NCH, 8])
        dtdec_tok_s = sb("dtdec_tok_s", [128, NCH, 8])
        y_tok_s = sb("y_tok_s", [128, NCH, 256])
        dtp = sb("dtp", [8, 1024])
        adtp = sb("adtp", [8, 1024])
        csT_r = [sb(f"csT_r{i}", [8, 128]) for i in range(2)]
        xdt_r = [sb(f"xdt_r{i}", [128, 256], BF16) for i in range(2)]
        xdd_r = [sb(f"xdd_r{i}", [128, 256], BF16) for i in range(2)]
        t1s = sb("t1s", [128, 4, 128])
        t2s = sb("t2s", [128, 4, 128])
        MTs = [sb(f"MTs{i}", [128, 4, 128], BF16) for i in range(2)]
        ytmp_s = sb("ytmp_s", [128, 256])
        STs = [sb(f"STs{d}", [128, 256]) for d in range(2)]
        STbs = [sb(f"STbs{d}", [128, 256], BF16) for d in range(2)]
        stld = sb("stld", [128, 2, 128])
        for q4 in range(4):
            dma("sp", dtp, dt_d[:, q4 * 1024:(q4 + 1) * 1024], ["dt_d"], ["dtp"])
            act(dtp, dtp, AF.Exp, ["dtp", "ssdp_s"], ["dtp"], bias=ssdp_s[:, 0:1], scale=1.0)
            act(dtp, dtp, AF.Ln, ["dtp"], ["dtp"], bias=1.0, scale=1.0)
            ts("dve", adtp, dtp, aneg_s[:, 0:1], None, ALU.mult, None, ["dtp", "aneg_s"], ["adtp"])
            b = bank()
            for c8 in range(8):
                tr(ps[b][:, c8 * 8:(c8 + 1) * 8], dtp[:, c8 * 128:(c8 + 1) * 128], identf[0:8, 0:8], ["dtp", "identf"], [("ps", b)])
                tr(ps[b][:, 64 + c8 * 8:64 + (c8 + 1) * 8], adtp[:, c8 * 128:(c8 + 1) * 128], identf[0:8, 0:8], ["adtp", "identf"], [("ps", b)])
            cp("dve", dt_tok_s[:, q4 * 8:(q4 + 1) * 8, :], ps[b][:, 0:64].rearrange("p (c j) -> p c j", j=8), [("ps", b)], ["dt_tok_s"])
            cp("dve", adt_tok_s[:, q4 * 8:(q4 + 1) * 8, :], ps[b][:, 64:128].rearrange("p (c j) -> p c j", j=8), [("ps", b)], ["adt_tok_s"])
        adt_v = adt_tok_s.rearrange("p c (d j) -> p c d j", d=2)
        b = bank()
        b2 = bank()
        b3 = bank()
        mm(ps[b][:, 0:NCH * 4], triLE, adt_v[:, :, 0, :], True, True, ["triLE", "adt_tok_s"], [("ps", b)])
        mm(ps[b2][:, 0:NCH * 4], triGE, adt_v[:, :, 1, :], True, True, ["triGE", "adt_tok_s"], [("ps", b2)])
        mm(ps[b3][:, 0:NCH * 8], onesf, adt_tok_s, True, True, ["onesf", "adt_tok_s"], [("ps", b3)])
        cs_v = cs_tok_s.rearrange("p c (d j) -> p c d j", d=2)
        cp("dve", cs_v[:, :, 0, :], ps[b][:, 0:NCH * 4].rearrange("p (c j) -> p c j", j=4), [("ps", b)], ["cs_tok_s"])
        cp("dve", cs_v[:, :, 1, :], ps[b2][:, 0:NCH * 4].rearrange("p (c j) -> p c j", j=4), [("ps", b2)], ["cs_tok_s"])
        ts("dve", ncs_tok_s, cs_tok_s, -1.0, None, ALU.mult, None, ["cs_tok_s"], ["ncs_tok_s"])
        act(etot_s, ps[b3][:, 0:NCH * 8].rearrange("p (c j) -> p c j", j=8), AF.Exp, [("ps", b3)], ["etot_s"])
        tt("dve", dec_tok_s, ps[b3][:, 0:NCH * 8].rearrange("p (c j) -> p c j", j=8), cs_tok_s, ALU.subtract, [("ps", b3), "cs_tok_s"], ["dec_tok_s"])
        act(dec_tok_s, dec_tok_s, AF.Exp, ["dec_tok_s"], ["dec_tok_s"])
        act(ecs_tok_s, cs_tok_s, AF.Exp, ["cs_tok_s"], ["ecs_tok_s"])
        tt("dve", dtdec_tok_s, dt_tok_s, dec_tok_s, ALU.mult, ["dt_tok_s", "dec_tok_s"], ["dtdec_tok_s"])
        for c in range(NCH):
            b = bank()
            pb = psb(b)
            for j in range(3):
                tr(pb[:, j * 128:(j + 1) * 128], xbcT_s[:, j, c * 128:(c + 1) * 128], identb, ["xbcT_s", "identb"], [("ps", b)])
            cp("act", x_tok_s[:, c, :], pb[:, 0:256], [("ps", b)], ["x_tok_s"])
            cp("dve", B_tok_s[:, c, :], pb[:, 256:384], [("ps", b)], ["B_tok_s"])
            b = bank()
            mm(ps[b][:, 0:128], xbcT_s[:, 2, c * 128:(c + 1) * 128], xbcT_s[:, 3, c * 128:(c + 1) * 128], True, True, ["xbcT_s"], [("ps", b)])
            cp("act", GT_s[:, c, :], ps[b][:, 0:128], [("ps", b)], ["GT_s"])
        tt("dve", y_tok_s, x_tok_s, dsk_s.unsqueeze(1).to_broadcast([128, NCH, 256]), ALU.mult, ["x_tok_s", "dsk_s"], ["y_tok_s"])
        def load_state(d):
            dma("sp", stld, st_in[d].rearrange("(a p) n -> p a n", p=128), [], ["stld"])
            b = bank()
            for a2 in range(2):
                tr(ps[b][:, a2 * 128:(a2 + 1) * 128], stld[:, a2, :], identf, ["stld", "identf"], [("ps", b)])
            cp("dve", STs[d], ps[b][:, 0:256], [("ps", b)], [("STs", d)])
            cp("act", STbs[d], ps[b][:, 0:256], [("ps", b)], [("STbs", d)])

        def prep(d, c, k):
            mask = maskF if d == 0 else maskB
            b = bank()
            tr(ps[b][0:8, 0:128], cs_tok_s[:, c, :], identf, ["cs_tok_s", "identf"], [("ps", b)])
            cp("act", csT_r[k], ps[b][0:8, 0:128], [("ps", b)], [("csT_r", k)])
            tt("dve", xdt_r[k].rearrange("p (h q) -> p h q", q=64), x_tok_s[:, c, :].rearrange("p (h q) -> p h q", q=64),
               dt_tok_s[:, c, d * 4:(d + 1) * 4].unsqueeze(2).to_broadcast([128, 4, 64]), ALU.mult, ["x_tok_s", "dt_tok_s"], [("xdt_r", k)])
            tt("dve", xdd_r[k].rearrange("p (h q) -> p h q", q=64), x_tok_s[:, c, :].rearrange("p (h q) -> p h q", q=64),
               dtdec_tok_s[:, c, d * 4:(d + 1) * 4].unsqueeze(2).to_broadcast([128, 4, 64]), ALU.mult, ["x_tok_s", "dtdec_tok_s"], [("xdd_r", k)])
            b = bank()
            for j in range(4):
                mm(ps[b][:, j * 128:(j + 1) * 128], sel[0:8, d * 4 + j, :], csT_r[k], True, True, ["sel", ("csT_r", k)], [("ps", b)])
            tt("dve", t1s, ps[b][:, :].rearrange("p (j l) -> p j l", l=128), mask.unsqueeze(1).to_broadcast([128, 4, 128]),
               ALU.add, [("ps", b), "maskF", "maskB"], ["t1s"])
            for j in range(4):
                act(t2s[:, j, :], t1s[:, j, :], AF.Exp, ["t1s", "ncs_tok_s"], ["t2s"],
                    bias=ncs_tok_s[:, c, d * 4 + j:d * 4 + j + 1], scale=1.0)
            tt("dve", MTs[k], t2s, GT_s[:, c, :].unsqueeze(1).to_broadcast([128, 4, 128]), ALU.mult, ["t2s", "GT_s"], [("MTs", k)])

        def consume(d, c, k):
            by = bank()
            for j in range(4):
                mm(ps[by][:, j * 64:(j + 1) * 64], MTs[k][:, j, :], xdt_r[k][:, j * 64:(j + 1) * 64], True, True,
                   [("MTs", k), ("xdt_r", k)], [("ps", by)])
            mm(ps[by][:, 256:512], xbcT_s[:, 3, c * 128:(c + 1) * 128], STbs[d], True, True, ["xbcT_s", ("STbs", d)], [("ps", by)])
            tt("dve", ytmp_s.rearrange("p (h q) -> p h q", q=64), ps[by][:, 256:512].rearrange("p (h q) -> p h q", q=64),
               ecs_tok_s[:, c, d * 4:(d + 1) * 4].unsqueeze(2).to_broadcast([128, 4, 64]), ALU.mult, [("ps", by), "ecs_tok_s"], ["ytmp_s"])
            tt("dve", ytmp_s, ytmp_s, ps[by][:, 0:256], ALU.add, ["ytmp_s", ("ps", by)], ["ytmp_s"])
            tt("dve", y_tok_s[:, c, :], y_tok_s[:, c, :], ytmp_s, ALU.add, ["y_tok_s", "ytmp_s"], ["y_tok_s"])
            b = bank()
            mm(ps[b][:, 0:256], B_tok_s[:, c, :], xdd_r[k], True, True, ["B_tok_s", ("xdd_r", k)], [("ps", b)])
            tt("dve", STs[d].rearrange("p (h q) -> p h q", q=64), STs[d].rearrange("p (h q) -> p h q", q=64),
               etot_s[:, c, d * 4:(d + 1) * 4].unsqueeze(2).to_broadcast([128, 4, 64]), ALU.mult, [("STs", d), "etot_s"], [("STs", d)])
            tt("dve", STs[d], STs[d], ps[b][:, 0:256], ALU.add, [("STs", d), ("ps", b)], [("STs", d)])
            cp("act", STbs[d], STs[d], [("STs", d)], [("STbs", d)])

        iters = [(0, c) for c in range(NCH)] + [(1, c) for c in range(NCH - 1, -1, -1)]
        load_state(0)
        load_state(1)
        prep(iters[0][0], iters[0][1], 0)
        for i_, (d, c) in enumerate(iters):
            if i_ + 1 < len(iters):
                prep(iters[i_ + 1][0], iters[i_ + 1][1], (i_ + 1) % 2)
            consume(d, c, i_ % 2)
        zs_s = sb("zs_s", [128, 2, 512], BF16)
        ygs = sb("ygs", [128, 2, 512], BF16)
        ydbg = sb("ydbg", [128, 2, 512])
        for blk in range(8):
            dma("sp", zs_s, zs_d[:, :, blk * 512:(blk + 1) * 512].rearrange("a p t -> p a t"), ["zs_d"], ["zs_s"])
            for ft in range(2):
                b = bank()
                for c4 in range(4):
                    c = blk * 4 + c4
                    tr(ps[b][:, c4 * 128:(c4 + 1) * 128], y_tok_s[:, c, ft * 128:(ft + 1) * 128], identf, ["y_tok_s", "identf"], [("ps", b)])
                tt("dve", ygs[:, ft, :], ps[b][:, :], zs_s[:, ft, :], ALU.mult, [("ps", b), "zs_s"], ["ygs"])
                if STAGE_S == 1:
                    tt("dve", ydbg[:, ft, :], ps[b][:, :], zs_s[:, ft, :], ALU.mult, [("ps", b), "zs_s"], ["ydbg"])
            dma("sp", y_all[256 * hg:256 * hg + 256, blk * 512:(blk + 1) * 512].rearrange("(a p) t -> p a t", p=128), ygs, ["ygs"], ["y_all"])
            if STAGE_S == 1:
                out_toks.append(dma("sp", o_dbg[:, blk * 512:(blk + 1) * 512].rearrange("(a p) t -> p a t", p=128), ydbg, ["ydbg"], [("o_dbg", blk)]))
        P.barrier()
        return uvb_s, hy_mark


    def sample_hyena(uvb_s, ph_mark, hg):
        hfw3_s, hyp_s_fm = hfw3_s4[hg], hyp_s_fm4[hg]
        cur[0] = ph_mark
        n2 = 2 * L_S
        w1t = sb("w1t", [64, 128], BF16)
        w1i = sb("w1i", [128, 32], BF16)
        wtab = sb("wtab", [128, 3, 128], BF16)
        tw = sb("tw", [128, 2, 64])
        dma("pool", w1t, w1tab, [], ["w1t"])
        dma("pool", w1i, w1inv, [], ["w1i"])
        dma("pool", wtab, w128.rearrange("t p f -> p t f"), [], ["wtab"])
        dma("sp", tw, twid.rearrange("t p f -> p t f"), [], ["tw"])
        w1s_ = sb("w1s_", [33, 64])
        w2s_ = sb("w2s_", [64, 64])
        w3s_ = sb("w3s_", [64, 512])
        hp2 = sb("hp2", [64, 3])
        hsc2 = sb("hsc2", [64, 4])
        hyp = sb("hyp", [128, 2, 2])
        dma("sp", w1s_, hfw1, [], ["w1s_"])
        dma("sp", w2s_, hfw2, [], ["w2s_"])
        dma("sp", w3s_, hfw3_s, [], ["w3s_"])
        dma("sp", hp2, hfpar, [], ["hp2"])
        dma("sp", hyp, hyp_s_fm, [], ["hyp"])
        ts("dve", hsc2[:, 0:1], hp2[:, 2:3], 1.0 / TWO_PI, None, ALU.mult, None, ["hp2"], ["hsc2"])
        for j in range(2):
            tt("dve", hsc2[:, 1 + j:2 + j], hp2[:, j:j + 1], hsc2[:, 0:1], ALU.mult, ["hp2", "hsc2"], ["hsc2"])
        MAGIC = 12582912.0
        feb = sb("feb", [33, 512])
        tnb2 = sb("tnb2", [128, 512])
        hA = sb("hA", [64, 512])
        hB = sb("hB", [64, 512])
        kk2 = sb("kk2", [64, 512])
        dec2 = sb("dec2", [128, 512])
        kT_f = sb("kT_f", [128, n2], BF16)
        tnb2x = [tnb2, sb("tnb2b", [128, 512])]
        hBx = [hB, sb("hBb", [64, 512])]
        dec2x = [dec2, sb("dec2b", [128, 512])]
        lay = sb("lay", [128, 128 * 64], BF16)
        Gb = sb("Gb", [128, 64 * 128], BF16)
        Gp = sb("Gp", [128, 2, 4096], BF16)
        tq = [sb(f"tq{i}", [128, 1024]) for i in range(2)]
        Ksp = sb("Ksp", [128, 2, 4096], BF16)
        Xe = sb("Xe", [128, 2, 512])
        Ysp = sb("Ysp", [128, 2, 4096], BF16)
        y1 = Ksp.rearrange("p t n -> p (t n)")[0:32, :]
        ycv = sb("ycv", [64, L_S], BF16)
        x0h = sb("x0h", [64, L_S // 2], BF16)
        gsh = sb("gsh", [64, L_S // 2], BF16)
        ybh = sb("ybh", [64, L_S // 2], BF16)

        def sin_l(dst, src_w, src_x, j, keys):
            b = bank()
            mm(ps[b][0:64, :], src_w, src_x, True, True, keys, [("ps", b)])
            ts("dve", dst, ps[b][0:64, :], hsc2[:, 0:1], hsc2[:, 1 + j:2 + j], ALU.mult, ALU.add, [("ps", b), "hsc2"], ["hl2"])
            ts("dve", kk2, dst, MAGIC, None, ALU.add, None, ["hl2"], ["kk2"])
            ts("dve", kk2, kk2, -MAGIC, None, ALU.add, None, ["kk2"], ["kk2"])
            tt("dve", dst, dst, kk2, ALU.subtract, ["hl2", "kk2"], ["hl2"])
            ts("dve", dst, dst, TWO_PI, None, ALU.mult, None, ["hl2"], ["hl2"])
            ts("dve", dst, dst, math.pi, -math.pi, ALU.min, ALU.max, ["hl2"], ["hl2"])
            act(dst, dst, AF.Sin, ["hl2"], ["hA", "hB"])

        def fwd_half(srcT, hoff, NB, srckey):
            layv = lay[0:NB, :].rearrange("p (a c) -> p a c", c=64)
            for a0 in range(0, 128, 16):
                b = bank()
                pb = psb(b)
                for j in range(16):
                    a = a0 + j
                    tr(pb[0:NB, j * 64:(j + 1) * 64], srcT[hoff:hoff + 64, a:NB * 128:128], identb[hoff:hoff + 64, hoff:hoff + 64],
                       [srckey, "identb"], [("ps", b)])
                cp("act", layv[:, a0:a0 + 16, :], pb[0:NB, 0:1024].rearrange("p (a c) -> p a c", c=64),
                   [("ps", b)], ["lay"])
            Gv = Gb.rearrange("p (c f) -> p c f", f=128)
            for c0 in range(0, 64, 4):
                b = bank()
                for j in range(4):
                    mm(ps[b][:, j * 128:(j + 1) * 128], layv[:, :, c0 + j], w1t[0:NB, :], True, True, ["lay", "w1t"], [("ps", b)])
                cp("act", Gv[:, c0:c0 + 4, :], ps[b][:, :].rearrange("p (c f) -> p c f", f=128), [("ps", b)], ["Gb"])
            for hf in range(4):
                gre = Gv[:, hf * 16:(hf + 1) * 16, 0:64]
                gim = Gv[:, hf * 16:(hf + 1) * 16, 64:128]
                tre = tw[:, 0, :].unsqueeze(1).to_broadcast([128, 16, 64])
                tim = tw[:, 1, :].unsqueeze(1).to_broadcast([128, 16, 64])
                q0 = tq[0].rearrange("p (c f) -> p c f", f=64)
                q1 = tq[1].rearrange("p (c f) -> p c f", f=64)
                ore = Gp[:, 0, hf * 1024:(hf + 1) * 1024].rearrange("p (c f) -> p c f", f=64)
                oim = Gp[:, 1, hf * 1024:(hf + 1) * 1024].rearrange("p (c f) -> p c f", f=64)
                tt("dve", q0, gre, tre, ALU.mult, ["Gb", "tw"], [("tq", 0)])
                tt("dve", q1, gim, tim, ALU.mult, ["Gb", "tw"], [("tq", 1)])
                tt("dve", ore, q0, q1, ALU.subtract, [("tq", 0), ("tq", 1)], ["Gp"])
                tt("dve", q1, gre, tim, ALU.mult, ["Gb", "tw"], [("tq", 1)])
                tt("dve", q0, gim, tre, ALU.mult, ["Gb", "tw"], [("tq", 0)])
                tt("dve", oim, q0, q1, ALU.add, [("tq", 0), ("tq", 1)], ["Gp"])

        def stageB(ch, src, srckey, inverse):
            bre, bim = bank(), bank()
            sl = slice(ch * 512, (ch + 1) * 512)
            s_a, s_b = (2, 1) if inverse else (1, 2)
            mm(ps[bre][:, :], wtab[:, 0, :], src[:, 0, sl], True, False, ["wtab", srckey], [("ps", bre)])
            mm(ps[bre][:, :], wtab[:, s_a, :], src[:, 1, sl], False, True, ["wtab", srckey], [("ps", bre)])
            mm(ps[bim][:, :], wtab[:, 0, :], src[:, 1, sl], True, False, ["wtab", srckey], [("ps", bim)])
            mm(ps[bim][:, :], wtab[:, s_b, :], src[:, 0, sl], False, True, ["wtab", srckey], [("ps", bim)])
            return bre, bim

        for ct in range(2):
            for blk in range(n2 // 512):
                c0 = blk * 512
                kq = blk % 2
                tnb_, hB_, dec_ = tnb2x[kq], hBx[kq], dec2x[kq]
                dma("sp", tnb_, tnS[c0:c0 + 512].partition_broadcast(128), [], [("tnb2", kq)])
                if hg == 0 and ct == 0:
                    dma("sp", feb, featsS[:, c0:c0 + 512], [], ["feb"])
                    sin_l(hA, w1s_, feb, 0, ["w1s_", "feb"])
                    sin_l(hB_, w2s_, hA, 1, ["w2s_", "hA"])
                    dma("sp", h2_d[:, c0:c0 + 512], hB_, ["hA", "hB"], [("h2_d", blk)])
                else:
                    dma("sp", hB_, h2_d[:, c0:c0 + 512], [("h2_d", blk)], [("hBx", kq)])
                b = bank()
                wcol = (0 if c0 < L_S else 256) + ct * 128
                mm(ps[b][:, :], w3s_[:, wcol:wcol + 128], hB_, True, True, ["w3s_", "hB", ("hBx", kq)], [("ps", b)])
                act(dec_, tnb_, AF.Exp, [("tnb2", kq), "hyp"], [("dec2", kq)], scale=hyp[:, ct, 0:1])
                tt("dve", kT_f[:, c0:c0 + 512], ps[b][:, :], dec_, ALU.mult, [("ps", b), ("dec2", kq)], ["kT_f"])
            memset(kT_f[:, L_S:L_S + 1], 0.0, ["kT_f"], eng="dve")
            ts("dve", kT_f[:, 0:1], kT_f[:, 0:1], hyp[:, ct, 1:2], None, ALU.add, None, ["kT_f", "hyp"], ["kT_f"])
            for half in range(2):
                hoff = half * 64
                fwd_half(kT_f, hoff, 64, "kT_f")
                for ch in range(8):
                    bre, bim = stageB(ch, Gp, "Gp", False)
                    cp("act", Ksp[:, 0, ch * 512:(ch + 1) * 512], ps[bre][:, :], [("ps", bre)], ["Ksp"])
                    cp("dve", Ksp[:, 1, ch * 512:(ch + 1) * 512], ps[bim][:, :], [("ps", bim)], ["Ksp"])
                fwd_half(uvb_s[:, ct, :], hoff, 32, "uvb_s")
                for ch in range(8):
                    bre, bim = stageB(ch, Gp, "Gp", False)
                    sl = slice(ch * 512, (ch + 1) * 512)
                    cp("act", Xe[:, 0, :], ps[bre][:, :], [("ps", bre)], [("Xe", 0)])
                    cp("act", Xe[:, 1, :], ps[bim][:, :], [("ps", bim)], [("Xe", 1)])
                    qa = tq[0][:, 0:512]
                    qb_ = tq[1][:, 0:512]
                    tt("dve", qa, Xe[:, 0, :], Ksp[:, 0, sl], ALU.mult, [("Xe", 0), "Ksp"], [("tq", 0)])
                    tt("dve", qb_, Xe[:, 1, :], Ksp[:, 1, sl], ALU.mult, [("Xe", 1), "Ksp"], [("tq", 1)])
                    tt("dve", Ysp[:, 0, sl], qa, qb_, ALU.subtract, [("tq", 0), ("tq", 1)], ["Ysp"])
                    tt("dve", qb_, Xe[:, 0, :], Ksp[:, 1, sl], ALU.mult, [("Xe", 0), "Ksp"], [("tq", 1)])
                    tt("dve", qa, Xe[:, 1, :], Ksp[:, 0, sl], ALU.mult, [("Xe", 1), "Ksp"], [("tq", 0)])
                    tt("dve", Ysp[:, 1, sl], qa, qb_, ALU.add, [("tq", 0), ("tq", 1)], ["Ysp"])
                Vv = Gb.rearrange("p (c f) -> p c f", f=128)
                for ch in range(8):
                    bre, bim = stageB(ch, Ysp, "Ysp", True)
                    csl = slice(ch * 8, (ch + 1) * 8)
                    cp("act", Xe[:, 0, :], ps[bre][:, :], [("ps", bre)], [("Xe", 0)])
                    cp("act", Xe[:, 1, :], ps[bim][:, :], [("ps", bim)], [("Xe", 1)])
                    vre = Xe[:, 0, :].rearrange("p (c f) -> p c f", f=64)
                    vim = Xe[:, 1, :].rearrange("p (c f) -> p c f", f=64)
                    tre = tw[:, 0, :].unsqueeze(1).to_broadcast([128, 8, 64])
                    tim = tw[:, 1, :].unsqueeze(1).to_broadcast([128, 8, 64])
                    qa = tq[0][:, 0:512].rearrange("p (c f) -> p c f", f=64)
                    qb_ = tq[1][:, 0:512].rearrange("p (c f) -> p c f", f=64)
                    tt("dve", qa, vre, tre, ALU.mult, [("Xe", 0), "tw"], [("tq", 0)])
                    tt("dve", qb_, vim, tim, ALU.mult, [("Xe", 1), "tw"], [("tq", 1)])
                    tt("dve", Vv[:, csl, 0:64], qa, qb_, ALU.add, [("tq", 0), ("tq", 1)], ["Gb"])
                    tt("dve", qb_, vre, tim, ALU.mult, [("Xe", 0), "tw"], [("tq", 1)])
                    tt("dve", qa, vim, tre, ALU.mult, [("Xe", 1), "tw"], [("tq", 0)])
                    tt("dve", Vv[:, csl, 64:128], qa, qb_, ALU.subtract, [("tq", 0), ("tq", 1)], ["Gb"])
                Tl = lay.rearrange("p (c a) -> p c a", a=128)
                for c0 in range(0, 64, 8):
                    b = bank()
                    pb = psb(b)
                    for j in range(8):
                        tr(pb[:, j * 128:(j + 1) * 128], Vv[:, c0 + j, :], identb, ["Gb", "identb"], [("ps", b)])
                    cp("act", Tl[:, c0:c0 + 8, :], pb[:, 0:1024].rearrange("p (c a) -> p c a", a=128),
                       [("ps", b)], ["lay"])
                ycv3 = ycv.rearrange("p (i a) -> p i a", a=128)
                for a0 in range(0, 128, 16):
                    b = bank()
                    for j in range(16):
                        mm(ps[b][0:64, j * 32:(j + 1) * 32], Tl[:, :, a0 + j], w1i, True, True, ["lay", "w1i"], [("ps", b)])
                    cp("act", ycv3[:, :, a0:a0 + 16], ps[b][0:64, 0:512].rearrange("p (a i) -> p i a", i=32), [("ps", b)], ["ycv"])
                r0 = ct * 128 + hoff
                for tq_ in range(2):
                    tsl = slice(tq_ * 2048, (tq_ + 1) * 2048)
                    dma("sp", x0h, x0_d[ct, hoff:hoff + 64, tsl], ["x0_d"], ["x0h"])
                    dma("sp", gsh, gs_d[ct, hoff:hoff + 64, tsl], ["gs_d"], ["gsh"])
                    tt("dve", ybh, ycv[:, tsl], x0h, ALU.mult, ["ycv", "x0h"], ["ybh"])
                    tt("dve", ybh, ybh, gsh, ALU.mult, ["ybh", "gsh"], ["ybh"])
                    dma("sp", y_all[D + 256 * hg + r0:D + 256 * hg + r0 + 64, tsl], ybh, ["ybh"], ["y_all"])
                    if STAGE_S == 2:
                        out_toks.append(dma("sp", o_dbg.bitcast(BF16)[r0:r0 + 64, tsl], ybh, ["ybh"], [("o_dbg", r0, tq_)]))

        P.barrier()


    def sample_tail():
        P.barrier()
        cur[0] = prompt_mark
        NE = 2048
        onesb2 = sb("onesb2", [128, 128], BF16)
        cp("dve", onesb2, onesf, ["onesf"], ["onesb2"])
        bm = sb("bm", [128, 16])
        dma("sp", bm, blkmask.partition_broadcast(128), [], ["bm"])
        hT1x = sb("hT1x", [128, 8, NE], BF16)
        ogT_own = sb("ogT_own", [128, 8, 1024], BF16)
        t1_mark = cur[0]
        wo = sb("wo", [128, 16, D], BF16)
        for half in range(2):
            dma("pool", wo[:, half * 8:(half + 1) * 8, :], w_out_e[half * 1024:(half + 1) * 1024, :].rearrange("(kc p) n -> p kc n", p=128),
                [], [("wo", half)])
        if y_all_dbg is not None:
            stgd = sb("stgd", [128, L_S], BF16)
            for rt in range(16):
                dma("pool", stgd, y_all_dbg[rt * 128:(rt + 1) * 128, :], [], ["stgd"])
                dma("sp", y_all[rt * 128:(rt + 1) * 128, :], stgd, ["stgd"], ["y_all"])
        cand = [sb(f"cand{i}", [128, 4, 512], BF16) for i in range(2)]
        ygx = sb("ygx", [128, 16, 512], BF16)
        xin_t = [sb(f"xin_t{i}", [128, D]) for i in range(2)]
        xT_b = sb("xT_b", [128, 8, 512])
        sq_b = sb("sq_b", [128, 8, 512], BF16)
        rstd_b = sb("rstd_b", [128, 512])
        rstdy_b = sb("rstdy_b", [128, 512])
        tmp_b = [sb(f"tmp_b{i}", [128, 512]) for i in range(2)]
        for kb in range(4):
            piece = 0 if kb == 0 else (2 if kb == 3 else 1)
            off = 512 if kb in (0, 2) else 0
            for rt in range(16):
                cd = cand[rt % 2]
                ck = ("cand", rt % 2)
                dma("sp", cd, y_all[rt * 128:(rt + 1) * 128, :].rearrange("p (j t) -> p j t", t=1024)[:, :, off:off + 512], ["y_all"], [ck])
                for jj in range(4):
                    m = bm[:, piece * 4 + jj:piece * 4 + jj + 1]
                    if jj == 0:
                        ts("dve", ygx[:, rt, :], cd[:, jj, :], m, None, ALU.mult, None, [ck, "bm"], ["ygx"])
                    else:
                        stt("dve", ygx[:, rt, :], cd[:, jj, :], m, ygx[:, rt, :], ALU.mult, ALU.add, [ck, "bm", "ygx"], ["ygx"])
            for ft in range(8):
                pass
            bks = [bank() for _ in range(8)]
            for tt_ in range(4):
                xt = xin_t[tt_ % 2]
                xk = ("xin_t", tt_ % 2)
                dma("sp", xt, x_ext[kb * 512 + tt_ * 128:kb * 512 + (tt_ + 1) * 128, :], [], [xk])
                for ft in range(8):
                    tr(ps[bks[ft]][:, tt_ * 128:(tt_ + 1) * 128], xt[:, ft * 128:(ft + 1) * 128], identf, [xk, "identf"], [("ps", bks[ft])])
            for ft in range(8):
                cp("act", xT_b[:, ft, :], ps[bks[ft]][:, :], [("ps", bks[ft])], ["xT_b"])
            act(sq_b, ygx[:, 0:8, :], AF.Square, ["ygx"], ["sq_b"])
            b = bank()
            for ft in range(8):
                mm(ps[b][:, :], onesb2, sq_b[:, ft, :], ft == 0, ft == 7, ["onesb2", "sq_b"], [("ps", b)])
            ts("dve", rstdy_b, ps[b][:, :], 1.0 / D, EPS, ALU.mult, ALU.add, [("ps", b)], ["rstdy_b"])
            act(rstdy_b, rstdy_b, AF.Sqrt, ["rstdy_b"], ["rstdy_b"])
            P.op("dve", lambda e: e.reciprocal(out=rstdy_b, in_=rstdy_b), ["rstdy_b"], ["rstdy_b"])
            for ft in range(8):
                act(ygx[:, ft, :], ygx[:, ft, :], AF.Copy, ["ygx", "naw"], ["ygx"], scale=naw[:, ft:ft + 1])
            for ot in range(8):
                ba, bb = bank(), bank()
                for kc in range(8):
                    mm(ps[ba][:, :], wo[:, kc, ot * 128:(ot + 1) * 128], ygx[:, kc, :], kc == 0, kc == 7, [("wo", 0), "ygx"], [("ps", ba)])
                for kc in range(8):
                    mm(ps[bb][:, :], wo[:, 8 + kc, ot * 128:(ot + 1) * 128], ygx[:, 8 + kc, :], kc == 0, kc == 7, [("wo", 1), "ygx"], [("ps", bb)])
                tb = tmp_b[ot % 2]
                tk = ("tmp_b", ot % 2)
                tt("dve", tb, ps[ba][:, :], rstdy_b, ALU.mult, [("ps", ba), "rstdy_b"], [tk])
                tt("dve", tb, tb, ps[bb][:, :], ALU.add, [tk, ("ps", bb)], [tk])
                stt("dve", xT_b[:, ot, :], tb, mod[:, 0, 16 + ot, 1:2], xT_b[:, ot, :], ALU.mult, ALU.add, [tk, ("mod", 0), "xT_b"], ["xT_b"])
            dma("sp", x1_d[:, :, kb * 512:(kb + 1) * 512].rearrange("a p t -> p a t"), xT_b, ["xT_b"], ["x1_d"])
            act(sq_b, xT_b, AF.Square, ["xT_b"], ["sq_b"])
            b = bank()
            for ft in range(8):
                mm(ps[b][:, :], onesb2, sq_b[:, ft, :], ft == 0, ft == 7, ["onesb2", "sq_b"], [("ps", b)])
            ts("dve", rstd_b, ps[b][:, :], 1.0 / D, EPS, ALU.mult, ALU.add, [("ps", b)], ["rstd_b"])
            act(rstd_b, rstd_b, AF.Sqrt, ["rstd_b"], ["rstd_b"])
            P.op("dve", lambda e: e.reciprocal(out=rstd_b, in_=rstd_b), ["rstd_b"], ["rstd_b"])
            for ft in range(8):
                stt("dve", xT_b[:, ft, :], xT_b[:, ft, :], modA[:, 1, ft, 1:2], rstd_b, ALU.mult, ALU.mult,
                    ["xT_b", ("modA", 1), "rstd_b"], ["xT_b"])
                act(hT1x[:, ft, kb * 512:(kb + 1) * 512], xT_b[:, ft, :], AF.Identity, ["xT_b", ("mod", 1)], ["hT1x"],
                    bias=mod[:, 1, ft, 1:2], scale=1.0)
        P.barrier()
        cur[0] = t1_mark
        wp = [sb(f"wp{i}", [128, 8, 512], BF16) for i in range(2)]
        qT_p = sb("qT_p", [128, NE], BF16)
        kT_p = sb("kT_p", [128, NE], BF16)
        vT_p = sb("vT_p", [128, NE], BF16)
        gs_p = sb("gs_p", [128, NE], BF16)
        v_tok_p = sb("v_tok_p", [128, 16, 128], BF16)
        ckT_p = sb("ckT_p", [128, 256], BF16)
        cv_p = sb("cv_p", [128, 2, 128], BF16)
        strip = sb("strip", [128, 19 * 64])
        rm = sb("rm", [16, 18 * 64], BF16)
        dma("pool", rm, rm_in, [], ["rm"])
        selb = sb("selb", [16, 16, 128], BF16)
        cp("dve", selb, sel[0:16, 0:16, :], ["sel"], ["selb"])
        qbd = [sb(f"qbd{i}", [128, 128], BF16) for i in range(2)]
        sc = [sb(f"sc{i}", [128, 1408]) for i in range(2)]
        Pn = [sb(f"Pn{i}", [128, 1408], BF16) for i in range(2)]
        PTn = [sb(f"PTn{i}", [128, 11, 128], BF16) for i in range(2)]
        sms = sb("sms", [128, 8])
        for i in range(2):
            memset(qbd[i], 0.0, [("qbd", i)])
        itn = [0]
        for hp2 in range(8):
            w_ = wp[hp2 % 2]
            wk_ = ("wp", hp2 % 2)
            dma("pool", w_, w_in_o_pairs[hp2].rearrange("(kc p) n -> p kc n", p=128), [], [wk_])
            dma("pool", ckT_p, ckT_in[hp2], [], ["ckT_p"])
            dma("pool", cv_p, cv_in[hp2], [], ["cv_p"])
            dma("sp", strip, strip_in[hp2], [], ["strip"])
            for seg, dst_, key_ in ((0, qT_p, "qT_p"), (1, kT_p, "kT_p"), (2, vT_p, "vT_p"), (3, gs_p, "gs_p")):
                for kb in ((1, 2) if seg in (0, 3) else range(4)):
                    b = bank()
                    for kc in range(8):
                        mm(ps[b][:, :], w_[:, kc, seg * 128:(seg + 1) * 128], hT1x[:, kc, kb * 512:(kb + 1) * 512], kc == 0, kc == 7,
                           [wk_, "hT1x"], [("ps", b)])
                    if seg == 3:
                        act(dst_[:, kb * 512:(kb + 1) * 512], ps[b][:, :], AF.Silu, [("ps", b)], [key_])
                    else:
                        cp("act", dst_[:, kb * 512:(kb + 1) * 512], ps[b][:, :], [("ps", b)], [key_])
            for t4 in range(4):
                b = bank()
                pb = psb(b)
                for j in range(4):
                    tl = t4 * 4 + j
                    tr(pb[:, j * 128:(j + 1) * 128], vT_p[:, tl * 128:(tl + 1) * 128], identb, ["vT_p", "identb"], [("ps", b)])
                cp("act", v_tok_p[:, t4 * 4:(t4 + 1) * 4, :].rearrange("p a d -> p (a d)"), pb[:, 0:512], [("ps", b)], ["v_tok_p"])
            def S1(rho, k):
                q0 = 512 + 64 * rho
                nt = 9 if rho % 2 == 0 else 8
                st_ = rho // 2 if rho % 2 == 0 else (rho + 1) // 2
                nk = nt * 128
                cp("dve", qbd[k][0:64, 0:64], qT_p[0:64, q0:q0 + 64], ["qT_p"], [("qbd", k)])
                cp("dve", qbd[k][64:128, 64:128], qT_p[64:128, q0:q0 + 64], ["qT_p"], [("qbd", k)])
                banks = []
                for c0 in range(0, nk, 512):
                    cn = min(512, nk - c0)
                    b = bank()
                    banks.append((b, c0, cn))
                    mm(ps[b][:, 0:cn], qbd[k], kT_p[:, st_ * 128 + c0:st_ * 128 + c0 + cn], True, False, [("qbd", k), "kT_p"], [("ps", b)])
                    mm(ps[b][:, 0:cn], selb[:, rho, :], rm[:, c0:c0 + cn], False, True, ["selb", "rm"], [("ps", b)])
                bc = bank()
                mm(ps[bc][:, 0:256], qbd[k], ckT_p, True, True, [("qbd", k), "ckT_p"], [("ps", bc)])
                return banks, bc

            def S2(rho, k, banks, bc):
                nt = 9 if rho % 2 == 0 else 8
                nk = nt * 128
                srow0 = 0 if rho % 2 == 0 else 1
                for (b, c0, cn) in banks:
                    stt("dve", sc[k][:, c0:c0 + cn], ps[b][:, 0:cn], 0.125, strip[:, srow0 * 64 + c0:srow0 * 64 + c0 + cn], ALU.mult, ALU.add,
                        [("ps", b), "strip"], [("sc", k)])
                act(sc[k][:, nk:nk + 256], ps[bc][:, 0:256], AF.Copy, [("ps", bc)], [("sc", k)], scale=0.125)
                tot = nk + 256
                P.op("dve", lambda e, k=k, tot=tot: e.tensor_reduce(out=sms[:, 0:1], in_=sc[k][:, 0:tot], axis=AX.X, op=ALU.max, negate=True),
                     [("sc", k)], ["sms0"])
                act(Pn[k][:, 0:tot], sc[k][:, 0:tot], AF.Exp, [("sc", k), "sms0"], [("Pn", k), "sms2"], bias=sms[:, 0:1], scale=1.0, accum=sms[:, 2:3])
                P.op("dve", lambda e: e.reciprocal(out=sms[:, 3:4], in_=sms[:, 2:3]), ["sms2"], ["sms3"])
                ts("dve", Pn[k][:, 0:tot], Pn[k][:, 0:tot], sms[:, 3:4], None, ALU.mult, None, [("Pn", k), "sms3"], [("Pn", k)])

            def S3(rho, k):
                q0 = 512 + 64 * rho
                nt = 9 if rho % 2 == 0 else 8
                st_ = rho // 2 if rho % 2 == 0 else (rho + 1) // 2
                ntt = nt + 2
                for g0 in range(0, ntt, 8):
                    gn = min(8, ntt - g0)
                    b = bank()
                    pb = psb(b)
                    for j in range(gn):
                        tr(pb[:, j * 128:(j + 1) * 128], Pn[k][:, (g0 + j) * 128:(g0 + j + 1) * 128], identb, [("Pn", k), "identb"], [("ps", b)])
                    cp("act", PTn[k][:, g0:g0 + gn, :].rearrange("p a q -> p (a q)"), pb[:, 0:gn * 128], [("ps", b)], [("PTn", k)])
                bo = bank()
                for j in range(ntt):
                    lhs = v_tok_p[:, st_ + j, :] if j < nt else cv_p[:, j - nt, :]
                    mm(ps[bo][:, 0:128], lhs, PTn[k][:, j, :], j == 0, j == ntt - 1, ["v_tok_p", "cv_p", ("PTn", k)], [("ps", bo)])
                tt("dve", ogT_own[0:64, hp2, 64 * rho:64 * rho + 64], ps[bo][0:64, 0:64], gs_p[0:64, q0:q0 + 64], ALU.mult,
                   [("ps", bo), "gs_p"], ["ogT_own"])
                tt("dve", ogT_own[64:128, hp2, 64 * rho:64 * rho + 64], ps[bo][64:128, 64:128], gs_p[64:128, q0:q0 + 64], ALU.mult,
                   [("ps", bo), "gs_p"], ["ogT_own"])

            nxt = S1(0, itn[0] % 2)
            for rho in range(16):
                k = itn[0] % 2
                itn[0] += 1
                S2(rho, k, *nxt)
                if rho + 1 < 16:
                    nxt = S1(rho + 1, itn[0] % 2)
                S3(rho, k)
        P.barrier()
        cur[0] = t1_mark
        woo = sb("woo", [128, 8, D], BF16)
        dma("pool", woo, w_out_o.rearrange("(kc p) n -> p kc n", p=128), [], ["woo"])
        x1o = sb("x1o", [128, 8, 512])
        ytk = sb("ytk", [128, 4, D])
        sq3 = sb("sq3", [128, D])
        sms3 = sb("sms3", [128, 8])
        for kb in range(2):
            dma("sp", x1o, x1_d[:, :, 512 + kb * 512:512 + (kb + 1) * 512].rearrange("a p t -> p a t"), ["x1_d"], ["x1o"])
            for ot in range(8):
                b = bank()
                for kc in range(8):
                    mm(ps[b][:, :], woo[:, kc, ot * 128:(ot + 1) * 128], ogT_own[:, kc, kb * 512:(kb + 1) * 512], kc == 0, kc == 7,
                       ["woo", "ogT_own"], [("ps", b)])
                stt("dve", x1o[:, ot, :], ps[b][:, :], mod[:, 1, 16 + ot, 1:2], x1o[:, ot, :], ALU.mult, ALU.add,
                    [("ps", b), ("mod", 1), "x1o"], ["x1o"])
            for c in range(4):
                for half in range(2):
                    b = bank()
                    for j in range(4):
                        ft = half * 4 + j
                        tr(ps[b][:, j * 128:(j + 1) * 128], x1o[:, ft, c * 128:(c + 1) * 128], identf, ["x1o", "identf"], [("ps", b)])
                    cp("act", ytk[:, c, half * 512:(half + 1) * 512], ps[b][:, :], [("ps", b)], ["ytk"])
                act(sq3, ytk[:, c, :], AF.Square, ["ytk"], ["sq3", "sms4"], accum=sms3[:, 4:5])
                ts("dve", sms3[:, 5:6], sms3[:, 4:5], 1.0 / D, EPS, ALU.mult, ALU.add, ["sms4"], ["sms5"])
                act(sms3[:, 5:6], sms3[:, 5:6], AF.Sqrt, ["sms5"], ["sms5"])
                P.op("dve", lambda e: e.reciprocal(out=sms3[:, 6:7], in_=sms3[:, 5:6]), ["sms5"], ["sms6"])
                stt("dve", ytk[:, c, :], ytk[:, c, :], sms3[:, 6:7], fnw_bc, ALU.mult, ALU.mult, ["ytk", "sms6", "dsk2"], ["ytk"])
            out_toks.append(dma("sp", o_ys[kb * 512:(kb + 1) * 512, :].rearrange("(c p) d -> p c d", p=128), ytk, ["ytk"], [("o_ys", kb)]))

    NHG = 4
    if DO_SAMPLE:
        for hg in range(NHG):
            uvb_s_, phm_ = sample_layer0(hg)
            if STAGE_S >= 2:
                sample_hyena(uvb_s_, phm_, hg)
        if STAGE_S >= 3:
            sample_tail()

    P.finish("sp", out_toks)
    P.emit()
    return nc


def kernel(**inp):
    f32 = np.float32
    inp = {k: np.asarray(v) for k, v in inp.items()}
    nc = build_nc()
    sel = np.zeros((32, 32, 128), f32)
    for j in range(32):
        sel[j, j, :] = 1.0
    conv_a = np.concatenate([inp["conv_a_w"][0], inp["conv_a_b"][0][None]], 0)
    conv_a_fm = np.ascontiguousarray(conv_a.reshape(4, 12, 128).transpose(2, 1, 0))
    conv_b = np.concatenate([inp["conv_b_w"][0], inp["conv_b_b"][0][None]], 0)
    conv_b_fm = np.ascontiguousarray(conv_b.reshape(4, 24, 128).transpose(2, 1, 0))
    ssd_par = np.ascontiguousarray(np.stack([inp["dt_bias"][0].reshape(32), inp["a_log"][0].reshape(32)], 1))
    dskip_rep = np.ascontiguousarray(np.repeat(inp["d_skip"][0], 64, axis=1))
    featsP, tnP = hyena_feats(L_P)
    deltas = np.linspace(math.log(1e-2) / 0.3, math.log(1e-2) / 1.5, D, dtype=f32)
    ndelta_fm = fm(-np.abs(deltas), 8)
    n = 2 * L_P
    tt_ = np.arange(n, dtype=np.float64)
    ang = 2.0 * np.pi * np.outer(tt_, tt_) / n
    dftP = np.stack([np.cos(ang), -np.sin(ang), np.cos(ang) / n, -np.sin(ang) / n]).astype(f32)
    hfpar = np.ascontiguousarray(np.stack([inp["hf_b1"][0], inp["hf_b2"][0], inp["hf_freq"][0]], 1))
    featsS, tnS = hyena_feats(L_S)
    ii = np.arange(64, dtype=np.float64)
    w1tab = np.concatenate([np.cos(2 * np.pi * np.outer(ii, ii) / 64), -np.sin(2 * np.pi * np.outer(ii, ii) / 64)], 1).astype(f32)
    i32 = np.arange(32, dtype=np.float64)
    w1inv = np.concatenate([np.cos(2 * np.pi * np.outer(ii, i32) / 64), -np.sin(2 * np.pi * np.outer(ii, i32) / 64)], 0).astype(np.float64)
    w1inv = (w1inv / (2 * L_S)).astype(f32)
    aa = np.arange(128, dtype=np.float64)
    th = 2 * np.pi * np.outer(aa, aa) / 128
    w128 = np.stack([np.cos(th), np.sin(th), -np.sin(th)]).astype(f32)
    tht = 2 * np.pi * np.outer(aa, ii) / (2 * L_S)
    twid = np.stack([np.cos(tht), -np.sin(tht)]).astype(f32)
    WE = inp["w_in_e"][0]
    W3 = inp["hf_w3"][0]

    def hg_slices(g):
        g2 = g // 2
        dtc = [2560 + d_ * 16 + 4 * g + j for d_ in range(2) for j in range(4)]
        r256 = np.arange(256 * g, 256 * g + 256)
        r128 = np.arange(128 * g2, 128 * g2 + 128)
        cols = np.concatenate([r256, 1024 + r256, 2048 + r128, 2304 + r128, 2592 + r256, 3616 + r256, 4640 + r256, 5664 + r256,
                               np.array(dtc)])
        ca_cols = np.concatenate([r256, 1024 + r128, 1280 + r128])
        cb_cols = np.concatenate([r256, 1024 + r256, 2048 + r256])
        return dict(
            w_in_es=WE[:, cols],
            conva_s_fm=conv_a[:, ca_cols].reshape(4, 4, 128).transpose(2, 1, 0),
            convb_s_fm=conv_b[:, cb_cols].reshape(4, 6, 128).transpose(2, 1, 0),
            ssd_par_s=np.stack([inp["dt_bias"][0][:, 4 * g:4 * g + 4].reshape(8), inp["a_log"][0][:, 4 * g:4 * g + 4].reshape(8)], 1),
            dskip_s=np.repeat(inp["d_skip"][0][:, 4 * g:4 * g + 4], 64, axis=1),
            hfw3_s=np.concatenate([W3[:, r256], W3[:, 1024 + r256]], 1),
            hyp_s_fm=np.stack([fm(-np.abs(deltas)[r256], 2), fm(inp["hy_bias"][0][r256], 2)], -1),
        )
    hgs = [hg_slices(g) for g in range(4)]
    hg_in = {k: np.ascontiguousarray(np.stack([h[k] for h in hgs]).astype(f32)) for k in hgs[0]}
    WO = inp["w_in_o"][0]
    w_in_o_pairs = np.ascontiguousarray(np.stack([
        np.concatenate([WO[:, seg * 1024 + hp * 128:seg * 1024 + hp * 128 + 128] for seg in range(4)], 1) for hp in range(8)]))
    rpb = inp["rpb"][0]
    qc_ = np.arange(64)[:, None]
    kc_ = np.arange(64)[None, :]
    col0 = np.clip(qc_ - 8, 0, 48)
    col_in = (kc_ >= col0) & (kc_ < col0 + 16)
    dcidx = np.clip(kc_ - qc_ + 15, 0, 30)
    strip_all = np.full((16, 64, 19, 64), NEG, f32)
    for sr in range(15):
        strip_all[:, :, sr + 1, :] = np.where(col_in[None], rpb[:, sr][:, dcidx], f32(NEG))
    strip_in = np.ascontiguousarray(strip_all.reshape(8, 128, 19 * 64))
    in_maps = []
    for core in range(8):
        b = core // 4
        g = core % 4
        j_ = g
        xe = np.zeros((2048, D), f32)
        lo, hi = 1024 * j_ - 512, 1024 * j_ + 1536
        slo, shi = max(lo, 0), min(hi, L_S)
        xe[slo - lo:shi - lo] = inp["x_sample"][b][slo:shi]
        bmk = np.zeros((16,), f32)
        if j_ - 1 >= 0:
            bmk[0 + j_ - 1] = 1.0
        bmk[4 + j_] = 1.0
        if j_ + 1 <= 3:
            bmk[8 + j_ + 1] = 1.0
        rm_ = np.full((16, 18, 64), NEG, f32)
        for rho in range(16):
            r = 16 * j_ + rho
            row0 = min(max(r - 4, 0), 56)
            for ip in range(18):
                sr = ip - 1 if rho % 2 == 0 else ip
                kr = r + sr - 7
                if 0 <= sr <= 14 and row0 <= kr < row0 + 8:
                    rm_[rho, ip, :] = 0.0
        ck = inp["cache_k"][b, 0]
        cvv = inp["cache_v"][b, 0]
        sample_maps = {
            "xs_in": np.ascontiguousarray(inp["x_sample"][b]),
            "st_in": np.ascontiguousarray(np.stack([inp["state_ssd"][b, 0, :, 4 * gg:4 * gg + 4].reshape(2, 256, 128) for gg in range(4)])),
            "featsS": featsS, "tnS": tnS,
            "w1tab": w1tab, "w1inv": w1inv, "w128": w128, "twid": twid,
            "x_ext": xe, "blkmask": bmk, "w_in_o_pairs": w_in_o_pairs,
            "ckT_in": np.ascontiguousarray(ck.reshape(8, 2, 256, 64).transpose(0, 1, 3, 2).reshape(8, 128, 256)),
            "cv_in": np.ascontiguousarray(cvv.reshape(8, 2, 2, 128, 64).transpose(0, 3, 2, 1, 4).reshape(8, 128, 2, 128)),
            "strip_in": strip_in, "rm_in": np.ascontiguousarray(rm_.reshape(16, 18 * 64)),
        }
        sample_maps.update(hg_in)
        m = {
            "xp": np.ascontiguousarray(inp["x_prompt"][core * NSEQ:(core + 1) * NSEQ]),
            "cv": np.ascontiguousarray(np.stack([fm(inp["c_ctx"], 8), fm(inp["c"][b], 8)], -1)),
            "w_ada": inp["w_ada"],
            "b_ada_fm": np.ascontiguousarray(np.stack([fm(inp["b_ada"][l], 24) for l in range(2)])),
            "norm_w_fm": np.ascontiguousarray(np.stack([fm(inp["norm_w"][l], 8) for l in range(2)])),
            "w_in_e": inp["w_in_e"][0],
            "w_out_e": inp["w_out_e"][0],
            "w_in_o": inp["w_in_o"][0],
            "w_out_o": inp["w_out_o"][0],
            "conv_a_fm": conv_a_fm,
            "conv_b_fm": conv_b_fm,
            "ssd_par": ssd_par,
            "sel_c": sel.reshape(32, 32 * 128),
            "dskip_rep": dskip_rep,
            "naw_fm": fm(inp["norm_a_w"][0], 8),
            "hyb_fm": fm(inp["hy_bias"][0], 8),
            "fnw": inp["final_norm_w"],
            "featsP": featsP, "tnP": tnP,
            "hfw1": inp["hf_w1"][0], "hfw2": inp["hf_w2"][0], "hfw3": inp["hf_w3"][0], "hfpar": hfpar,
            "ndelta_fm": ndelta_fm, "dftP": dftP,
        }
        m.update(sample_maps)
        in_maps.append(m)
    res = run_bass_kernel_spmd(nc, in_maps, core_ids=list(range(8)))
    st = np.concatenate([r["o_state"] for r in res.results], 0)
    new_state = st.reshape(32, 1, 2, 16, 64, 128).astype(f32)
    y_prompt = np.concatenate([r["o_yp"] for r in res.results], 0).astype(f32)
    new_k = np.concatenate([r["o_k"] for r in res.results], 0).reshape(32, 1, 16, 256, 64).astype(f32)
    new_v = np.concatenate([r["o_v"] for r in res.results], 0).reshape(32, 1, 16, 256, 64).astype(f32)
    y_sample = np.stack([np.concatenate([res.results[4 * b_ + j]["o_ys"] for j in range(4)], 0) for b_ in range(2)]).astype(f32)
    return (y_prompt, y_sample, new_state, new_k, new_v)
```
